# Optimizing a Trainium2 kernel written in Bass

```python
import jax, jax.numpy as jnp
from jax import lax
import numpy as np

D_MODEL = 2048
BATCH = 4
SEQ = 2048
DEPTH = 2
DEC_BATCH = 128
DEC_SEQ = 8
PAST_LEN = 16384
PAGE_SIZE = 128

E_A = D_MODEL // 2
E_B = D_MODEL // 2
E_C = D_MODEL // 2
CONV_W = 31
CHUNK = 128
B_GROUPS = 8
B_GC = E_B // B_GROUPS
POOL_WINDOWS = (2, 4, 8, 16)
C_GROUPS = len(POOL_WINDOWS)
C_GC = E_C // C_GROUPS
POOL_MAX = max(POOL_WINDOWS)
N_BRANCH = 3
N_IN = 3 * E_A + 3 * E_B + 2 * E_C + N_BRANCH * D_MODEL
EPS = 1e-6

kernel_name = "hybrid_conv_gmlp_pool_decoder_step"


def _rmsnorm(x, g):
    xf = x.astype(jnp.float32)
    y = xf * lax.rsqrt(jnp.mean(xf * xf, axis=-1, keepdims=True) + EPS)
    return (y * g.astype(jnp.float32)).astype(x.dtype)


def _layernorm(x, g, b):
    xf = x.astype(jnp.float32)
    mu = jnp.mean(xf, axis=-1, keepdims=True)
    xc = xf - mu
    var = jnp.mean(xc * xc, axis=-1, keepdims=True)
    return (xc * lax.rsqrt(var + EPS) * g.astype(jnp.float32) + b.astype(jnp.float32)).astype(x.dtype)


def _conv_module(a_val, a_gate, prefix, conv_w, conv_b, ln_g, ln_b):
    z = a_val * jax.nn.sigmoid(a_gate)
    zp = jnp.concatenate([prefix.astype(z.dtype), z], axis=1)
    y = lax.conv_general_dilated(
        zp, conv_w[:, None, :].astype(z.dtype), window_strides=(1,), padding='VALID',
        dimension_numbers=('NWC', 'WIO', 'NWC'), feature_group_count=E_A) + conv_b
    y = jax.nn.silu(_layernorm(y, ln_g, ln_b))
    return y, zp[:, -(CONV_W - 1):]


def _gmlp_module(u, v, ln_g, ln_b, w_s, b_s):
    v = _layernorm(v, ln_g, ln_b)
    n, t, _ = v.shape
    L = min(t, CHUNK)
    vc = v.reshape(n, t // L, L, B_GROUPS, B_GC)
    mask = jnp.tril(jnp.ones((L, L), dtype=bool))
    ws = jnp.where(mask[None], w_s[:, :L, :L], jnp.zeros((), w_s.dtype))
    mixed = jnp.einsum('gts,bnsgc->bntgc', ws, vc) + jnp.transpose(b_s[:, :L])[None, None, :, :, None]
    return u * mixed.reshape(n, t, E_B), v


def _pool_module(c, prefix, start, pool_w, pool_scale):
    n, t, _ = c.shape
    L = POOL_MAX - 1
    zc = jnp.concatenate([prefix.astype(c.dtype), c], axis=1)
    z = zc.astype(jnp.float32)
    cs = jnp.concatenate([jnp.zeros((n, 1, E_C), jnp.float32), lax.cumsum(z, axis=1)], axis=1)
    hi = cs[:, L + 1:]
    pos = start + jnp.arange(t)
    outs = []
    for g, w in enumerate(POOL_WINDOWS):
        sl = slice(g * C_GC, (g + 1) * C_GC)
        lo = cs[:, L + 1 - w:L + 1 - w + t, sl]
        cnt = jnp.minimum(w, pos + 1).astype(jnp.float32)[None, :, None]
        outs.append((hi[..., sl] - lo) / cnt)
    d = (jnp.concatenate(outs, axis=-1) - z[:, L:]).astype(c.dtype).reshape(n, t, C_GROUPS, C_GC)
    y = jnp.einsum('btgc,gcd->btgd', d, pool_w).reshape(n, t, E_C) * pool_scale
    return y, zc[:, -L:]


def _layer(x, conv_prefix, pool_prefix, start, g_pre, w_in, conv_w, conv_b, conv_ln_g, conv_ln_b,
           w_br_a, gmlp_ln_g, gmlp_ln_b, gmlp_ws, gmlp_bs, w_br_b, pool_w, pool_scale, w_br_c,
           w_out, g_post):
    n, t, _ = x.shape
    h = _rmsnorm(x, g_pre)
    proj = jnp.einsum('btd,df->btf', h, w_in)
    sizes = (E_A, E_A, E_A, E_B, E_B, E_B, E_C, E_C)
    idx = [int(i) for i in np.cumsum(sizes)]
    a_val, a_gate, a_silu, b_u, b_v, b_silu, c_in, c_silu, gate_logits = jnp.split(proj, idx, axis=-1)

    ya, conv_state = _conv_module(a_val, a_gate, conv_prefix, conv_w, conv_b, conv_ln_g, conv_ln_b)
    yb, v_rows = _gmlp_module(b_u, b_v, gmlp_ln_g, gmlp_ln_b, gmlp_ws, gmlp_bs)
    yc, pool_state = _pool_module(c_in, pool_prefix, start, pool_w, pool_scale)

    ya = jnp.einsum('bte,ed->btd', ya * jax.nn.silu(a_silu), w_br_a)
    yb = jnp.einsum('bte,ed->btd', yb * jax.nn.silu(b_silu), w_br_b)
    yc = jnp.einsum('bte,ed->btd', yc * jax.nn.silu(c_silu), w_br_c)

    gates = jax.nn.sigmoid(gate_logits).reshape(n, t, N_BRANCH, D_MODEL)
    m = gates[:, :, 0] * ya + gates[:, :, 1] * yb + gates[:, :, 2] * yc
    y = jnp.einsum('btd,de->bte', m, w_out)
    return x + _rmsnorm(y, g_post), conv_state, pool_state, v_rows


def setup_inputs(seed: int = 0) -> dict:
    key = jax.random.key(seed)
    ks = jax.random.split(key, 24)
    f32 = jnp.float32
    nrm = lambda k, shape, s: jax.random.normal(k, shape, f32) * s
    return {
        "x_prompt": nrm(ks[0], (BATCH, SEQ, D_MODEL), 1.0),
        "x_sample": nrm(ks[1], (DEC_BATCH, DEC_SEQ, D_MODEL), 1.0),
        "state_conv": nrm(ks[2], (DEPTH, DEC_BATCH, CONV_W - 1, E_A), 0.5),
        "state_pool": nrm(ks[3], (DEPTH, DEC_BATCH, POOL_MAX - 1, E_C), 1.0),
        "g_pre": 1.0 + nrm(ks[4], (DEPTH, D_MODEL), 0.02),
        "w_in": nrm(ks[5], (DEPTH, D_MODEL, N_IN), D_MODEL ** -0.5),
        "conv_w": nrm(ks[6], (DEPTH, CONV_W, E_A), CONV_W ** -0.5),
        "conv_b": nrm(ks[7], (DEPTH, E_A), 0.02),
        "conv_ln_g": 1.0 + nrm(ks[8], (DEPTH, E_A), 0.02),
        "conv_ln_b": nrm(ks[9], (DEPTH, E_A), 0.02),
        "w_br_a": nrm(ks[10], (DEPTH, E_A, D_MODEL), E_A ** -0.5),
        "gmlp_ln_g": 1.0 + nrm(ks[11], (DEPTH, E_B), 0.02),
        "gmlp_ln_b": nrm(ks[12], (DEPTH, E_B), 0.02),
        "gmlp_ws": nrm(ks[13], (DEPTH, B_GROUPS, CHUNK, CHUNK), CHUNK ** -0.5),
        "gmlp_bs": 1.0 + nrm(ks[14], (DEPTH, B_GROUPS, CHUNK), 0.02),
        "w_br_b": nrm(ks[15], (DEPTH, E_B, D_MODEL), E_B ** -0.5),
        "pool_w": nrm(ks[16], (DEPTH, C_GROUPS, C_GC, C_GC), C_GC ** -0.5),
        "pool_scale": 1.0 + nrm(ks[17], (DEPTH, E_C), 0.02),
        "w_br_c": nrm(ks[18], (DEPTH, E_C, D_MODEL), E_C ** -0.5),
        "w_out": nrm(ks[19], (DEPTH, D_MODEL, D_MODEL), D_MODEL ** -0.5),
        "g_post": 1.0 + nrm(ks[20], (DEPTH, D_MODEL), 0.02),
    }


def reference(x_prompt, x_sample, state_conv, state_pool, g_pre, w_in, conv_w, conv_b, conv_ln_g,
              conv_ln_b, w_br_a, gmlp_ln_g, gmlp_ln_b, gmlp_ws, gmlp_bs, w_br_b, pool_w, pool_scale,
              w_br_c, w_out, g_post):
    xp, xs = x_prompt, x_sample
    conv_p, pool_p, conv_s, pool_s, v_s = [], [], [], [], []
    zero_conv = jnp.zeros((BATCH, CONV_W - 1, E_A), x_prompt.dtype)
    zero_pool = jnp.zeros((BATCH, POOL_MAX - 1, E_C), x_prompt.dtype)
    for l in range(DEPTH):
        w = (g_pre[l], w_in[l], conv_w[l], conv_b[l], conv_ln_g[l], conv_ln_b[l], w_br_a[l],
             gmlp_ln_g[l], gmlp_ln_b[l], gmlp_ws[l], gmlp_bs[l], w_br_b[l], pool_w[l],
             pool_scale[l], w_br_c[l], w_out[l], g_post[l])
        xp, cst, pst, _ = _layer(xp, zero_conv, zero_pool, 0, *w)
        xs, cst_s, pst_s, v_rows = _layer(xs, state_conv[l], state_pool[l], PAST_LEN, *w)
        conv_p.append(cst); pool_p.append(pst)
        conv_s.append(cst_s); pool_s.append(pst_s); v_s.append(v_rows)
    new_conv_prompt = jnp.stack(conv_p)
    new_pool_prompt = jnp.stack(pool_p)
    new_conv_sample = jnp.stack(conv_s)
    new_pool_sample = jnp.stack(pool_s)
    new_gmlp_v_sample = jnp.stack(v_s)
    return (xp, xs, new_conv_prompt, new_pool_prompt, new_conv_sample, new_pool_sample, new_gmlp_v_sample)
```

```python
import numpy as np
import concourse.bass as bass
import concourse.mybir as mybir
from concourse.bass_utils import run_bass_kernel_spmd

F32 = mybir.dt.float32
BF16 = mybir.dt.bfloat16
AF = mybir.ActivationFunctionType
ALU = mybir.AluOpType
AX = mybir.AxisListType

NCORE = 8
D = 2048
NT = 1280
NQ = 10
L = 2
EPS = 1e-6
TILES = [(0, 512), (512, 512), (1024, 256)]
TILES2 = [(128, 384), (512, 384), (896, 384)]
NIN = 14336
P_GPRE, P_CW, P_CB, P_LG, P_LB, P_PS, P_GLG, P_GLB, NPAR = 0, 16, 264, 272, 280, 288, 296, 304, 312
NSLAB_L = 153
RING = 8
CORDER = [6, 7, 4, 5, 2, 3, 0, 1]


class Sem:
    def __init__(self, nc, name):
        self.h = nc.alloc_semaphore(name)
        self.name = name
        self.v = 0


class Builder:
    def __init__(self, nc):
        self.nc = nc
        self.E = {"pe": nc.tensor, "act": nc.scalar, "dve": nc.vector, "pool": nc.gpsimd, "sp": nc.sync}
        self.S = {e: Sem(nc, "s_" + e) for e in ("pe", "act", "dve")}
        self.waited = {}
        self.last = {e: None for e in ("pe", "act", "dve")}

    def wait(self, eng, tok):
        if tok is None:
            return
        sem, v = tok
        key = (eng, sem.name)
        if self.waited.get(key, 0) >= v:
            return
        self.E[eng].wait_ge(sem.h, v)
        self.waited[key] = v

    def waits(self, eng, deps):
        for d in deps:
            if isinstance(d, (list, tuple)) and d and isinstance(d[0], (tuple, list)):
                for dd in d:
                    self.wait(eng, dd)
            elif isinstance(d, list):
                for dd in d:
                    self.wait(eng, dd)
            else:
                self.wait(eng, d)

    def op(self, eng, fn, deps=()):
        self.waits(eng, deps)
        ins = fn(self.E[eng])
        sem = self.S[eng]
        sem.v += 1
        ins.then_inc(sem.h, 1)
        tok = (sem, sem.v)
        self.last[eng] = tok
        return tok

    def dma(self, eng, sem, out, in_, deps=()):
        self.waits(eng, deps)
        self.E[eng].dma_start(out=out, in_=in_).then_inc(sem.h, 16)
        sem.v += 16
        return (sem, sem.v)

    def mm(self, out, pairs, deps=()):
        self.waits("pe", deps)
        n = len(pairs)
        ins = None
        for i, (l, r) in enumerate(pairs):
            ins = self.nc.tensor.matmul(out, lhsT=l, rhs=r, start=(i == 0), stop=(i == n - 1))
        sem = self.S["pe"]
        sem.v += 1
        ins.then_inc(sem.h, 1)
        tok = (sem, sem.v)
        self.last["pe"] = tok
        return tok

    def pe_mark(self, ins):
        sem = self.S["pe"]
        sem.v += 1
        ins.then_inc(sem.h, 1)
        tok = (sem, sem.v)
        self.last["pe"] = tok
        return tok

    def barrier(self, extra=(), pe=True):
        toks = [self.last[e] for e in ("pe", "act", "dve")] + list(extra)
        for e in (("pe",) if pe else ()) + ("act", "dve", "sp"):
            self.waits(e, toks)


def build_program():
    nc = bass.Bass("TRN2", target_bir_lowering=False)
    B = Builder(nc)
    dt_in = lambda name, shape: nc.dram_tensor(name, shape, F32, kind="ExternalInput").ap()
    dt_out = lambda name, shape: nc.dram_tensor(name, shape, F32, kind="ExternalOutput").ap()
    xin = dt_in("xin", [NT, D])
    wst = dt_in("wst", [L * NSLAB_L, 128, 2048])
    par_d = dt_in("par", [L, 128, NPAR])
    gpost_d = dt_in("gpost", [L, 128, D])
    wsT_d = dt_in("wsT", [L, 128, 1024])
    wsS_d = dt_in("wsS", [L, 128, 1024])
    bsrow_d = dt_in("bsrow", [L, 128, 2048])
    cst_d = dt_in("cst", [128, 384])
    icnt_d = dt_in("icnt", [128, 64])
    hmask_d = dt_in("hmask", [128, 1])
    sconv_d = dt_in("sconv", [L, 8, 128, 16, 30])
    spool_d = dt_in("spool", [L, 8, 128, 16, 15])
    x1s = nc.dram_tensor("x1s", [NT, D], F32, kind="Internal").ap()
    yp_o = dt_out("yp", [1024, D])
    ys_o = dt_out("ys", [128, D])
    convp_o = dt_out("convp", [L, 8, 128, 30])
    poolp_o = dt_out("poolp", [L, 8, 128, 15])
    convs_o = dt_out("convs", [L, 8, 128, 16 * 38])
    pools_o = dt_out("pools", [L, 8, 128, 16 * 23])
    vs_o = dt_out("vs", [L, 128, 1024])

    RH = nc.alloc_sbuf_tensor("RH", [128, 16, NT], BF16)
    RA = nc.alloc_sbuf_tensor("RA", [128, 24 * NT], BF16)
    RS = nc.alloc_sbuf_tensor("RS", [128, 8 * NT], F32)
    RW = nc.alloc_sbuf_tensor("RW", [128, 7168], F32)
    ring = [nc.alloc_sbuf_tensor(f"ring{i}", [128, 2048], BF16) for i in range(RING)]
    par = nc.alloc_sbuf_tensor("par_sb", [128, L, NPAR], F32)
    cst = nc.alloc_sbuf_tensor("cst_sb", [128, 384], F32)
    onesf = nc.alloc_sbuf_tensor("onesf", [128, 128], F32)
    icnt = nc.alloc_sbuf_tensor("icnt_sb", [128, 4, 16], F32)
    hmask = nc.alloc_sbuf_tensor("hmask_sb", [128, 1], F32)
    sm = nc.alloc_sbuf_tensor("sm", [128, 96], F32)
    ident = cst[:, 0:128]
    maskP = cst[:, 128:256]
    maskS = cst[:, 256:384]
    banks = [nc.alloc_psum_tensor(f"bank{i}", [128, 512], F32) for i in range(8)]

    RA_f = RA.bitcast(F32)
    RS_b = RS.bitcast(BF16)
    RW_b = RW.bitcast(BF16)
    actA = lambda c, a, b: RA[:, c * NT + a: c * NT + b]
    actB = lambda c, a, b: RA[:, (8 + c) * NT + a: (8 + c) * NT + b]
    actC = lambda c, a, b: RA[:, (16 + c) * NT + a: (16 + c) * NT + b]
    acts = [actA, actB, actC]

    ld0 = Sem(nc, "ld0")
    slot_sem = [Sem(nc, f"slot{i}") for i in range(RING)]
    ldx = [Sem(nc, "ldx0"), Sem(nc, "ldx1")]
    stx = [Sem(nc, "stx0"), Sem(nc, "stx1")]
    ld_s2 = [Sem(nc, "ld_s0"), Sem(nc, "ld_s1")]
    st_p2 = [Sem(nc, "st_p0"), Sem(nc, "st_p1")]
    st_s2 = [Sem(nc, "st_s0"), Sem(nc, "st_s1")]
    ld_s, st_p, st_s = ld_s2[0], st_p2[0], st_s2[0]
    ld_m = Sem(nc, "ld_m")
    st_v = Sem(nc, "st_v")

    st = {"bank_i": 0, "slab_issued": 0}
    bank_free = [[] for _ in range(8)]

    bank_held = [False] * 8

    def acquire():
        i = st["bank_i"] % 8
        st["bank_i"] += 1
        assert not bank_held[i], f"PSUM bank {i} re-acquired before its release was recorded"
        bank_held[i] = True
        return i, banks[i], list(bank_free[i])

    def release_bank(i, toks):
        bank_free[i] = list(toks)
        bank_held[i] = False

    slot_free = [[] for _ in range(RING)]
    NSLAB = L * NSLAB_L
    slab_tok = {}

    released = set()

    def issue_slab(i):
        s = i % RING
        assert i < RING or (i - RING) in released, f"slab {i}: slot still held by {i - RING}"
        B.waits("pool", slot_free[s])
        nc.gpsimd.dma_start(out=ring[s][:], in_=wst[i], max_dma_last_dim=2048).then_inc(slot_sem[s].h, 16)
        slot_sem[s].v += 16
        tok = (slot_sem[s], slot_sem[s].v)
        slab_tok[i] = tok

    def get_slab(i):
        while st["slab_issued"] <= i:
            issue_slab(st["slab_issued"])
            st["slab_issued"] += 1
        return ring[i % RING], slab_tok[i]

    def release_slab(i, tok):
        slot_free[i % RING] = [tok]
        released.add(i)
        while st["slab_issued"] < NSLAB and (st["slab_issued"] - RING) in released:
            issue_slab(st["slab_issued"])
            st["slab_issued"] += 1

    B.dma("sp", ld0, par[:], par_d.rearrange("l p n -> p l n"))
    B.dma("sp", ld0, cst[:], cst_d)
    B.dma("sp", ld0, icnt[:], icnt_d.rearrange("p (g j) -> p g j", g=4))
    t_ld0 = B.dma("sp", ld0, hmask[:], hmask_d)
    for i in range(RING):
        get_slab(i)
    for e in ("pe", "act", "dve"):
        B.wait(e, t_ld0)
    t_ones = B.op("dve", lambda e: e.memset(onesf[:], 1.0))
    B.wait("pe", t_ones)

    def p0_tile(xt, xn, junk, q, ln, x_deps, xn_free):
        t4 = p0_front(xt, xn, junk, q, x_deps, xn_free)
        pe_last, evs = p0_back(xn, q, ln, t4)
        return t4, pe_last, evs

    def p0_front(xt, xn, junk, q, x_deps, xn_free):
        col = q
        t1 = B.op("act", lambda e: e.activation(out=junk, in_=xt, func=AF.Square, accum_out=sm[:, col:col + 1]),
                  deps=list(x_deps) + [junk_tok[0]])
        junk_tok[0] = t1
        t2 = B.op("act", lambda e: e.activation(out=sm[:, 10 + col:11 + col], in_=sm[:, col:col + 1], func=AF.Sqrt,
                                                scale=1.0 / D, bias=EPS), deps=[t1])
        t3 = B.op("dve", lambda e: e.reciprocal(out=sm[:, 20 + col:21 + col], in_=sm[:, 10 + col:11 + col]), deps=[t2])
        t4 = B.op("act", lambda e: e.activation(out=xn, in_=xt, func=AF.Copy, scale=sm[:, 20 + col:21 + col]),
                  deps=[t3] + list(xn_free))
        return t4

    def p0_back(xn, q, ln, t4):
        evs = []
        pe_last = None
        for j in range(4):
            bi, bk, bdeps = acquire()
            B.waits("pe", [t4] + bdeps)
            ins = None
            for kk in range(4):
                k = 4 * j + kk
                ins = nc.tensor.transpose(bk[:, kk * 128:(kk + 1) * 128], xn[:, k * 128:(k + 1) * 128], ident)
            tp = B.pe_mark(ins)
            pe_last = tp
            gb = par[:, ln, P_GPRE + 4 * j:P_GPRE + 4 * j + 4].unsqueeze(2).broadcast_to([128, 4, 128])
            te = B.op("dve", lambda e, bk=bk, j=j, gb=gb: e.tensor_tensor(
                out=RH[:, 4 * j:4 * j + 4, q * 128:(q + 1) * 128],
                in0=bk[:].rearrange("p (a b) -> p a b", a=4), in1=gb, op=ALU.mult), deps=[tp])
            release_bank(bi, [te])
            evs.append(te)
        return pe_last, evs

    def proj(slab, slab_t, K, koff, src, consumer, tiles=TILES):
        last = None
        for ti, (c0, w) in enumerate(tiles):
            bi, bk, bdeps = acquire()
            tok = B.mm(bk[:, 0:w], [(slab[:, (koff + k) * 128:(koff + k + 1) * 128], src(k, c0, c0 + w)) for k in range(K)],
                       deps=[slab_t] + bdeps)
            last = tok
            consumer(bi, bk, ti, c0, w, tok)
        return last

    hsrc = lambda k, a, b: RH[:, k, a:b]

    ln_tv = [None]

    def ln_finish(S1, S2, tmp, deps, tiles=TILES):
        return [ln_tile(S1, S2, tmp, deps, c0, w) for (c0, w) in tiles]

    def ln_tile(S1, S2, tmp, deps, c0, w):
        toks = []
        tv = ln_tv[0]
        if True:
            b1i, b1, d1 = acquire()
            tA = B.mm(b1[:, 0:w], [(onesf[:], S1[:, c0:c0 + w])], deps=list(deps) + d1)
            b2i, b2, d2 = acquire()
            tB = B.mm(b2[:, 0:w], [(onesf[:], S2[:, c0:c0 + w])], deps=list(deps) + d2)
            tm = B.op("dve", lambda e: e.tensor_scalar(out=S1[:, c0:c0 + w], in0=b1[:, 0:w], scalar1=1.0 / 1024,
                                                       scalar2=None, op0=ALU.mult), deps=[tA, tB])
            release_bank(b1i, [tm])
            tq = B.op("dve", lambda e: e.tensor_tensor(out=tmp[:, 0:w], in0=S1[:, c0:c0 + w], in1=S1[:, c0:c0 + w],
                                                       op=ALU.mult), deps=[tm, tv])
            tv = B.op("dve", lambda e: e.scalar_tensor_tensor(out=S2[:, c0:c0 + w], in0=b2[:, 0:w], scalar=1.0 / 1024,
                                                              in1=tmp[:, 0:w], op0=ALU.mult, op1=ALU.subtract),
                      deps=[tq])
            release_bank(b2i, [tv])
            ts = B.op("act", lambda e: e.activation(out=S2[:, c0:c0 + w], in_=S2[:, c0:c0 + w], func=AF.Sqrt,
                                                    scale=1.0, bias=EPS), deps=[tv])
            tr = B.op("dve", lambda e: e.reciprocal(out=S2[:, c0:c0 + w], in_=S2[:, c0:c0 + w]), deps=[ts])
            tn = B.op("dve", lambda e: e.scalar_tensor_tensor(out=S1[:, c0:c0 + w], in0=S1[:, c0:c0 + w], scalar=-1.0,
                                                              in1=S2[:, c0:c0 + w], op0=ALU.mult, op1=ALU.mult),
                      deps=[tr])
            ln_tv[0] = tv
        return tn

    out_stores = []
    junk_tok = [None]
    hT_toks = [[] for _ in range(NQ)]

    for l in range(L):
        sb = l * NSLAB_L
        TL = TILES if l == 0 else TILES2
        hs = 0 if l == 0 else 128
        q0 = 0 if l == 0 else 1
        if l == 0:
            xb = [RS[:, 0:2048], RS[:, 2048:4096]]
            xnb = [RS[:, 4096:6144], RS[:, 6144:8192]]
            junk = RS_b[:, 16384:18432]
            xb_free = [[], []]
            xn_free = [[], []]
            for q in range(NQ):
                tl = B.dma("sp", ldx[q % 2], xb[q % 2], xin[q * 128:(q + 1) * 128, :], deps=xb_free[q % 2])
                t4, pel, evs = p0_tile(xb[q % 2], xnb[q % 2], junk, q, 0, [tl], xn_free[q % 2])
                xb_free[q % 2] = [t4]
                xn_free[q % 2] = [pel]
                hT_toks[q] = evs

        yconv = lambda c, a, b: RS[:, c * NT + a: c * NT + b]
        S1 = RA_f[:, 10240:11520]
        S2 = RA_f[:, 11520:12800]
        sq = RA_f[:, 12800:14080]
        dgb = [RA[:, 10240 + i * 3968:10240 + (i + 1) * 3968].rearrange("p (k m) -> p k m", m=128) for i in range(2)]
        zbP = [RA[:, 18176:19360], RA[:, 28160:29344]]
        zbSf = [RA[:, 19360:19968], RA[:, 29344:29952]]
        zbS = [z.rearrange("p (b j) -> p b j", j=38) for z in zbSf]
        zSf = [RW[:, 0:608], RW[:, 608:1216]]
        zS = [z.rearrange("p (b j) -> p b j", j=38) for z in zSf]
        zP128 = [RW[:, 1216:1344], RW[:, 1344:1472]]
        sg = [RW[:, 1536:2048], RW[:, 2048:2560]]
        tmpA = RW[:, 2560:3072]
        pcv = [RW[:, 3072:4352], RW[:, 4352:5632]]
        NDTC = [8] * 7 + [2]
        pcv_rd = [None, None]
        t_z0 = B.op("dve", lambda e: e.memset(zbP[0][:, 0:30], 0.0))
        t_z1 = B.op("dve", lambda e: e.memset(zbP[1][:, 0:30], 0.0))
        sg_free = [[], []]
        sgi = 0
        si = sb
        conv_pe = [None] * 8
        zst = [None] * 8
        st_s_hist = [None] * 8
        st_p_hist = [None] * 8
        CG = [(0, 384), (384, 384), (768, 384)] if l == 0 else [(128, 512), (640, 512)]
        NCG = len(CG)

        def build_diag(c):
            dg = dgb[c % 2]
            deps = [conv_pe[c - 2]] if c >= 2 else []
            t = None
            for k in range(NDTC[c], 31):
                t = B.op("act", lambda e, k=k: e.activation(out=dg[:, k, :], in_=ident, func=AF.Copy,
                                                            scale=par[:, l, P_CW + c * 31 + k:P_CW + c * 31 + k + 1]),
                         deps=deps)
            return t

        def proj_glu(c):
            nonlocal sgi, si
            p = c % 2
            av, av_t = get_slab(si)
            ag, ag_t = get_slab(si + 1)
            old = [conv_pe[c - 2], st_s_hist[c - 2], st_p_hist[c - 2]] if c >= 2 else [t_z0, t_z1]
            tpre = B.dma("sp", ld_s2[p], zS[p][:, :, 0:30], sconv_d[l, c], deps=old)
            tcast = B.op("act", lambda e: e.activation(out=zbS[p][:, :, 0:30], in_=zS[p][:, :, 0:30], func=AF.Copy),
                         deps=[tpre] + old)
            z_toks = [tcast]
            f32_toks = []
            last_pe = None
            for ti, (c0, w) in enumerate(TILES):
                bai, ba, da = acquire()
                hdeps = [t for qq in range(c0 // 128, (c0 + w) // 128) for t in hT_toks[qq]] if c == 0 else []
                ta = B.mm(ba[:, 0:w], [(av[:, k * 128:(k + 1) * 128], RH[:, k, c0:c0 + w]) for k in range(16)],
                          deps=[av_t] + da + hdeps)
                bgi, bg, dg_ = acquire()
                tg = B.mm(bg[:, 0:w], [(ag[:, k * 128:(k + 1) * 128], RH[:, k, c0:c0 + w]) for k in range(16)],
                          deps=[ag_t] + dg_)
                last_pe = tg
                sgb = sg[sgi % 2]
                tsg = B.op("act", lambda e: e.activation(out=sgb[:, 0:w], in_=bg[:, 0:w], func=AF.Sigmoid),
                           deps=[tg] + sg_free[sgi % 2])
                release_bank(bgi, [tsg])
                if ti < 2:
                    tz = B.op("dve", lambda e: e.tensor_tensor(out=zbP[p][:, 30 + c0:30 + c0 + w], in0=ba[:, 0:w],
                                                               in1=sgb[:, 0:w], op=ALU.mult), deps=[ta, tsg] + old)
                    z_toks.append(tz)
                else:
                    tz1 = B.op("dve", lambda e: e.tensor_tensor(out=zbP[p][:, 30 + 1024:30 + 1152], in0=ba[:, 0:128],
                                                                in1=sgb[:, 0:128], op=ALU.mult), deps=[ta, tsg] + old)
                    tz2 = B.op("dve", lambda e: e.tensor_tensor(out=zP128[p], in0=ba[:, 0:128], in1=sgb[:, 0:128],
                                                                op=ALU.mult), deps=[ta, tsg] + old)
                    tz3 = B.op("dve", lambda e: e.tensor_tensor(
                        out=zbS[p][:, :, 30:38], in0=ba[:, 128:256].rearrange("p (b j) -> p b j", j=8),
                        in1=sgb[:, 128:256].rearrange("p (b j) -> p b j", j=8), op=ALU.mult), deps=[ta, tsg] + old)
                    tz = B.op("dve", lambda e: e.tensor_tensor(
                        out=zS[p][:, :, 30:38], in0=ba[:, 128:256].rearrange("p (b j) -> p b j", j=8),
                        in1=sgb[:, 128:256].rearrange("p (b j) -> p b j", j=8), op=ALU.mult), deps=[ta, tsg] + old)
                    z_toks += [tz1, tz3]
                    f32_toks = [tz2, tz]
                release_bank(bai, [tz])
                sg_free[sgi % 2] = [tz]
                sgi += 1
            release_slab(si, last_pe)
            release_slab(si + 1, last_pe)
            si += 2
            st_p_hist[c] = B.dma("sp", st_p2[p], convp_o[l, c], zP128[p][:, 98:128], deps=f32_toks)
            st_s_hist[c] = B.dma("sp", st_s2[p], convs_o[l, c], zSf[p], deps=f32_toks + [tpre])
            out_stores.extend([st_p_hist[c], st_s_hist[c]])
            zst[c] = z_toks

        def conv(c, t_dg, s_toks):
            p = c % 2
            dg = dgb[p]
            NDT = NDTC[c]
            pc = pcv[p]
            cwk = lambda k: par[:, l, P_CW + c * 31 + k:P_CW + c * 31 + k + 1]
            cbk = par[:, l, P_CB + c:P_CB + c + 1]
            pcP = pc[:, hs:1152]
            pcS = pc[:, 1152:1280].rearrange("p (b j) -> p b j", j=8)
            tP = B.op("dve", lambda e: e.tensor_scalar(out=pcP, in0=zbP[p][:, hs:1152], scalar1=cwk(0), scalar2=cbk,
                                                       op0=ALU.mult, op1=ALU.add), deps=zst[c] + [pcv_rd[p]])
            tS = B.op("dve", lambda e: e.tensor_scalar(out=pcS, in0=zbS[p][:, :, 0:8], scalar1=cwk(0), scalar2=cbk,
                                                       op0=ALU.mult, op1=ALU.add), deps=zst[c] + [pcv_rd[p]])
            for k in range(1, NDT):
                tP = B.op("dve", lambda e: e.scalar_tensor_tensor(out=pcP, in0=zbP[p][:, hs + k:1152 + k], scalar=cwk(k),
                                                                  in1=pcP, op0=ALU.mult, op1=ALU.add), deps=[tP])
                tS = B.op("dve", lambda e: e.scalar_tensor_tensor(out=pcS, in0=zbS[p][:, :, k:k + 8], scalar=cwk(k),
                                                                  in1=pcS, op0=ALU.mult, op1=ALU.add), deps=[tS])
            bks = [acquire() for _ in range(NCG + 1)]
            B.waits("pe", zst[c] + [t_dg] + [d for (_, _, dd) in bks for d in dd])
            ins = None
            for k in range(NDT, 31):
                for gi_, (o, n) in enumerate(CG):
                    ins = nc.tensor.matmul(bks[gi_][1][:, 0:n], lhsT=dg[:, k, :], rhs=zbP[p][:, o + k:o + k + n],
                                           start=(k == NDT), stop=(k == 30))
                ins = nc.tensor.matmul(bks[NCG][1][:, 0:128].rearrange("p (b j) -> p b j", j=8), lhsT=dg[:, k, :],
                                       rhs=zbS[p][:, :, k:k + 8], start=(k == NDT), stop=(k == 30))
            tp = B.pe_mark(ins)
            conv_pe[c] = tp
            cb = par[:, l, P_CB + c:P_CB + c + 1]
            ev = []
            for gi_, (o, n) in enumerate(CG):
                te = B.op("dve", lambda e: e.tensor_tensor(out=yconv(c, o, o + n), in0=bks[gi_][1][:, 0:n],
                                                           in1=pc[:, o:o + n], op=ALU.add), deps=[tp, tP])
                release_bank(bks[gi_][0], [te])
                ev.append(te)
            te = B.op("dve", lambda e: e.tensor_tensor(out=yconv(c, 1152, 1280), in0=bks[NCG][1][:, 0:128],
                                                       in1=pc[:, 1152:1280], op=ALU.add), deps=[tp, tS])
            release_bank(bks[NCG][0], [te])
            ev.append(te)
            pcv_rd[p] = te
            yc = yconv(c, hs, NT)
            s1v, s2v, sqv = S1[:, hs:NT], S2[:, hs:NT], sq[:, hs:NT]
            tsq = B.op("act", lambda e: e.activation(out=sqv, in_=yc, func=AF.Square), deps=ev + s_toks)
            if c == 0:
                ts1 = B.op("dve", lambda e: e.tensor_copy(out=s1v, in_=yc), deps=ev)
                ts2 = B.op("dve", lambda e: e.tensor_copy(out=s2v, in_=sqv), deps=[tsq])
            else:
                ts1 = B.op("dve", lambda e: e.tensor_tensor(out=s1v, in0=s1v, in1=yc, op=ALU.add), deps=ev + s_toks)
                ts2 = B.op("dve", lambda e: e.tensor_tensor(out=s2v, in0=s2v, in1=sqv, op=ALU.add), deps=[tsq, ts1])
            return [ts1, ts2]

        t_dgs = [None] * 8
        t_dgs[0] = build_diag(0)
        proj_glu(0)
        s_toks = []
        for c in range(8):
            if c + 1 < 8:
                t_dgs[c + 1] = build_diag(c + 1)
                proj_glu(c + 1)
            s_toks = conv(c, t_dgs[c], s_toks)
        st_p_tok = [st_p_hist[6], st_p_hist[7]]
        st_s_tok = [st_s_hist[6], st_s_hist[7]]
        ln_toks = ln_finish(S1, S2, tmpA, s_toks, TL)
        for c in range(8):
            yc = yconv(c, hs, NT)
            t1 = B.op("dve", lambda e: e.tensor_tensor(out=yc, in0=yc, in1=S2[:, hs:NT], op=ALU.mult), deps=ln_toks)
            t2 = B.op("dve", lambda e: e.tensor_tensor(out=yc, in0=yc, in1=S1[:, hs:NT], op=ALU.add), deps=[t1])
            t3n = B.op("act", lambda e: e.activation(out=yc, in_=yc, func=AF.Silu,
                                                     bias=par[:, l, P_LB + c:P_LB + c + 1],
                                                     scale=par[:, l, P_LG + c:P_LG + c + 1]), deps=[t2])
            sl, sl_t = get_slab(si)

            def consA(bi, bk, ti, c0, w, tok, c=c, t3n=t3n):
                nonlocal sgi
                sgb = sg[sgi % 2]
                t1 = B.op("act", lambda e: e.activation(out=sgb[:, 0:w], in_=bk[:, 0:w], func=AF.Silu),
                          deps=[tok] + sg_free[sgi % 2])
                release_bank(bi, [t1])
                t2 = B.op("dve", lambda e: e.tensor_tensor(out=actA(c, c0, c0 + w), in0=sgb[:, 0:w],
                                                           in1=yconv(c, c0, c0 + w), op=ALU.mult),
                          deps=[t1, t3n])
                sg_free[sgi % 2] = [t2]
                sgi += 1
            lp = proj(sl, sl_t, 16, 0, hsrc, consA, TL)
            release_slab(si, lp)
            si += 1
        B.barrier(extra=[st_p_tok, st_s_tok], pe=False)

        vT = lambda c, a, b: RS[:, c * NT + a: c * NT + b]
        vbf = RA[:, 16 * NT:24 * NT].rearrange("p (q f) -> p q f", f=1024)
        S1 = RA_f[:, 5120:6400]
        S2 = RA_f[:, 6400:7680]
        sq = RA_f[:, 7680:8960]
        bsb = RW[:, 0:2048]
        tmpB = RW[:, 2048:2560]
        wsTm = RW_b[:, 5120:6144].rearrange("p (g t) -> p g t", g=8)
        wsSm = RW_b[:, 6144:7168].rearrange("p (g t) -> p g t", g=8)
        stg = RW[:, 3584:4608]
        vn32 = RW[:, 3584:4608]
        t_bs = B.dma("sp", ld_m, bsb, bsrow_d[l])
        t_w1 = B.dma("sp", ld_m, stg, wsT_d[l])
        t_m1 = B.op("dve", lambda e: e.tensor_tensor(out=wsTm, in0=stg.rearrange("p (g t) -> p g t", g=8),
                                                     in1=maskP.unsqueeze(1).broadcast_to([128, 8, 128]), op=ALU.mult),
                    deps=[t_w1])
        t_w2 = B.dma("sp", ld_m, stg, wsS_d[l], deps=[t_m1])
        t_m2 = B.op("dve", lambda e: e.tensor_tensor(out=wsSm, in0=stg.rearrange("p (g t) -> p g t", g=8),
                                                     in1=maskS.unsqueeze(1).broadcast_to([128, 8, 128]), op=ALU.mult),
                    deps=[t_w2])
        s_toks = []
        for c in range(8):
            sl, sl_t = get_slab(si)
            ev = []

            def consV(bi, bk, ti, c0, w, tok, c=c, ev=ev):
                t1 = B.op("act", lambda e: e.activation(out=vT(c, c0, c0 + w), in_=bk[:, 0:w], func=AF.Copy), deps=[tok])
                release_bank(bi, [t1])
                ev.append(t1)
            lp = proj(sl, sl_t, 16, 0, hsrc, consV, TL)
            release_slab(si, lp)
            si += 1
            vc = vT(c, hs, NT)
            s1v, s2v, sqv = S1[:, hs:NT], S2[:, hs:NT], sq[:, hs:NT]
            tsq = B.op("act", lambda e: e.activation(out=sqv, in_=vc, func=AF.Square), deps=ev + s_toks)
            if c == 0:
                ts1 = B.op("dve", lambda e: e.tensor_copy(out=s1v, in_=vc), deps=ev)
                ts2 = B.op("dve", lambda e: e.tensor_copy(out=s2v, in_=sqv), deps=[tsq])
            else:
                ts1 = B.op("dve", lambda e: e.tensor_tensor(out=s1v, in0=s1v, in1=vc, op=ALU.add), deps=ev + s_toks)
                ts2 = B.op("dve", lambda e: e.tensor_tensor(out=s2v, in0=s2v, in1=sqv, op=ALU.add), deps=[tsq, ts1])
            s_toks = [ts1, ts2]
        ln_toks = [None, None, None]
        vb_toks = []
        tvs_box = [None]
        t3f = [RW[:, 4608:5888], RW[:, 5888:7168]]
        t3_rd = [[], []]
        proj_state = {}
        si_b = si

        def b_norm(ti):
            c0, w = TL[ti]
            n_toks = []
            for c in range(8):
                vc = vT(c, c0, c0 + w)
                t1 = B.op("dve", lambda e: e.tensor_tensor(out=vc, in0=vc, in1=S2[:, c0:c0 + w], op=ALU.mult),
                          deps=[ln_toks[ti]])
                t2 = B.op("dve", lambda e: e.tensor_tensor(out=vc, in0=vc, in1=S1[:, c0:c0 + w], op=ALU.add), deps=[t1])
                t3 = B.op("act", lambda e: e.activation(out=vc, in_=vc, func=AF.Identity,
                                                        bias=par[:, l, P_GLB + c:P_GLB + c + 1],
                                                        scale=par[:, l, P_GLG + c:P_GLG + c + 1]), deps=[t2])
                n_toks.append(t3)
            return n_toks

        def b_transposes(ti, n_toks):
            c0, w = TL[ti]
            for q in range(c0 // 128, (c0 + w) // 128):
                evq = []
                for hh in range(2):
                    bi, bk, bd = acquire()
                    B.waits("pe", n_toks + bd)
                    ins = None
                    for cc in range(4):
                        c = 4 * hh + cc
                        ins = nc.tensor.transpose(bk[:, cc * 128:(cc + 1) * 128], vT(c, q * 128, (q + 1) * 128), ident)
                    tp = B.pe_mark(ins)
                    te = B.op("act", lambda e: e.activation(out=vbf[:, q, hh * 512:(hh + 1) * 512], in_=bk[:],
                                                            func=AF.Copy), deps=[tp])
                    rel = [te]
                    if q == NQ - 1:
                        t32 = B.op("act", lambda e: e.activation(out=vn32[:, hh * 512:(hh + 1) * 512], in_=bk[:],
                                                                 func=AF.Copy), deps=[tp, te])
                        rel.append(t32)
                        evq.append(t32)
                    release_bank(bi, rel)
                    vb_toks.append(te)
                if q == NQ - 1:
                    tvs_box[0] = B.dma("sp", st_v, vs_o[l], vn32, deps=evq)
                    out_stores.append(tvs_box[0])

        def b_proj_pe(g):
            bs_i, bu_i = si_b + 2 * g, si_b + 2 * g + 1
            buf = t3f[g % 2]
            bs_, bs_t = get_slab(bs_i)
            sil = []
            lp = None
            for ti, (c0, w) in enumerate(TL):
                psi, ps, dps = acquire()
                tsl = B.mm(ps[:, 0:w], [(bs_[:, k * 128:(k + 1) * 128], RH[:, k, c0:c0 + w]) for k in range(16)],
                           deps=[bs_t] + dps)
                lp = tsl
                t1 = B.op("act", lambda e: e.activation(out=buf[:, c0:c0 + w], in_=ps[:, 0:w], func=AF.Silu),
                          deps=[tsl] + t3_rd[g % 2])
                release_bank(psi, [t1])
                sil.append(t1)
            release_slab(bs_i, lp)
            bu, bu_t = get_slab(bu_i)
            pus = []
            for ti, (c0, w) in enumerate(TL):
                pui, pu, dpu = acquire()
                tu = B.mm(pu[:, 0:w], [(bu[:, k * 128:(k + 1) * 128], RH[:, k, c0:c0 + w]) for k in range(16)],
                          deps=[bu_t] + dpu)
                lp = tu
                pus.append((pui, pu, tu))
            release_slab(bu_i, lp)
            proj_state[g] = (sil, pus)

        def b_proj_dve(g):
            sil, pus = proj_state[g]
            buf = t3f[g % 2]
            toks = []
            for ti, (c0, w) in enumerate(TL):
                pui, pu, tu = pus[ti]
                t2 = B.op("dve", lambda e: e.tensor_tensor(out=buf[:, c0:c0 + w], in0=pu[:, 0:w], in1=buf[:, c0:c0 + w],
                                                           op=ALU.mult), deps=[tu, sil[ti]])
                release_bank(pui, [t2])
                toks.append(t2)
            proj_state[g] = toks

        tmpm_rd = [None]

        def b_mix(g):
            toks = proj_state[g]
            buf = t3f[g % 2]
            rd = []
            for ti, (c0, w) in enumerate(TL):
                pmi, pm, dpm = acquire()
                B.waits("pe", vb_toks + dpm + [t_m1, t_m2, t_w2])
                ins = None
                for j in range(w // 128):
                    q = c0 // 128 + j
                    wm = wsSm if q == NQ - 1 else wsTm
                    so = 1 if q == NQ - 1 else 0
                    ins = nc.tensor.matmul(pm[:, j * 128:(j + 1) * 128], lhsT=vbf[:, q, g * 128:(g + 1) * 128],
                                           rhs=wm[:, g, :], start=True, stop=True)
                tm = B.pe_mark(ins)
                npr = min(w, 1152 - c0) // 128
                tmpm = tmpB
                ta_ = B.op("dve", lambda e: e.tensor_tensor(
                    out=tmpm[:, 0:npr * 128].rearrange("p (j t) -> p j t", t=128),
                    in0=pm[:, 0:npr * 128].rearrange("p (j t) -> p j t", t=128),
                    in1=bsb[:, g * 128:(g + 1) * 128].unsqueeze(1).broadcast_to([128, npr, 128]), op=ALU.add),
                    deps=[tm, t_w2, tmpm_rd[0]])
                if npr * 128 < w:
                    ta_ = B.op("dve", lambda e: e.tensor_tensor(out=tmpm[:, npr * 128:w], in0=pm[:, npr * 128:w],
                                                                in1=bsb[:, 1024 + g * 128:1024 + (g + 1) * 128],
                                                                op=ALU.add), deps=[tm, t_w2])
                release_bank(pmi, [ta_])
                t3 = B.op("dve", lambda e: e.tensor_tensor(out=actB(g, c0, c0 + w), in0=tmpm[:, 0:w], in1=buf[:, c0:c0 + w],
                                                           op=ALU.mult), deps=[ta_, toks[ti]])
                tmpm_rd[0] = t3
                rd.append(t3)
            t3_rd[g % 2] = rd

        ln_toks[0] = ln_tile(S1, S2, tmpB, s_toks, *TL[0])
        nt0 = b_norm(0)
        b_proj_pe(0)
        b_proj_dve(0)
        b_transposes(0, nt0)
        ln_toks[1] = ln_tile(S1, S2, tmpB, s_toks, *TL[1])
        ln_toks[2] = ln_tile(S1, S2, tmpB, s_toks, *TL[2])
        nt1 = b_norm(1)
        b_proj_pe(1)
        b_proj_dve(1)
        b_transposes(1, nt1)
        nt2 = b_norm(2)
        b_transposes(2, nt2)
        for g in range(8):
            b_mix(g)
            if g + 2 < 8:
                b_proj_pe(g + 2)
                b_proj_dve(g + 2)
        si = si_b + 16
        tvs = tvs_box[0]
        tokB_end = B.last["dve"]
        B.barrier(extra=[tvs], pe=False)

        dT = lambda c, a, b: RS_b[:, c * NT + a: c * NT + b]
        cP = RW[:, 0:1168]
        w1 = RW[:, 1168:2336]
        w2 = RW[:, 2336:3504]
        cSf = RW[:, 3504:3872]
        cS = cSf.rearrange("p (b j) -> p b j", j=23)
        u1 = RW[:, 3872:4240].rearrange("p (b j) -> p b j", j=23)
        u2 = RW[:, 4240:4608].rearrange("p (b j) -> p b j", j=23)
        t16 = RW[:, 4608:4624]
        slb = [RW[:, 4624:5136], RW[:, 5136:5648]]
        t_c0 = B.op("dve", lambda e: e.memset(cP[:, 0:16], 0.0))
        t_c1 = B.op("act", lambda e: e.memset(w1[:, 0:16], 0.0)) if False else None
        st_p_tok = None
        st_s_tok = None
        c_last = None
        for c in CORDER:
            sl, sl_t = get_slab(si)
            tpre = B.dma("sp", ld_s, cS[:, :, 0:15], spool_d[l, c], deps=[c_last, st_s_tok])
            wdeps = [c_last, st_p_tok, st_s_tok, t_c0]
            ev = []

            def consC(bi, bk, ti, c0, w, tok, ev=ev, wdeps=wdeps):
                if ti < 2:
                    t1 = B.op("act", lambda e: e.activation(out=cP[:, 16 + c0:16 + c0 + w], in_=bk[:, 0:w], func=AF.Copy),
                              deps=[tok] + wdeps)
                else:
                    t0 = B.op("act", lambda e: e.activation(out=cP[:, 16 + 1024:16 + 1152], in_=bk[:, 0:128], func=AF.Copy),
                              deps=[tok] + wdeps)
                    ev.append(t0)
                    t1 = B.op("act", lambda e: e.activation(out=cS[:, :, 15:23],
                                                            in_=bk[:, 128:256].rearrange("p (b j) -> p b j", j=8),
                                                            func=AF.Copy), deps=[tok] + wdeps)
                release_bank(bi, [t1])
                ev.append(t1)
            lp = proj(sl, sl_t, 16, 0, hsrc, consC)
            release_slab(si, lp)
            si += 1
            st_p_tok = B.dma("sp", st_p, poolp_o[l, c], cP[:, 16 + 128 + 1009:16 + 128 + 1024], deps=ev)
            st_s_tok = B.dma("sp", st_s, pools_o[l, c], cSf, deps=ev + [tpre])
            out_stores += [st_p_tok, st_s_tok]
            g = c // 2
            W = 2 ** (g + 1)
            srcP, srcS = cP, cS
            bufsP, bufsS = [w1, w2], [u1, u2]
            tP = None
            tS = None
            for lev in range(1, g + 2):
                sh = 2 ** (lev - 1)
                v0 = 2 ** lev - 1
                dP = bufsP[(lev - 1) % 2]
                dS = bufsS[(lev - 1) % 2]
                tP = B.op("dve", lambda e, dP=dP, srcP=srcP, v0=v0, sh=sh: e.tensor_tensor(
                    out=dP[:, v0:1168], in0=srcP[:, v0:1168], in1=srcP[:, v0 - sh:1168 - sh], op=ALU.add),
                    deps=ev + [tP, c_last])
                tS = B.op("dve", lambda e, dS=dS, srcS=srcS, v0=v0, sh=sh: e.tensor_tensor(
                    out=dS[:, :, v0:23], in0=srcS[:, :, v0:23], in1=srcS[:, :, v0 - sh:23 - sh], op=ALU.add),
                    deps=ev + [tS, tpre, c_last])
                srcP, srcS = dP, dS
            td1 = B.op("dve", lambda e: e.scalar_tensor_tensor(out=dT(c, 0, 1152), in0=srcP[:, 16:1168], scalar=1.0 / W,
                                                               in1=cP[:, 16:1168], op0=ALU.mult, op1=ALU.subtract),
                       deps=[tP])
            td2 = B.op("dve", lambda e: e.tensor_tensor(out=t16, in0=srcP[:, 144:160], in1=icnt[:, g, :], op=ALU.mult),
                       deps=[tP, c_last])
            td3 = B.op("dve", lambda e: e.tensor_tensor(out=dT(c, 128, 144), in0=t16, in1=cP[:, 144:160],
                                                        op=ALU.subtract), deps=[td2, td1])
            td4 = B.op("dve", lambda e: e.scalar_tensor_tensor(
                out=dT(c, 1152, 1280).rearrange("p (b j) -> p b j", j=8), in0=srcS[:, :, 15:23], scalar=1.0 / W,
                in1=cS[:, :, 15:23], op0=ALU.mult, op1=ALU.subtract), deps=[tS])
            c_last = td4
            d_last = [td3, td4]
        pw, pw_t = get_slab(si)
        pw_i = si
        si += 1
        pwbuf = RW_b[:, 11296:13344]
        pw_t = B.op("act", lambda e: e.activation(out=pwbuf, in_=pw[:], func=AF.Copy), deps=[pw_t])
        release_slab(pw_i, pw_t)
        pwv = pwbuf.rearrange("p (g k n) -> p g k n", g=4, k=2)
        slf = [RS[:, 5120:6400], RS[:, 6400:7680]]
        slf_rd = [[], []]
        c_sil = {}
        si_c = si

        def c_proj(dc):
            buf = slf[dc % 2]
            sl, sl_t = get_slab(si_c + dc)
            toks = []

            def cons(bi, bk, ti, c0, w, tok):
                t1 = B.op("act", lambda e: e.activation(out=buf[:, c0:c0 + w], in_=bk[:, 0:w], func=AF.Silu),
                          deps=[tok] + slf_rd[dc % 2])
                release_bank(bi, [t1])
                toks.append(t1)
            lp = proj(sl, sl_t, 16, 0, hsrc, cons, TL)
            release_slab(si_c + dc, lp)
            c_sil[dc] = toks

        def c_pool(dc):
            g = dc // 2
            buf = slf[dc % 2]
            rd = []
            for ti, (c0, w) in enumerate(TL):
                ppi, pp, dpp = acquire()
                tpp = B.mm(pp[:, 0:w], [(pwv[:, g, kc, (dc % 2) * 128:(dc % 2 + 1) * 128], dT(2 * g + kc, c0, c0 + w))
                                        for kc in range(2)], deps=[pw_t] + dpp + d_last)
                t2 = B.op("dve", lambda e: e.scalar_tensor_tensor(out=actC(dc, c0, c0 + w), in0=pp[:, 0:w],
                                                                  scalar=par[:, l, P_PS + dc:P_PS + dc + 1],
                                                                  in1=buf[:, c0:c0 + w], op0=ALU.mult, op1=ALU.mult),
                          deps=[tpp, c_sil[dc][ti]])
                release_bank(ppi, [t2])
                rd.append(t2)
            slf_rd[dc % 2] = rd

        c_proj(0)
        c_proj(1)
        for dc in range(8):
            c_pool(dc)
            if dc + 2 < 8:
                c_proj(dc + 2)
        si = si_c + 8
        tokC_end = B.last["dve"]
        B.barrier(extra=[st_p_tok, st_s_tok], pe=False)

        mT = lambda d, a, b: RS_b[:, d * NT + a: d * NT + b]
        gbuf = [RW[:, 0:1280], RW[:, 1280:2560]]
        tacc = [RW[:, 2560:3840], RW[:, 3840:5120]]
        tmpb = [RW[:, 5120:5632], RW[:, 5632:6144]]
        gbuf_free = [[], []]
        gstep = 0
        tmi = 0
        tmpb_rd = [None, None]
        tacc_rd = [[None] * 3, [None] * 3]
        scc = None
        for d in range(16):
            i_ga, i_ab, i_gb, i_gc = si, si + 1, si + 2, si + 3
            nsl = 4
            if d % 2 == 0:
                scc_i = si + 4
                nsl = 5
            gate_idx = [i_ga, i_gb, i_gc]
            ta_ = tacc[d % 2]
            ta_tok = [None, None, None]
            last_pe = None
            for i in range(3):
                gsl, gsl_t = get_slab(gate_idx[i])
                gb_ = gbuf[gstep % 2]
                sig = []
                lp = None
                for ti, (c0, w) in enumerate(TL):
                    bgi, bg, dbg = acquire()
                    tg = B.mm(bg[:, 0:w], [(gsl[:, k * 128:(k + 1) * 128], RH[:, k, c0:c0 + w]) for k in range(16)],
                              deps=[gsl_t] + dbg)
                    lp = tg
                    t1 = B.op("act", lambda e: e.activation(out=gb_[:, c0:c0 + w], in_=bg[:, 0:w], func=AF.Sigmoid),
                              deps=[tg] + gbuf_free[gstep % 2])
                    release_bank(bgi, [t1])
                    sig.append(t1)
                release_slab(gate_idx[i], lp)
                if i < 2:
                    wsl, wsl_t = get_slab(i_ab)
                    ko = 8 * i
                else:
                    wsl, wsl_t = get_slab(scc_i)
                    ko = 8 * (d % 2)
                rd = []
                for ti, (c0, w) in enumerate(TL):
                    byi, by, dby = acquire()
                    ty = B.mm(by[:, 0:w], [(wsl[:, (ko + k) * 128:(ko + k + 1) * 128], acts[i](k, c0, c0 + w))
                                           for k in range(8)],
                              deps=[wsl_t] + dby + ([tokC_end if i == 2 else tokB_end] if d == 0 else []))
                    last_pe = ty
                    if i == 0:
                        t2 = B.op("dve", lambda e: e.tensor_tensor(out=ta_[:, c0:c0 + w], in0=by[:, 0:w],
                                                                   in1=gb_[:, c0:c0 + w], op=ALU.mult),
                                  deps=[ty, sig[ti], tacc_rd[d % 2][ti]])
                        release_bank(byi, [t2])
                        rd.append(t2)
                        ta_tok[ti] = t2
                    else:
                        tm_ = tmpb[tmi % 2]
                        tmk = tmi % 2
                        tmi += 1
                        t2 = B.op("dve", lambda e: e.tensor_tensor(out=tm_[:, 0:w], in0=by[:, 0:w],
                                                                   in1=gb_[:, c0:c0 + w], op=ALU.mult),
                                  deps=[ty, sig[ti], tmpb_rd[tmk]])
                        release_bank(byi, [t2])
                        rd.append(t2)
                        if i == 1:
                            ta_tok[ti] = B.op("dve", lambda e: e.tensor_tensor(out=ta_[:, c0:c0 + w], in0=ta_[:, c0:c0 + w],
                                                                               in1=tm_[:, 0:w], op=ALU.add),
                                              deps=[t2, ta_tok[ti]])
                            tmpb_rd[tmk] = ta_tok[ti]
                        else:
                            tfin = B.op("dve", lambda e: e.tensor_tensor(out=mT(d, c0, c0 + w), in0=ta_[:, c0:c0 + w],
                                                                         in1=tm_[:, 0:w], op=ALU.add),
                                        deps=[t2, ta_tok[ti]])
                            tmpb_rd[tmk] = tfin
                            tacc_rd[d % 2][ti] = tfin
                gbuf_free[gstep % 2] = rd
                gstep += 1
                if i == 1:
                    release_slab(i_ab, last_pe)
                if i == 2 and d % 2 == 1:
                    release_slab(scc_i, last_pe)
            si += nsl
        tokG_end = B.last["dve"]
        B.barrier(pe=False)

        y0 = lambda q, a, b: RA_f[:, q * 1024 + a: q * 1024 + b]
        gpost = RA_f[:, 10240:12288]
        xn = RA_f[:, 12288:14336]
        junk = RA[:, 28672:30720]
        junk5 = RA[:, 28672:29184]
        xt_ = [RW[:, 0:2048], RW[:, 2048:4096]]
        t1_ = [RW[:, 4096:4608], RW[:, 4608:5120]]
        xsrc = xin if l == 0 else x1s
        t_gp = B.dma("sp", ld_m, gpost, gpost_d[l])
        for pq in range(2):
            fsl = [get_slab(si + i) for i in range(4)]
            tp = None
            for q in range(q0, NQ):
                bi, bk, dd = acquire()
                B.waits("pe", dd + [tokG_end])
                ins = None
                for d in range(16):
                    if d % 4 == 0:
                        B.wait("pe", fsl[d // 4][1])
                    ins = nc.tensor.matmul(bk[:], lhsT=mT(d, q * 128, (q + 1) * 128),
                                           rhs=fsl[d // 4][0][:, (d % 4) * 512:(d % 4 + 1) * 512],
                                           start=(d == 0), stop=(d == 15))
                tp = B.pe_mark(ins)
                ta = B.op("act", lambda e: e.activation(out=y0(q, pq * 512, (pq + 1) * 512), in_=bk[:], func=AF.Copy),
                          deps=[tp])
                tb = B.op("act", lambda e: e.activation(out=junk5, in_=bk[:], func=AF.Square,
                                                        accum_out=sm[:, 30 + q * 4 + pq:31 + q * 4 + pq]),
                          deps=[ta, junk_tok[0]])
                junk_tok[0] = tb
                release_bank(bi, [tb])
            for i in range(4):
                release_slab(si + i, tp)
            si += 4
        fs = [get_slab(si + i) for i in range(8)]
        xt_free = [[], []]
        t1_free = [[], []]
        xn_free = []
        t1i = 0
        def f2_mm(q):
            b2i, b2, d2 = acquire()
            b3i, b3, d3 = acquire()
            B.waits("pe", d2 + d3)
            ins = None
            for d in range(16):
                for (bk, qq) in ((b2, 0), (b3, 1)):
                    slab, slab_t = fs[(d // 4) * 2 + qq]
                    if d % 4 == 0:
                        B.wait("pe", slab_t)
                    ins = nc.tensor.matmul(bk[:], lhsT=mT(d, q * 128, (q + 1) * 128),
                                           rhs=slab[:, (d % 4) * 512:(d % 4 + 1) * 512], start=(d == 0), stop=(d == 15))
            return b2i, b2, b3i, b3, B.pe_mark(ins)

        pend = f2_mm(q0)
        xn2 = [xn, RW[:, 5120:7168]]
        xn2_free = [[], []]
        pend_back = None
        for q in range(q0, NQ):
            xt = xt_[q % 2]
            xdeps = list(xt_free[q % 2])
            if l == 1:
                xdeps += [(stx[0], stx_final[0]), (stx[1], stx_final[1])]
            tl = B.dma("sp", ldx[q % 2], xt, xsrc[q * 128:(q + 1) * 128, :], deps=xdeps)
            b2i, b2, b3i, b3, tp = pend
            f2_last = tp
            if q + 1 < NQ:
                pend = f2_mm(q + 1)
                f2_last = pend[4]
            tsq = None
            for (bk, jj) in ((b2, 2), (b3, 3)):
                tsq = B.op("act", lambda e, bk=bk, jj=jj: e.activation(out=junk5, in_=bk[:], func=AF.Square,
                                                                       accum_out=sm[:, 30 + q * 4 + jj:31 + q * 4 + jj]),
                           deps=[tp, tsq, junk_tok[0]])
                junk_tok[0] = tsq
            tr1 = B.op("dve", lambda e: e.reduce_sum(out=sm[:, 70 + q:71 + q], in_=sm[:, 30 + q * 4:34 + q * 4], axis=AX.X),
                       deps=[tsq])
            tr2 = B.op("act", lambda e: e.activation(out=sm[:, 80 + q:81 + q], in_=sm[:, 70 + q:71 + q], func=AF.Sqrt,
                                                     scale=1.0 / D, bias=EPS), deps=[tr1])
            tr3 = B.op("dve", lambda e: e.reciprocal(out=sm[:, 80 + q:81 + q], in_=sm[:, 80 + q:81 + q]), deps=[tr2])
            rstd = sm[:, 80 + q:81 + q]
            xlast = None
            for jj in range(4):
                src = y0(q, jj * 512, (jj + 1) * 512) if jj < 2 else (b2 if jj == 2 else b3)[:]
                tb1 = t1_[t1i % 2]
                ta = B.op("dve", lambda e, src=src, tb1=tb1, jj=jj: e.scalar_tensor_tensor(
                    out=tb1, in0=src, scalar=rstd, in1=gpost[:, jj * 512:(jj + 1) * 512], op0=ALU.mult, op1=ALU.mult),
                    deps=[tr3, t_gp] + t1_free[t1i % 2])
                if jj == 2:
                    release_bank(b2i, [ta])
                if jj == 3:
                    release_bank(b3i, [ta])
                tb = B.op("dve", lambda e, tb1=tb1, jj=jj, xt=xt: e.tensor_tensor(
                    out=xt[:, jj * 512:(jj + 1) * 512], in0=xt[:, jj * 512:(jj + 1) * 512], in1=tb1, op=ALU.add),
                    deps=[ta, tl])
                t1_free[t1i % 2] = [tb]
                t1i += 1
                xlast = tb
            if q == 0:
                xlast = B.op("dve", lambda e, xt=xt: e.tensor_scalar(out=xt, in0=xt, scalar1=hmask[:, 0:1], scalar2=None,
                                                                     op0=ALU.mult), deps=[xlast])
            if l == 0:
                tst = B.dma("sp", stx[q % 2], x1s[q * 128:(q + 1) * 128, :], xt, deps=[xlast])
            elif q == 0:
                tst = None
            elif q < NQ - 1:
                tst = B.dma("sp", stx[q % 2], yp_o[(q - 1) * 128:q * 128, :], xt, deps=[xlast])
            else:
                tst = B.dma("sp", stx[q % 2], ys_o, xt, deps=[xlast])
            rd = [tst] if tst is not None else [xlast]
            if l == 0:
                t4 = p0_front(xt, xn2[q % 2], junk, q, [xlast], xn2_free[q % 2])
                rd.append(t4)
                if pend_back is not None:
                    pq_, pt4 = pend_back
                    pel, evs = p0_back(xn2[pq_ % 2], pq_, 1, pt4)
                    xn2_free[pq_ % 2] = [pel]
                    hT_toks[pq_] = evs
                pend_back = (q, t4)
            xt_free[q % 2] = rd
        if l == 0 and pend_back is not None:
            pq_, pt4 = pend_back
            pel, evs = p0_back(xn2[pq_ % 2], pq_, 1, pt4)
            hT_toks[pq_] = evs
        for i in range(8):
            release_slab(si + i, f2_last)
        si += 8
        stx_final = [stx[0].v, stx[1].v]
        B.barrier(extra=[(stx[0], stx[0].v), (stx[1], stx[1].v)], pe=False)

    for sem in (stx[0], stx[1], st_p2[0], st_p2[1], st_s2[0], st_s2[1], st_v):
        B.wait("sp", (sem, sem.v))
    return nc


def _slab(w, c0, K):
    return np.ascontiguousarray(w[:, c0:c0 + 128].reshape(K, 128, 128).transpose(1, 0, 2)).reshape(128, K * 128)


def _pack_weights(w_in, w_br_a, w_br_b, w_br_c, w_out, pool_w):
    out = np.empty((L * NSLAB_L, 128, 2048), np.float32)
    i = 0
    for l in range(L):
        wi = w_in[l]
        for c in range(8):
            out[i] = _slab(wi, c * 128, 16); i += 1
            out[i] = _slab(wi, 1024 + c * 128, 16); i += 1
        for c in range(8):
            out[i] = _slab(wi, 2048 + c * 128, 16); i += 1
        for c in range(8):
            out[i] = _slab(wi, 4096 + c * 128, 16); i += 1
        for g in range(8):
            out[i] = _slab(wi, 5120 + g * 128, 16); i += 1
            out[i] = _slab(wi, 3072 + g * 128, 16); i += 1
        for c in CORDER:
            out[i] = _slab(wi, 6144 + c * 128, 16); i += 1
        out[i] = np.ascontiguousarray(pool_w[l].reshape(4, 2, 128, 256).transpose(2, 0, 1, 3)).reshape(128, 2048); i += 1
        for c in range(8):
            out[i] = _slab(wi, 7168 + c * 128, 16); i += 1
        for d in range(16):
            out[i] = _slab(wi, 8192 + d * 128, 16); i += 1
            out[i, :, 0:1024] = _slab(w_br_a[l], d * 128, 8)
            out[i, :, 1024:2048] = _slab(w_br_b[l], d * 128, 8)
            i += 1
            out[i] = _slab(wi, 8192 + 2048 + d * 128, 16); i += 1
            out[i] = _slab(wi, 8192 + 4096 + d * 128, 16); i += 1
            if d % 2 == 0:
                out[i, :, 0:1024] = _slab(w_br_c[l], d * 128, 8)
                out[i, :, 1024:2048] = _slab(w_br_c[l], (d + 1) * 128, 8)
                i += 1
        wo = w_out[l]
        forder = [(q, dg) for q in range(2) for dg in range(4)] + [(q, dg) for dg in range(4) for q in (2, 3)]
        for (q, dg) in forder:
            blk = wo[dg * 512:(dg + 1) * 512, q * 512:(q + 1) * 512].reshape(4, 128, 512).transpose(1, 0, 2)
            out[i] = np.ascontiguousarray(blk).reshape(128, 2048); i += 1
    assert i == L * NSLAB_L
    return out


_PROG = {}


def kernel(x_prompt, x_sample, state_conv, state_pool, g_pre, w_in, conv_w, conv_b, conv_ln_g, conv_ln_b,
           w_br_a, gmlp_ln_g, gmlp_ln_b, gmlp_ws, gmlp_bs, w_br_b, pool_w, pool_scale, w_br_c, w_out, g_post):
    f = lambda a: np.asarray(a, dtype=np.float32)
    x_prompt, x_sample, state_conv, state_pool = f(x_prompt), f(x_sample), f(state_conv), f(state_pool)
    g_pre, w_in, conv_w, conv_b, conv_ln_g, conv_ln_b = f(g_pre), f(w_in), f(conv_w), f(conv_b), f(conv_ln_g), f(conv_ln_b)
    w_br_a, gmlp_ln_g, gmlp_ln_b, gmlp_ws, gmlp_bs, w_br_b = f(w_br_a), f(gmlp_ln_g), f(gmlp_ln_b), f(gmlp_ws), f(gmlp_bs), f(w_br_b)
    pool_w, pool_scale, w_br_c, w_out, g_post = f(pool_w), f(pool_scale), f(w_br_c), f(w_out), f(g_post)

    wst = _pack_weights(w_in, w_br_a, w_br_b, w_br_c, w_out, pool_w)
    par = np.zeros((L, 128, NPAR), np.float32)
    for l in range(L):
        par[l, :, P_GPRE:P_GPRE + 16] = g_pre[l].reshape(16, 128).T
        par[l, :, P_CW:P_CW + 248] = conv_w[l].reshape(31, 8, 128).transpose(2, 1, 0).reshape(128, 248)
        par[l, :, P_CB:P_CB + 8] = conv_b[l].reshape(8, 128).T
        par[l, :, P_LG:P_LG + 8] = conv_ln_g[l].reshape(8, 128).T
        par[l, :, P_LB:P_LB + 8] = conv_ln_b[l].reshape(8, 128).T
        par[l, :, P_PS:P_PS + 8] = pool_scale[l].reshape(8, 128).T
        par[l, :, P_GLG:P_GLG + 8] = gmlp_ln_g[l].reshape(8, 128).T
        par[l, :, P_GLB:P_GLB + 8] = gmlp_ln_b[l].reshape(8, 128).T
    gpost = np.ascontiguousarray(np.broadcast_to(g_post[:, None, :], (L, 128, D)))
    wsT = np.ascontiguousarray(gmlp_ws.transpose(0, 3, 1, 2)).reshape(L, 128, 1024)
    a8 = gmlp_ws[:, :, :8, :8].transpose(0, 3, 1, 2)
    wsS = np.ascontiguousarray(np.broadcast_to(a8[:, None, :, :, None, :], (L, 16, 8, 8, 16, 8))).reshape(L, 128, 1024)
    bsrow = np.zeros((L, 128, 2048), np.float32)
    bsrow[:, :, 0:1024] = gmlp_bs.reshape(L, 1, 1024)
    bsrow[:, :, 1024:2048] = np.tile(gmlp_bs[:, :, :8], (1, 1, 16)).reshape(L, 1, 1024)
    cst = np.zeros((128, 384), np.float32)
    cst[:, 0:128] = np.eye(128, dtype=np.float32)
    s_idx = np.arange(128)
    cst[:, 128:256] = (s_idx[None, :] >= s_idx[:, None]).astype(np.float32)
    cst[:, 256:384] = ((s_idx[:, None] // 8 == s_idx[None, :] // 8) & (s_idx[None, :] % 8 >= s_idx[:, None] % 8)).astype(np.float32)

    in_maps = []
    for c in range(NCORE):
        seq, half = c // 2, c % 2
        xin = np.zeros((NT, D), np.float32)
        if half == 1:
            xin[0:128] = x_prompt[seq, 896:1024]
        xin[128:1152] = x_prompt[seq, half * 1024:(half + 1) * 1024]
        xin[1152:1280] = x_sample[16 * c:16 * c + 16].reshape(128, D)
        icnt = np.zeros((128, 4, 16), np.float32)
        for g in range(4):
            W = 2 ** (g + 1)
            pos = half * 1024 + np.arange(16)
            icnt[:, g, :] = (1.0 / np.minimum(W, pos + 1))[None, :]
        hmask = np.full((128, 1), float(half), np.float32)
        sc = state_conv[:, 16 * c:16 * c + 16]
        sconv = np.ascontiguousarray(sc.reshape(L, 16, 30, 8, 128).transpose(0, 3, 4, 1, 2))
        sp_ = state_pool[:, 16 * c:16 * c + 16]
        spool = np.ascontiguousarray(sp_.reshape(L, 16, 15, 8, 128).transpose(0, 3, 4, 1, 2))
        in_maps.append({"xin": xin, "wst": wst, "par": par, "gpost": gpost, "wsT": wsT, "wsS": wsS, "bsrow": bsrow,
                        "cst": cst, "icnt": icnt.reshape(128, 64), "hmask": hmask, "sconv": sconv, "spool": spool})

    if "nc" not in _PROG:
        _PROG["nc"] = build_program()
    res = run_bass_kernel_spmd(_PROG["nc"], in_maps, core_ids=list(range(NCORE)))
    R = res.results

    y_prompt = np.zeros((4, 2048, D), np.float32)
    y_sample = np.zeros((128, 8, D), np.float32)
    ncp = np.zeros((L, 4, 30, 1024), np.float32)
    npp = np.zeros((L, 4, 15, 1024), np.float32)
    ncs = np.zeros((L, 128, 30, 1024), np.float32)
    nps = np.zeros((L, 128, 15, 1024), np.float32)
    nvs = np.zeros((L, 128, 8, 1024), np.float32)
    for c in range(NCORE):
        seq, half = c // 2, c % 2
        r = R[c]
        y_prompt[seq, half * 1024:(half + 1) * 1024] = r["yp"]
        y_sample[16 * c:16 * c + 16] = r["ys"].reshape(16, 8, D)
        if half == 1:
            ncp[:, seq] = r["convp"].transpose(0, 3, 1, 2).reshape(L, 30, 1024)
            npp[:, seq] = r["poolp"].transpose(0, 3, 1, 2).reshape(L, 15, 1024)
        cs = r["convs"].reshape(L, 8, 128, 16, 38)[..., 8:38]
        ncs[:, 16 * c:16 * c + 16] = cs.transpose(0, 3, 4, 1, 2).reshape(L, 16, 30, 1024)
        ps = r["pools"].reshape(L, 8, 128, 16, 23)[..., 8:23]
        nps[:, 16 * c:16 * c + 16] = ps.transpose(0, 3, 4, 1, 2).reshape(L, 16, 15, 1024)
        nvs[:, 16 * c:16 * c + 16] = r["vs"].reshape(L, 16, 8, 1024)
    return (y_prompt, y_sample, ncp, npp, ncs, nps, nvs)
```

```python
import numpy as np
import concourse.bass as bass
import concourse.mybir as mybir
from concourse.bass_utils import run_bass_kernel_spmd

F32 = mybir.dt.float32
BF16 = mybir.dt.bfloat16
AF = mybir.ActivationFunctionType
ALU = mybir.AluOpType
AX = mybir.AxisListType

NCORE = 8
D = 2048
NT = 1280
NQ = 10
L = 2
EPS = 1e-6
TILES = [(0, 512), (512, 512), (1024, 256)]
TILES2 = [(128, 384), (512, 384), (896, 384)]
NIN = 14336
P_GPRE, P_CW, P_CB, P_LG, P_LB, P_PS, P_GLG, P_GLB, NPAR = 0, 16, 264, 272, 280, 288, 296, 304, 312
NSLAB_L = 153
RING = 8
CORDER = [6, 7, 4, 5, 2, 3, 0, 1]


class Sem:
    def __init__(self, nc, name):
        self.h = nc.alloc_semaphore(name)
        self.name = name
        self.v = 0


class Builder:
    def __init__(self, nc):
        self.nc = nc
        self.E = {"pe": nc.tensor, "act": nc.scalar, "dve": nc.vector, "pool": nc.gpsimd, "sp": nc.sync}
        self.S = {e: Sem(nc, "s_" + e) for e in ("pe", "act", "dve")}
        self.waited = {}
        self.last = {e: None for e in ("pe", "act", "dve")}

    def wait(self, eng, tok):
        if tok is None:
            return
        sem, v = tok
        key = (eng, sem.name)
        if self.waited.get(key, 0) >= v:
            return
        self.E[eng].wait_ge(sem.h, v)
        self.waited[key] = v

    def waits(self, eng, deps):
        for d in deps:
            if isinstance(d, (list, tuple)) and d and isinstance(d[0], (tuple, list)):
                for dd in d:
                    self.wait(eng, dd)
            elif isinstance(d, list):
                for dd in d:
                    self.wait(eng, dd)
            else:
                self.wait(eng, d)

    def op(self, eng, fn, deps=()):
        self.waits(eng, deps)
        ins = fn(self.E[eng])
        sem = self.S[eng]
        sem.v += 1
        ins.then_inc(sem.h, 1)
        tok = (sem, sem.v)
        self.last[eng] = tok
        return tok

    def dma(self, eng, sem, out, in_, deps=()):
        self.waits(eng, deps)
        self.E[eng].dma_start(out=out, in_=in_).then_inc(sem.h, 16)
        sem.v += 16
        return (sem, sem.v)

    def mm(self, out, pairs, deps=()):
        self.waits("pe", deps)
        n = len(pairs)
        ins = None
        for i, (l, r) in enumerate(pairs):
            ins = self.nc.tensor.matmul(out, lhsT=l, rhs=r, start=(i == 0), stop=(i == n - 1))
        sem = self.S["pe"]
        sem.v += 1
        ins.then_inc(sem.h, 1)
        tok = (sem, sem.v)
        self.last["pe"] = tok
        return tok

    def pe_mark(self, ins):
        sem = self.S["pe"]
        sem.v += 1
        ins.then_inc(sem.h, 1)
        tok = (sem, sem.v)
        self.last["pe"] = tok
        return tok

    def barrier(self, extra=(), pe=True):
        toks = [self.last[e] for e in ("pe", "act", "dve")] + list(extra)
        for e in (("pe",) if pe else ()) + ("act", "dve", "sp"):
            self.waits(e, toks)


def build_program():
    nc = bass.Bass("TRN2", target_bir_lowering=False)
    B = Builder(nc)
    dt_in = lambda name, shape: nc.dram_tensor(name, shape, F32, kind="ExternalInput").ap()
    dt_out = lambda name, shape: nc.dram_tensor(name, shape, F32, kind="ExternalOutput").ap()
    xin = dt_in("xin", [NT, D])
    wst = dt_in("wst", [L * NSLAB_L, 128, 2048])
    par_d = dt_in("par", [L, 128, NPAR])
    gpost_d = dt_in("gpost", [L, 128, D])
    wsT_d = dt_in("wsT", [L, 128, 1024])
    wsS_d = dt_in("wsS", [L, 128, 1024])
    bsrow_d = dt_in("bsrow", [L, 128, 2048])
    cst_d = dt_in("cst", [128, 384])
    icnt_d = dt_in("icnt", [128, 64])
    hmask_d = dt_in("hmask", [128, 1])
    sconv_d = dt_in("sconv", [L, 8, 128, 16, 30])
    spool_d = dt_in("spool", [L, 8, 128, 16, 15])
    x1s = nc.dram_tensor("x1s", [NT, D], F32, kind="Internal").ap()
    yp_o = dt_out("yp", [1024, D])
    ys_o = dt_out("ys", [128, D])
    convp_o = dt_out("convp", [L, 8, 128, 30])
    poolp_o = dt_out("poolp", [L, 8, 128, 15])
    convs_o = dt_out("convs", [L, 8, 128, 16 * 38])
    pools_o = dt_out("pools", [L, 8, 128, 16 * 23])
    vs_o = dt_out("vs", [L, 128, 1024])

    RH = nc.alloc_sbuf_tensor("RH", [128, 16, NT], BF16)
    RA = nc.alloc_sbuf_tensor("RA", [128, 24 * NT], BF16)
    RS = nc.alloc_sbuf_tensor("RS", [128, 8 * NT], F32)
    RW = nc.alloc_sbuf_tensor("RW", [128, 7168], F32)
    ring = [nc.alloc_sbuf_tensor(f"ring{i}", [128, 2048], BF16) for i in range(RING)]
    par = nc.alloc_sbuf_tensor("par_sb", [128, L, NPAR], F32)
    cst = nc.alloc_sbuf_tensor("cst_sb", [128, 384], F32)
    onesf = nc.alloc_sbuf_tensor("onesf", [128, 128], F32)
    icnt = nc.alloc_sbuf_tensor("icnt_sb", [128, 4, 16], F32)
    hmask = nc.alloc_sbuf_tensor("hmask_sb", [128, 1], F32)
    sm = nc.alloc_sbuf_tensor("sm", [128, 96], F32)
    ident = cst[:, 0:128]
    maskP = cst[:, 128:256]
    maskS = cst[:, 256:384]
    banks = [nc.alloc_psum_tensor(f"bank{i}", [128, 512], F32) for i in range(8)]

    RA_f = RA.bitcast(F32)
    RS_b = RS.bitcast(BF16)
    RW_b = RW.bitcast(BF16)
    actA = lambda c, a, b: RA[:, c * NT + a: c * NT + b]
    actB = lambda c, a, b: RA[:, (8 + c) * NT + a: (8 + c) * NT + b]
    actC = lambda c, a, b: RA[:, (16 + c) * NT + a: (16 + c) * NT + b]
    acts = [actA, actB, actC]

    ld0 = Sem(nc, "ld0")
    slot_sem = [Sem(nc, f"slot{i}") for i in range(RING)]
    ldx = [Sem(nc, "ldx0"), Sem(nc, "ldx1")]
    stx = [Sem(nc, "stx0"), Sem(nc, "stx1")]
    ld_s2 = [Sem(nc, "ld_s0"), Sem(nc, "ld_s1")]
    st_p2 = [Sem(nc, "st_p0"), Sem(nc, "st_p1")]
    st_s2 = [Sem(nc, "st_s0"), Sem(nc, "st_s1")]
    ld_s, st_p, st_s = ld_s2[0], st_p2[0], st_s2[0]
    ld_m = Sem(nc, "ld_m")
    st_v = Sem(nc, "st_v")

    st = {"bank_i": 0, "slab_issued": 0}
    bank_free = [[] for _ in range(8)]

    bank_held = [False] * 8

    def acquire():
        i = st["bank_i"] % 8
        st["bank_i"] += 1
        assert not bank_held[i], f"PSUM bank {i} re-acquired before its release was recorded"
        bank_held[i] = True
        return i, banks[i], list(bank_free[i])

    def release_bank(i, toks):
        bank_free[i] = list(toks)
        bank_held[i] = False

    slot_free = [[] for _ in range(RING)]
    NSLAB = L * NSLAB_L
    slab_tok = {}

    released = set()

    def issue_slab(i):
        s = i % RING
        assert i < RING or (i - RING) in released, f"slab {i}: slot still held by {i - RING}"
        B.waits("pool", slot_free[s])
        nc.gpsimd.dma_start(out=ring[s][:], in_=wst[i], max_dma_last_dim=2048).then_inc(slot_sem[s].h, 16)
        slot_sem[s].v += 16
        tok = (slot_sem[s], slot_sem[s].v)
        slab_tok[i] = tok

    def get_slab(i):
        while st["slab_issued"] <= i:
            issue_slab(st["slab_issued"])
            st["slab_issued"] += 1
        return ring[i % RING], slab_tok[i]

    def release_slab(i, tok):
        slot_free[i % RING] = [tok]
        released.add(i)
        while st["slab_issued"] < NSLAB and (st["slab_issued"] - RING) in released:
            issue_slab(st["slab_issued"])
            st["slab_issued"] += 1

    B.dma("sp", ld0, par[:], par_d.rearrange("l p n -> p l n"))
    B.dma("sp", ld0, cst[:], cst_d)
    B.dma("sp", ld0, icnt[:], icnt_d.rearrange("p (g j) -> p g j", g=4))
    t_ld0 = B.dma("sp", ld0, hmask[:], hmask_d)
    for i in range(RING):
        get_slab(i)
    for e in ("pe", "act", "dve"):
        B.wait(e, t_ld0)
    t_ones = B.op("dve", lambda e: e.memset(onesf[:], 1.0))
    B.wait("pe", t_ones)

    def p0_tile(xt, xn, junk, q, ln, x_deps, xn_free):
        t4 = p0_front(xt, xn, junk, q, x_deps, xn_free)
        pe_last, evs = p0_back(xn, q, ln, t4)
        return t4, pe_last, evs

    def p0_front(xt, xn, junk, q, x_deps, xn_free):
        col = q
        t1 = B.op("act", lambda e: e.activation(out=junk, in_=xt, func=AF.Square, accum_out=sm[:, col:col + 1]),
                  deps=list(x_deps) + [junk_tok[0]])
        junk_tok[0] = t1
        t2 = B.op("act", lambda e: e.activation(out=sm[:, 10 + col:11 + col], in_=sm[:, col:col + 1], func=AF.Sqrt,
                                                scale=1.0 / D, bias=EPS), deps=[t1])
        t3 = B.op("dve", lambda e: e.reciprocal(out=sm[:, 20 + col:21 + col], in_=sm[:, 10 + col:11 + col]), deps=[t2])
        t4 = B.op("act", lambda e: e.activation(out=xn, in_=xt, func=AF.Copy, scale=sm[:, 20 + col:21 + col]),
                  deps=[t3] + list(xn_free))
        return t4

    def p0_back(xn, q, ln, t4):
        evs = []
        pe_last = None
        for j in range(4):
            bi, bk, bdeps = acquire()
            B.waits("pe", [t4] + bdeps)
            ins = None
            for kk in range(4):
                k = 4 * j + kk
                ins = nc.tensor.transpose(bk[:, kk * 128:(kk + 1) * 128], xn[:, k * 128:(k + 1) * 128], ident)
            tp = B.pe_mark(ins)
            pe_last = tp
            gb = par[:, ln, P_GPRE + 4 * j:P_GPRE + 4 * j + 4].unsqueeze(2).broadcast_to([128, 4, 128])
            te = B.op("dve", lambda e, bk=bk, j=j, gb=gb: e.tensor_tensor(
                out=RH[:, 4 * j:4 * j + 4, q * 128:(q + 1) * 128],
                in0=bk[:].rearrange("p (a b) -> p a b", a=4), in1=gb, op=ALU.mult), deps=[tp])
            release_bank(bi, [te])
            evs.append(te)
        return pe_last, evs

    def proj(slab, slab_t, K, koff, src, consumer, tiles=TILES):
        last = None
        for ti, (c0, w) in enumerate(tiles):
            bi, bk, bdeps = acquire()
            tok = B.mm(bk[:, 0:w], [(slab[:, (koff + k) * 128:(koff + k + 1) * 128], src(k, c0, c0 + w)) for k in range(K)],
                       deps=[slab_t] + bdeps)
            last = tok
            consumer(bi, bk, ti, c0, w, tok)
        return last

    hsrc = lambda k, a, b: RH[:, k, a:b]

    ln_tv = [None]

    def ln_finish(S1, S2, tmp, deps, tiles=TILES):
        return [ln_tile(S1, S2, tmp, deps, c0, w) for (c0, w) in tiles]

    def ln_tile(S1, S2, tmp, deps, c0, w):
        toks = []
        tv = ln_tv[0]
        if True:
            b1i, b1, d1 = acquire()
            tA = B.mm(b1[:, 0:w], [(onesf[:], S1[:, c0:c0 + w])], deps=list(deps) + d1)
            b2i, b2, d2 = acquire()
            tB = B.mm(b2[:, 0:w], [(onesf[:], S2[:, c0:c0 + w])], deps=list(deps) + d2)
            tm = B.op("dve", lambda e: e.tensor_scalar(out=S1[:, c0:c0 + w], in0=b1[:, 0:w], scalar1=1.0 / 1024,
                                                       scalar2=None, op0=ALU.mult), deps=[tA, tB])
            release_bank(b1i, [tm])
            tq = B.op("dve", lambda e: e.tensor_tensor(out=tmp[:, 0:w], in0=S1[:, c0:c0 + w], in1=S1[:, c0:c0 + w],
                                                       op=ALU.mult), deps=[tm, tv])
            tv = B.op("dve", lambda e: e.scalar_tensor_tensor(out=S2[:, c0:c0 + w], in0=b2[:, 0:w], scalar=1.0 / 1024,
                                                              in1=tmp[:, 0:w], op0=ALU.mult, op1=ALU.subtract),
                      deps=[tq])
            release_bank(b2i, [tv])
            ts = B.op("act", lambda e: e.activation(out=S2[:, c0:c0 + w], in_=S2[:, c0:c0 + w], func=AF.Sqrt,
                                                    scale=1.0, bias=EPS), deps=[tv])
            tr = B.op("dve", lambda e: e.reciprocal(out=S2[:, c0:c0 + w], in_=S2[:, c0:c0 + w]), deps=[ts])
            tn = B.op("dve", lambda e: e.scalar_tensor_tensor(out=S1[:, c0:c0 + w], in0=S1[:, c0:c0 + w], scalar=-1.0,
                                                              in1=S2[:, c0:c0 + w], op0=ALU.mult, op1=ALU.mult),
                      deps=[tr])
            ln_tv[0] = tv
        return tn

    out_stores = []
    junk_tok = [None]
    hT_toks = [[] for _ in range(NQ)]

    for l in range(L):
        sb = l * NSLAB_L
        TL = TILES if l == 0 else TILES2
        hs = 0 if l == 0 else 128
        q0 = 0 if l == 0 else 1
        TG = TILES if l == 0 else [(96, 416), (512, 512), (1024, 256)]
        if l == 0:
            xb = [RS[:, 0:2048], RS[:, 2048:4096]]
            xnb = [RS[:, 4096:6144], RS[:, 6144:8192]]
            junk = RS_b[:, 16384:18432]
            xb_free = [[], []]
            xn_free = [[], []]
            for q in range(NQ):
                tl = B.dma("sp", ldx[q % 2], xb[q % 2], xin[q * 128:(q + 1) * 128, :], deps=xb_free[q % 2])
                t4, pel, evs = p0_tile(xb[q % 2], xnb[q % 2], junk, q, 0, [tl], xn_free[q % 2])
                xb_free[q % 2] = [t4]
                xn_free[q % 2] = [pel]
                hT_toks[q] = evs

        yconv = lambda c, a, b: RS[:, c * NT + a: c * NT + b]
        S1 = RA_f[:, 10240:11520]
        S2 = RA_f[:, 11520:12800]
        sq = RA_f[:, 12800:14080]
        dgb = [RA[:, 10240 + i * 3968:10240 + (i + 1) * 3968].rearrange("p (k m) -> p k m", m=128) for i in range(2)]
        zbP = [RA[:, 18176:19360], RA[:, 28160:29344]]
        zbSf = [RA[:, 19360:19968], RA[:, 29344:29952]]
        zbS = [z.rearrange("p (b j) -> p b j", j=38) for z in zbSf]
        zSf = [RW[:, 0:608], RW[:, 608:1216]]
        zS = [z.rearrange("p (b j) -> p b j", j=38) for z in zSf]
        zP128 = [RW[:, 1216:1344], RW[:, 1344:1472]]
        sg = [RW[:, 1536:2048], RW[:, 2048:2560]]
        tmpA = RW[:, 2560:3072]
        pcv = [RW[:, 3072:4352], RW[:, 4352:5632]]
        NDTC = [8] * 7 + [2]
        pcv_rd = [None, None]
        t_z0 = B.op("dve", lambda e: e.memset(zbP[0][:, 0:30], 0.0))
        t_z1 = B.op("dve", lambda e: e.memset(zbP[1][:, 0:30], 0.0))
        sg_free = [[], []]
        sgi = 0
        si = sb
        conv_pe = [None] * 8
        zst = [None] * 8
        st_s_hist = [None] * 8
        st_p_hist = [None] * 8
        CG = [(0, 384), (384, 384), (768, 384)] if l == 0 else [(128, 512), (640, 512)]
        NCG = len(CG)

        def build_diag(c):
            dg = dgb[c % 2]
            deps = [conv_pe[c - 2]] if c >= 2 else []
            t = None
            for k in range(NDTC[c], 31):
                t = B.op("act", lambda e, k=k: e.activation(out=dg[:, k, :], in_=ident, func=AF.Copy,
                                                            scale=par[:, l, P_CW + c * 31 + k:P_CW + c * 31 + k + 1]),
                         deps=deps)
            return t

        def proj_glu(c):
            nonlocal sgi, si
            p = c % 2
            av, av_t = get_slab(si)
            ag, ag_t = get_slab(si + 1)
            old = [conv_pe[c - 2], st_s_hist[c - 2], st_p_hist[c - 2]] if c >= 2 else [t_z0, t_z1]
            tpre = B.dma("sp", ld_s2[p], zS[p][:, :, 0:30], sconv_d[l, c], deps=old)
            tcast = B.op("act", lambda e: e.activation(out=zbS[p][:, :, 0:30], in_=zS[p][:, :, 0:30], func=AF.Copy),
                         deps=[tpre] + old)
            z_toks = [tcast]
            f32_toks = []
            last_pe = None
            for ti, (c0, w) in enumerate(TG):
                bai, ba, da = acquire()
                hdeps = [t for qq in range(c0 // 128, (c0 + w) // 128) for t in hT_toks[qq]] if c == 0 else []
                ta = B.mm(ba[:, 0:w], [(av[:, k * 128:(k + 1) * 128], RH[:, k, c0:c0 + w]) for k in range(16)],
                          deps=[av_t] + da + hdeps)
                bgi, bg, dg_ = acquire()
                tg = B.mm(bg[:, 0:w], [(ag[:, k * 128:(k + 1) * 128], RH[:, k, c0:c0 + w]) for k in range(16)],
                          deps=[ag_t] + dg_)
                last_pe = tg
                sgb = sg[sgi % 2]
                tsg = B.op("act", lambda e: e.activation(out=sgb[:, 0:w], in_=bg[:, 0:w], func=AF.Sigmoid),
                           deps=[tg] + sg_free[sgi % 2])
                release_bank(bgi, [tsg])
                if ti < 2:
                    tz = B.op("dve", lambda e: e.tensor_tensor(out=zbP[p][:, 30 + c0:30 + c0 + w], in0=ba[:, 0:w],
                                                               in1=sgb[:, 0:w], op=ALU.mult), deps=[ta, tsg] + old)
                    z_toks.append(tz)
                else:
                    tz1 = B.op("dve", lambda e: e.tensor_tensor(out=zbP[p][:, 30 + 1024:30 + 1152], in0=ba[:, 0:128],
                                                                in1=sgb[:, 0:128], op=ALU.mult), deps=[ta, tsg] + old)
                    tz2 = B.op("dve", lambda e: e.tensor_tensor(out=zP128[p], in0=ba[:, 0:128], in1=sgb[:, 0:128],
                                                                op=ALU.mult), deps=[ta, tsg] + old)
                    tz3 = B.op("dve", lambda e: e.tensor_tensor(
                        out=zbS[p][:, :, 30:38], in0=ba[:, 128:256].rearrange("p (b j) -> p b j", j=8),
                        in1=sgb[:, 128:256].rearrange("p (b j) -> p b j", j=8), op=ALU.mult), deps=[ta, tsg] + old)
                    tz = B.op("dve", lambda e: e.tensor_tensor(
                        out=zS[p][:, :, 30:38], in0=ba[:, 128:256].rearrange("p (b j) -> p b j", j=8),
                        in1=sgb[:, 128:256].rearrange("p (b j) -> p b j", j=8), op=ALU.mult), deps=[ta, tsg] + old)
                    z_toks += [tz1, tz3]
                    f32_toks = [tz2, tz]
                release_bank(bai, [tz])
                sg_free[sgi % 2] = [tz]
                sgi += 1
            release_slab(si, last_pe)
            release_slab(si + 1, last_pe)
            si += 2
            st_p_hist[c] = B.dma("sp", st_p2[p], convp_o[l, c], zP128[p][:, 98:128], deps=f32_toks)
            st_s_hist[c] = B.dma("sp", st_s2[p], convs_o[l, c], zSf[p], deps=f32_toks + [tpre])
            out_stores.extend([st_p_hist[c], st_s_hist[c]])
            zst[c] = z_toks

        def conv(c, t_dg, s_toks):
            p = c % 2
            dg = dgb[p]
            NDT = NDTC[c]
            pc = pcv[p]
            cwk = lambda k: par[:, l, P_CW + c * 31 + k:P_CW + c * 31 + k + 1]
            cbk = par[:, l, P_CB + c:P_CB + c + 1]
            pcP = pc[:, hs:1152]
            pcS = pc[:, 1152:1280].rearrange("p (b j) -> p b j", j=8)
            tP = B.op("dve", lambda e: e.tensor_scalar(out=pcP, in0=zbP[p][:, hs:1152], scalar1=cwk(0), scalar2=cbk,
                                                       op0=ALU.mult, op1=ALU.add), deps=zst[c] + [pcv_rd[p]])
            tS = B.op("dve", lambda e: e.tensor_scalar(out=pcS, in0=zbS[p][:, :, 0:8], scalar1=cwk(0), scalar2=cbk,
                                                       op0=ALU.mult, op1=ALU.add), deps=zst[c] + [pcv_rd[p]])
            for k in range(1, NDT):
                tP = B.op("dve", lambda e: e.scalar_tensor_tensor(out=pcP, in0=zbP[p][:, hs + k:1152 + k], scalar=cwk(k),
                                                                  in1=pcP, op0=ALU.mult, op1=ALU.add), deps=[tP])
                tS = B.op("dve", lambda e: e.scalar_tensor_tensor(out=pcS, in0=zbS[p][:, :, k:k + 8], scalar=cwk(k),
                                                                  in1=pcS, op0=ALU.mult, op1=ALU.add), deps=[tS])
            bks = [acquire() for _ in range(NCG + 1)]
            B.waits("pe", zst[c] + [t_dg] + [d for (_, _, dd) in bks for d in dd])
            ins = None
            for k in range(NDT, 31):
                for gi_, (o, n) in enumerate(CG):
                    ins = nc.tensor.matmul(bks[gi_][1][:, 0:n], lhsT=dg[:, k, :], rhs=zbP[p][:, o + k:o + k + n],
                                           start=(k == NDT), stop=(k == 30))
                ins = nc.tensor.matmul(bks[NCG][1][:, 0:128].rearrange("p (b j) -> p b j", j=8), lhsT=dg[:, k, :],
                                       rhs=zbS[p][:, :, k:k + 8], start=(k == NDT), stop=(k == 30))
            tp = B.pe_mark(ins)
            conv_pe[c] = tp
            cb = par[:, l, P_CB + c:P_CB + c + 1]
            ev = []
            for gi_, (o, n) in enumerate(CG):
                te = B.op("dve", lambda e: e.tensor_tensor(out=yconv(c, o, o + n), in0=bks[gi_][1][:, 0:n],
                                                           in1=pc[:, o:o + n], op=ALU.add), deps=[tp, tP])
                release_bank(bks[gi_][0], [te])
                ev.append(te)
            te = B.op("dve", lambda e: e.tensor_tensor(out=yconv(c, 1152, 1280), in0=bks[NCG][1][:, 0:128],
                                                       in1=pc[:, 1152:1280], op=ALU.add), deps=[tp, tS])
            release_bank(bks[NCG][0], [te])
            ev.append(te)
            pcv_rd[p] = te
            yc = yconv(c, hs, NT)
            s1v, s2v, sqv = S1[:, hs:NT], S2[:, hs:NT], sq[:, hs:NT]
            tsq = B.op("act", lambda e: e.activation(out=sqv, in_=yc, func=AF.Square), deps=ev + s_toks)
            if c == 0:
                ts1 = B.op("dve", lambda e: e.tensor_copy(out=s1v, in_=yc), deps=ev)
                ts2 = B.op("dve", lambda e: e.tensor_copy(out=s2v, in_=sqv), deps=[tsq])
            else:
                ts1 = B.op("dve", lambda e: e.tensor_tensor(out=s1v, in0=s1v, in1=yc, op=ALU.add), deps=ev + s_toks)
                ts2 = B.op("dve", lambda e: e.tensor_tensor(out=s2v, in0=s2v, in1=sqv, op=ALU.add), deps=[tsq, ts1])
            return [ts1, ts2]

        t_dgs = [None] * 8
        t_dgs[0] = build_diag(0)
        proj_glu(0)
        s_toks = []
        for c in range(8):
            if c + 1 < 8:
                t_dgs[c + 1] = build_diag(c + 1)
                proj_glu(c + 1)
            s_toks = conv(c, t_dgs[c], s_toks)
        st_p_tok = [st_p_hist[6], st_p_hist[7]]
        st_s_tok = [st_s_hist[6], st_s_hist[7]]
        ln_toks = ln_finish(S1, S2, tmpA, s_toks, TL)
        for c in range(8):
            yc = yconv(c, hs, NT)
            t1 = B.op("dve", lambda e: e.tensor_tensor(out=yc, in0=yc, in1=S2[:, hs:NT], op=ALU.mult), deps=ln_toks)
            t2 = B.op("dve", lambda e: e.tensor_tensor(out=yc, in0=yc, in1=S1[:, hs:NT], op=ALU.add), deps=[t1])
            t3n = B.op("act", lambda e: e.activation(out=yc, in_=yc, func=AF.Silu,
                                                     bias=par[:, l, P_LB + c:P_LB + c + 1],
                                                     scale=par[:, l, P_LG + c:P_LG + c + 1]), deps=[t2])
            sl, sl_t = get_slab(si)

            def consA(bi, bk, ti, c0, w, tok, c=c, t3n=t3n):
                nonlocal sgi
                sgb = sg[sgi % 2]
                t1 = B.op("act", lambda e: e.activation(out=sgb[:, 0:w], in_=bk[:, 0:w], func=AF.Silu),
                          deps=[tok] + sg_free[sgi % 2])
                release_bank(bi, [t1])
                t2 = B.op("dve", lambda e: e.tensor_tensor(out=actA(c, c0, c0 + w), in0=sgb[:, 0:w],
                                                           in1=yconv(c, c0, c0 + w), op=ALU.mult),
                          deps=[t1, t3n])
                sg_free[sgi % 2] = [t2]
                sgi += 1
            lp = proj(sl, sl_t, 16, 0, hsrc, consA, TL)
            release_slab(si, lp)
            si += 1
        B.barrier(extra=[st_p_tok, st_s_tok], pe=False)

        vT = lambda c, a, b: RS[:, c * NT + a: c * NT + b]
        vbf = RA[:, 16 * NT:24 * NT].rearrange("p (q f) -> p q f", f=1024)
        S1 = RA_f[:, 5120:6400]
        S2 = RA_f[:, 6400:7680]
        sq = RA_f[:, 7680:8960]
        bsb = RW[:, 0:2048]
        tmpB = RW[:, 2048:2560]
        wsTm = RW_b[:, 5120:6144].rearrange("p (g t) -> p g t", g=8)
        wsSm = RW_b[:, 6144:7168].rearrange("p (g t) -> p g t", g=8)
        stg = RW[:, 3584:4608]
        vn32 = RW[:, 3584:4608]
        t_bs = B.dma("sp", ld_m, bsb, bsrow_d[l])
        t_w1 = B.dma("sp", ld_m, stg, wsT_d[l])
        t_m1 = B.op("dve", lambda e: e.tensor_tensor(out=wsTm, in0=stg.rearrange("p (g t) -> p g t", g=8),
                                                     in1=maskP.unsqueeze(1).broadcast_to([128, 8, 128]), op=ALU.mult),
                    deps=[t_w1])
        t_w2 = B.dma("sp", ld_m, stg, wsS_d[l], deps=[t_m1])
        t_m2 = B.op("dve", lambda e: e.tensor_tensor(out=wsSm, in0=stg.rearrange("p (g t) -> p g t", g=8),
                                                     in1=maskS.unsqueeze(1).broadcast_to([128, 8, 128]), op=ALU.mult),
                    deps=[t_w2])
        s_toks = []
        for c in range(8):
            sl, sl_t = get_slab(si)
            ev = []

            def consV(bi, bk, ti, c0, w, tok, c=c, ev=ev):
                t1 = B.op("act", lambda e: e.activation(out=vT(c, c0, c0 + w), in_=bk[:, 0:w], func=AF.Copy), deps=[tok])
                release_bank(bi, [t1])
                ev.append(t1)
            lp = proj(sl, sl_t, 16, 0, hsrc, consV, TL)
            release_slab(si, lp)
            si += 1
            vc = vT(c, hs, NT)
            s1v, s2v, sqv = S1[:, hs:NT], S2[:, hs:NT], sq[:, hs:NT]
            tsq = B.op("act", lambda e: e.activation(out=sqv, in_=vc, func=AF.Square), deps=ev + s_toks)
            if c == 0:
                ts1 = B.op("dve", lambda e: e.tensor_copy(out=s1v, in_=vc), deps=ev)
                ts2 = B.op("dve", lambda e: e.tensor_copy(out=s2v, in_=sqv), deps=[tsq])
            else:
                ts1 = B.op("dve", lambda e: e.tensor_tensor(out=s1v, in0=s1v, in1=vc, op=ALU.add), deps=ev + s_toks)
                ts2 = B.op("dve", lambda e: e.tensor_tensor(out=s2v, in0=s2v, in1=sqv, op=ALU.add), deps=[tsq, ts1])
            s_toks = [ts1, ts2]
        ln_toks = [None, None, None]
        vb_toks = []
        tvs_box = [None]
        t3f = [RW[:, 4608:5888], RW[:, 5888:7168]]
        t3_rd = [[], []]
        proj_state = {}
        si_b = si

        def b_norm(ti):
            c0, w = TL[ti]
            n_toks = []
            for c in range(8):
                vc = vT(c, c0, c0 + w)
                t1 = B.op("dve", lambda e: e.tensor_tensor(out=vc, in0=vc, in1=S2[:, c0:c0 + w], op=ALU.mult),
                          deps=[ln_toks[ti]])
                t2 = B.op("dve", lambda e: e.tensor_tensor(out=vc, in0=vc, in1=S1[:, c0:c0 + w], op=ALU.add), deps=[t1])
                t3 = B.op("act", lambda e: e.activation(out=vc, in_=vc, func=AF.Identity,
                                                        bias=par[:, l, P_GLB + c:P_GLB + c + 1],
                                                        scale=par[:, l, P_GLG + c:P_GLG + c + 1]), deps=[t2])
                n_toks.append(t3)
            return n_toks

        def b_transposes(ti, n_toks):
            c0, w = TL[ti]
            for q in range(c0 // 128, (c0 + w) // 128):
                evq = []
                for hh in range(2):
                    bi, bk, bd = acquire()
                    B.waits("pe", n_toks + bd)
                    ins = None
                    for cc in range(4):
                        c = 4 * hh + cc
                        ins = nc.tensor.transpose(bk[:, cc * 128:(cc + 1) * 128], vT(c, q * 128, (q + 1) * 128), ident)
                    tp = B.pe_mark(ins)
                    te = B.op("act", lambda e: e.activation(out=vbf[:, q, hh * 512:(hh + 1) * 512], in_=bk[:],
                                                            func=AF.Copy), deps=[tp])
                    rel = [te]
                    if q == NQ - 1:
                        t32 = B.op("act", lambda e: e.activation(out=vn32[:, hh * 512:(hh + 1) * 512], in_=bk[:],
                                                                 func=AF.Copy), deps=[tp, te])
                        rel.append(t32)
                        evq.append(t32)
                    release_bank(bi, rel)
                    vb_toks.append(te)
                if q == NQ - 1:
                    tvs_box[0] = B.dma("sp", st_v, vs_o[l], vn32, deps=evq)
                    out_stores.append(tvs_box[0])

        def b_proj_pe(g):
            bs_i, bu_i = si_b + 2 * g, si_b + 2 * g + 1
            buf = t3f[g % 2]
            bs_, bs_t = get_slab(bs_i)
            sil = []
            lp = None
            for ti, (c0, w) in enumerate(TL):
                psi, ps, dps = acquire()
                tsl = B.mm(ps[:, 0:w], [(bs_[:, k * 128:(k + 1) * 128], RH[:, k, c0:c0 + w]) for k in range(16)],
                           deps=[bs_t] + dps)
                lp = tsl
                t1 = B.op("act", lambda e: e.activation(out=buf[:, c0:c0 + w], in_=ps[:, 0:w], func=AF.Silu),
                          deps=[tsl] + t3_rd[g % 2])
                release_bank(psi, [t1])
                sil.append(t1)
            release_slab(bs_i, lp)
            bu, bu_t = get_slab(bu_i)
            pus = []
            for ti, (c0, w) in enumerate(TL):
                pui, pu, dpu = acquire()
                tu = B.mm(pu[:, 0:w], [(bu[:, k * 128:(k + 1) * 128], RH[:, k, c0:c0 + w]) for k in range(16)],
                          deps=[bu_t] + dpu)
                lp = tu
                pus.append((pui, pu, tu))
            release_slab(bu_i, lp)
            proj_state[g] = (sil, pus)

        def b_proj_dve(g):
            sil, pus = proj_state[g]
            buf = t3f[g % 2]
            toks = []
            for ti, (c0, w) in enumerate(TL):
                pui, pu, tu = pus[ti]
                t2 = B.op("dve", lambda e: e.tensor_tensor(out=buf[:, c0:c0 + w], in0=pu[:, 0:w], in1=buf[:, c0:c0 + w],
                                                           op=ALU.mult), deps=[tu, sil[ti]])
                release_bank(pui, [t2])
                toks.append(t2)
            proj_state[g] = toks

        tmpm_rd = [None]

        def b_mix(g):
            toks = proj_state[g]
            buf = t3f[g % 2]
            rd = []
            for ti, (c0, w) in enumerate(TL):
                pmi, pm, dpm = acquire()
                B.waits("pe", vb_toks + dpm + [t_m1, t_m2, t_w2])
                ins = None
                for j in range(w // 128):
                    q = c0 // 128 + j
                    wm = wsSm if q == NQ - 1 else wsTm
                    so = 1 if q == NQ - 1 else 0
                    ins = nc.tensor.matmul(pm[:, j * 128:(j + 1) * 128], lhsT=vbf[:, q, g * 128:(g + 1) * 128],
                                           rhs=wm[:, g, :], start=True, stop=True)
                tm = B.pe_mark(ins)
                npr = min(w, 1152 - c0) // 128
                tmpm = tmpB
                ta_ = B.op("dve", lambda e: e.tensor_tensor(
                    out=tmpm[:, 0:npr * 128].rearrange("p (j t) -> p j t", t=128),
                    in0=pm[:, 0:npr * 128].rearrange("p (j t) -> p j t", t=128),
                    in1=bsb[:, g * 128:(g + 1) * 128].unsqueeze(1).broadcast_to([128, npr, 128]), op=ALU.add),
                    deps=[tm, t_w2, tmpm_rd[0]])
                if npr * 128 < w:
                    ta_ = B.op("dve", lambda e: e.tensor_tensor(out=tmpm[:, npr * 128:w], in0=pm[:, npr * 128:w],
                                                                in1=bsb[:, 1024 + g * 128:1024 + (g + 1) * 128],
                                                                op=ALU.add), deps=[tm, t_w2])
                release_bank(pmi, [ta_])
                t3 = B.op("dve", lambda e: e.tensor_tensor(out=actB(g, c0, c0 + w), in0=tmpm[:, 0:w], in1=buf[:, c0:c0 + w],
                                                           op=ALU.mult), deps=[ta_, toks[ti]])
                tmpm_rd[0] = t3
                rd.append(t3)
            t3_rd[g % 2] = rd

        ln_toks[0] = ln_tile(S1, S2, tmpB, s_toks, *TL[0])
        nt0 = b_norm(0)
        b_proj_pe(0)
        b_proj_dve(0)
        b_transposes(0, nt0)
        ln_toks[1] = ln_tile(S1, S2, tmpB, s_toks, *TL[1])
        ln_toks[2] = ln_tile(S1, S2, tmpB, s_toks, *TL[2])
        nt1 = b_norm(1)
        b_proj_pe(1)
        b_proj_dve(1)
        b_transposes(1, nt1)
        nt2 = b_norm(2)
        b_transposes(2, nt2)
        for g in range(8):
            b_mix(g)
            if g + 2 < 8:
                b_proj_pe(g + 2)
                b_proj_dve(g + 2)
        si = si_b + 16
        tvs = tvs_box[0]
        tokB_end = B.last["dve"]
        B.barrier(extra=[tvs], pe=False)

        dT = lambda c, a, b: RS_b[:, c * NT + a: c * NT + b]
        cP = RW[:, 0:1168]
        w1 = RW[:, 1168:2336]
        w2 = RW[:, 2336:3504]
        cSf = RW[:, 3504:3872]
        cS = cSf.rearrange("p (b j) -> p b j", j=23)
        u1 = RW[:, 3872:4240].rearrange("p (b j) -> p b j", j=23)
        u2 = RW[:, 4240:4608].rearrange("p (b j) -> p b j", j=23)
        t16 = RW[:, 4608:4624]
        slb = [RW[:, 4624:5136], RW[:, 5136:5648]]
        t_c0 = B.op("dve", lambda e: e.memset(cP[:, 0:16], 0.0))
        t_c1 = B.op("act", lambda e: e.memset(w1[:, 0:16], 0.0)) if False else None
        st_p_tok = None
        st_s_tok = None
        c_last = None
        for c in CORDER:
            sl, sl_t = get_slab(si)
            tpre = B.dma("sp", ld_s, cS[:, :, 0:15], spool_d[l, c], deps=[c_last, st_s_tok])
            wdeps = [c_last, st_p_tok, st_s_tok, t_c0]
            ev = []

            def consC(bi, bk, ti, c0, w, tok, ev=ev, wdeps=wdeps):
                if ti < 2:
                    t1 = B.op("act", lambda e: e.activation(out=cP[:, 16 + c0:16 + c0 + w], in_=bk[:, 0:w], func=AF.Copy),
                              deps=[tok] + wdeps)
                else:
                    t0 = B.op("act", lambda e: e.activation(out=cP[:, 16 + 1024:16 + 1152], in_=bk[:, 0:128], func=AF.Copy),
                              deps=[tok] + wdeps)
                    ev.append(t0)
                    t1 = B.op("act", lambda e: e.activation(out=cS[:, :, 15:23],
                                                            in_=bk[:, 128:256].rearrange("p (b j) -> p b j", j=8),
                                                            func=AF.Copy), deps=[tok] + wdeps)
                release_bank(bi, [t1])
                ev.append(t1)
            lp = proj(sl, sl_t, 16, 0, hsrc, consC, TG)
            release_slab(si, lp)
            si += 1
            st_p_tok = B.dma("sp", st_p, poolp_o[l, c], cP[:, 16 + 128 + 1009:16 + 128 + 1024], deps=ev)
            st_s_tok = B.dma("sp", st_s, pools_o[l, c], cSf, deps=ev + [tpre])
            out_stores += [st_p_tok, st_s_tok]
            g = c // 2
            W = 2 ** (g + 1)
            srcP, srcS = cP, cS
            bufsP, bufsS = [w1, w2], [u1, u2]
            tP = None
            tS = None
            for lev in range(1, g + 2):
                sh = 2 ** (lev - 1)
                v0 = 2 ** lev - 1
                dP = bufsP[(lev - 1) % 2]
                dS = bufsS[(lev - 1) % 2]
                tP = B.op("dve", lambda e, dP=dP, srcP=srcP, v0=v0, sh=sh: e.tensor_tensor(
                    out=dP[:, v0:1168], in0=srcP[:, v0:1168], in1=srcP[:, v0 - sh:1168 - sh], op=ALU.add),
                    deps=ev + [tP, c_last])
                tS = B.op("dve", lambda e, dS=dS, srcS=srcS, v0=v0, sh=sh: e.tensor_tensor(
                    out=dS[:, :, v0:23], in0=srcS[:, :, v0:23], in1=srcS[:, :, v0 - sh:23 - sh], op=ALU.add),
                    deps=ev + [tS, tpre, c_last])
                srcP, srcS = dP, dS
            td1 = B.op("dve", lambda e: e.scalar_tensor_tensor(out=dT(c, 0, 1152), in0=srcP[:, 16:1168], scalar=1.0 / W,
                                                               in1=cP[:, 16:1168], op0=ALU.mult, op1=ALU.subtract),
                       deps=[tP])
            td2 = B.op("dve", lambda e: e.tensor_tensor(out=t16, in0=srcP[:, 144:160], in1=icnt[:, g, :], op=ALU.mult),
                       deps=[tP, c_last])
            td3 = B.op("dve", lambda e: e.tensor_tensor(out=dT(c, 128, 144), in0=t16, in1=cP[:, 144:160],
                                                        op=ALU.subtract), deps=[td2, td1])
            td4 = B.op("dve", lambda e: e.scalar_tensor_tensor(
                out=dT(c, 1152, 1280).rearrange("p (b j) -> p b j", j=8), in0=srcS[:, :, 15:23], scalar=1.0 / W,
                in1=cS[:, :, 15:23], op0=ALU.mult, op1=ALU.subtract), deps=[tS])
            c_last = td4
            d_last = [td3, td4]
        pw, pw_t = get_slab(si)
        pw_i = si
        si += 1
        pwbuf = RW_b[:, 11296:13344]
        pw_t = B.op("act", lambda e: e.activation(out=pwbuf, in_=pw[:], func=AF.Copy), deps=[pw_t])
        release_slab(pw_i, pw_t)
        pwv = pwbuf.rearrange("p (g k n) -> p g k n", g=4, k=2)
        slf = [RS[:, 5120:6400], RS[:, 6400:7680]]
        slf_rd = [[], []]
        c_sil = {}
        si_c = si

        def c_proj(dc):
            buf = slf[dc % 2]
            sl, sl_t = get_slab(si_c + dc)
            toks = []

            def cons(bi, bk, ti, c0, w, tok):
                t1 = B.op("act", lambda e: e.activation(out=buf[:, c0:c0 + w], in_=bk[:, 0:w], func=AF.Silu),
                          deps=[tok] + slf_rd[dc % 2])
                release_bank(bi, [t1])
                toks.append(t1)
            lp = proj(sl, sl_t, 16, 0, hsrc, cons, TL)
            release_slab(si_c + dc, lp)
            c_sil[dc] = toks

        def c_pool(dc):
            g = dc // 2
            buf = slf[dc % 2]
            rd = []
            for ti, (c0, w) in enumerate(TL):
                ppi, pp, dpp = acquire()
                tpp = B.mm(pp[:, 0:w], [(pwv[:, g, kc, (dc % 2) * 128:(dc % 2 + 1) * 128], dT(2 * g + kc, c0, c0 + w))
                                        for kc in range(2)], deps=[pw_t] + dpp + d_last)
                t2 = B.op("dve", lambda e: e.scalar_tensor_tensor(out=actC(dc, c0, c0 + w), in0=pp[:, 0:w],
                                                                  scalar=par[:, l, P_PS + dc:P_PS + dc + 1],
                                                                  in1=buf[:, c0:c0 + w], op0=ALU.mult, op1=ALU.mult),
                          deps=[tpp, c_sil[dc][ti]])
                release_bank(ppi, [t2])
                rd.append(t2)
            slf_rd[dc % 2] = rd

        c_proj(0)
        c_proj(1)
        for dc in range(8):
            c_pool(dc)
            if dc + 2 < 8:
                c_proj(dc + 2)
        si = si_c + 8
        tokC_end = B.last["dve"]
        B.barrier(extra=[st_p_tok, st_s_tok], pe=False)

        mT = lambda d, a, b: RS_b[:, d * NT + a: d * NT + b]
        gbuf = [RW[:, 0:1280], RW[:, 1280:2560]]
        tacc = [RW[:, 2560:3840], RW[:, 3840:5120]]
        tmpb = [RW[:, 5120:5632], RW[:, 5632:6144]]
        gbuf_free = [[], []]
        gstep = 0
        tmi = 0
        tmpb_rd = [None, None]
        tacc_rd = [[None] * 3, [None] * 3]
        scc = None
        for d in range(16):
            i_ga, i_ab, i_gb, i_gc = si, si + 1, si + 2, si + 3
            nsl = 4
            if d % 2 == 0:
                scc_i = si + 4
                nsl = 5
            gate_idx = [i_ga, i_gb, i_gc]
            ta_ = tacc[d % 2]
            ta_tok = [None, None, None]
            last_pe = None
            for i in range(3):
                gsl, gsl_t = get_slab(gate_idx[i])
                gb_ = gbuf[gstep % 2]
                sig = []
                lp = None
                for ti, (c0, w) in enumerate(TL):
                    bgi, bg, dbg = acquire()
                    tg = B.mm(bg[:, 0:w], [(gsl[:, k * 128:(k + 1) * 128], RH[:, k, c0:c0 + w]) for k in range(16)],
                              deps=[gsl_t] + dbg)
                    lp = tg
                    t1 = B.op("act", lambda e: e.activation(out=gb_[:, c0:c0 + w], in_=bg[:, 0:w], func=AF.Sigmoid),
                              deps=[tg] + gbuf_free[gstep % 2])
                    release_bank(bgi, [t1])
                    sig.append(t1)
                release_slab(gate_idx[i], lp)
                if i < 2:
                    wsl, wsl_t = get_slab(i_ab)
                    ko = 8 * i
                else:
                    wsl, wsl_t = get_slab(scc_i)
                    ko = 8 * (d % 2)
                rd = []
                for ti, (c0, w) in enumerate(TL):
                    byi, by, dby = acquire()
                    ty = B.mm(by[:, 0:w], [(wsl[:, (ko + k) * 128:(ko + k + 1) * 128], acts[i](k, c0, c0 + w))
                                           for k in range(8)],
                              deps=[wsl_t] + dby + ([tokC_end if i == 2 else tokB_end] if d == 0 else []))
                    last_pe = ty
                    if i == 0:
                        t2 = B.op("dve", lambda e: e.tensor_tensor(out=ta_[:, c0:c0 + w], in0=by[:, 0:w],
                                                                   in1=gb_[:, c0:c0 + w], op=ALU.mult),
                                  deps=[ty, sig[ti], tacc_rd[d % 2][ti]])
                        release_bank(byi, [t2])
                        rd.append(t2)
                        ta_tok[ti] = t2
                    else:
                        tm_ = tmpb[tmi % 2]
                        tmk = tmi % 2
                        tmi += 1
                        t2 = B.op("dve", lambda e: e.tensor_tensor(out=tm_[:, 0:w], in0=by[:, 0:w],
                                                                   in1=gb_[:, c0:c0 + w], op=ALU.mult),
                                  deps=[ty, sig[ti], tmpb_rd[tmk]])
                        release_bank(byi, [t2])
                        rd.append(t2)
                        if i == 1:
                            ta_tok[ti] = B.op("dve", lambda e: e.tensor_tensor(out=ta_[:, c0:c0 + w], in0=ta_[:, c0:c0 + w],
                                                                               in1=tm_[:, 0:w], op=ALU.add),
                                              deps=[t2, ta_tok[ti]])
                            tmpb_rd[tmk] = ta_tok[ti]
                        else:
                            tfin = B.op("dve", lambda e: e.tensor_tensor(out=mT(d, c0, c0 + w), in0=ta_[:, c0:c0 + w],
                                                                         in1=tm_[:, 0:w], op=ALU.add),
                                        deps=[t2, ta_tok[ti]])
                            tmpb_rd[tmk] = tfin
                            tacc_rd[d % 2][ti] = tfin
                gbuf_free[gstep % 2] = rd
                gstep += 1
                if i == 1:
                    release_slab(i_ab, last_pe)
                if i == 2 and d % 2 == 1:
                    release_slab(scc_i, last_pe)
            si += nsl
        tokG_end = B.last["dve"]
        B.barrier(pe=False)

        y0 = lambda q, a, b: RA_f[:, q * 1024 + a: q * 1024 + b]
        gpost = RA_f[:, 10240:12288]
        xn = RA_f[:, 12288:14336]
        junk = RA[:, 28672:30720]
        junk5 = RA[:, 28672:29184]
        xt_ = [RW[:, 0:2048], RW[:, 2048:4096]]
        t1_ = [RW[:, 4096:4608], RW[:, 4608:5120]]
        xsrc = xin if l == 0 else x1s
        t_gp = B.dma("sp", ld_m, gpost, gpost_d[l])
        for pq in range(2):
            fsl = [get_slab(si + i) for i in range(4)]
            tp = None
            for q in range(q0, NQ):
                bi, bk, dd = acquire()
                B.waits("pe", dd + [tokG_end])
                ins = None
                for d in range(16):
                    if d % 4 == 0:
                        B.wait("pe", fsl[d // 4][1])
                    ins = nc.tensor.matmul(bk[:], lhsT=mT(d, q * 128, (q + 1) * 128),
                                           rhs=fsl[d // 4][0][:, (d % 4) * 512:(d % 4 + 1) * 512],
                                           start=(d == 0), stop=(d == 15))
                tp = B.pe_mark(ins)
                ta = B.op("act", lambda e: e.activation(out=y0(q, pq * 512, (pq + 1) * 512), in_=bk[:], func=AF.Copy),
                          deps=[tp])
                tb = B.op("act", lambda e: e.activation(out=junk5, in_=bk[:], func=AF.Square,
                                                        accum_out=sm[:, 30 + q * 4 + pq:31 + q * 4 + pq]),
                          deps=[ta, junk_tok[0]])
                junk_tok[0] = tb
                release_bank(bi, [tb])
            for i in range(4):
                release_slab(si + i, tp)
            si += 4
        fs = [get_slab(si + i) for i in range(8)]
        xt_free = [[], []]
        t1_free = [[], []]
        xn_free = []
        t1i = 0
        def f2_mm(q):
            b2i, b2, d2 = acquire()
            b3i, b3, d3 = acquire()
            B.waits("pe", d2 + d3)
            ins = None
            for d in range(16):
                for (bk, qq) in ((b2, 0), (b3, 1)):
                    slab, slab_t = fs[(d // 4) * 2 + qq]
                    if d % 4 == 0:
                        B.wait("pe", slab_t)
                    ins = nc.tensor.matmul(bk[:], lhsT=mT(d, q * 128, (q + 1) * 128),
                                           rhs=slab[:, (d % 4) * 512:(d % 4 + 1) * 512], start=(d == 0), stop=(d == 15))
            return b2i, b2, b3i, b3, B.pe_mark(ins)

        pend = f2_mm(q0)
        xn2 = [xn, RW[:, 5120:7168]]
        xn2_free = [[], []]
        pend_back = None
        for q in range(q0, NQ):
            xt = xt_[q % 2]
            xdeps = list(xt_free[q % 2])
            if l == 1:
                xdeps += [(stx[0], stx_final[0]), (stx[1], stx_final[1])]
            tl = B.dma("sp", ldx[q % 2], xt, xsrc[q * 128:(q + 1) * 128, :], deps=xdeps)
            b2i, b2, b3i, b3, tp = pend
            f2_last = tp
            if q + 1 < NQ:
                pend = f2_mm(q + 1)
                f2_last = pend[4]
            tsq = None
            for (bk, jj) in ((b2, 2), (b3, 3)):
                tsq = B.op("act", lambda e, bk=bk, jj=jj: e.activation(out=junk5, in_=bk[:], func=AF.Square,
                                                                       accum_out=sm[:, 30 + q * 4 + jj:31 + q * 4 + jj]),
                           deps=[tp, tsq, junk_tok[0]])
                junk_tok[0] = tsq
            tr1 = B.op("dve", lambda e: e.reduce_sum(out=sm[:, 70 + q:71 + q], in_=sm[:, 30 + q * 4:34 + q * 4], axis=AX.X),
                       deps=[tsq])
            tr2 = B.op("act", lambda e: e.activation(out=sm[:, 80 + q:81 + q], in_=sm[:, 70 + q:71 + q], func=AF.Sqrt,
                                                     scale=1.0 / D, bias=EPS), deps=[tr1])
            tr3 = B.op("dve", lambda e: e.reciprocal(out=sm[:, 80 + q:81 + q], in_=sm[:, 80 + q:81 + q]), deps=[tr2])
            rstd = sm[:, 80 + q:81 + q]
            xlast = None
            for jj in range(4):
                src = y0(q, jj * 512, (jj + 1) * 512) if jj < 2 else (b2 if jj == 2 else b3)[:]
                tb1 = t1_[t1i % 2]
                ta = B.op("dve", lambda e, src=src, tb1=tb1, jj=jj: e.scalar_tensor_tensor(
                    out=tb1, in0=src, scalar=rstd, in1=gpost[:, jj * 512:(jj + 1) * 512], op0=ALU.mult, op1=ALU.mult),
                    deps=[tr3, t_gp] + t1_free[t1i % 2])
                if jj == 2:
                    release_bank(b2i, [ta])
                if jj == 3:
                    release_bank(b3i, [ta])
                tb = B.op("dve", lambda e, tb1=tb1, jj=jj, xt=xt: e.tensor_tensor(
                    out=xt[:, jj * 512:(jj + 1) * 512], in0=xt[:, jj * 512:(jj + 1) * 512], in1=tb1, op=ALU.add),
                    deps=[ta, tl])
                t1_free[t1i % 2] = [tb]
                t1i += 1
                xlast = tb
            if q == 0:
                xlast = B.op("dve", lambda e, xt=xt: e.tensor_scalar(out=xt, in0=xt, scalar1=hmask[:, 0:1], scalar2=None,
                                                                     op0=ALU.mult), deps=[xlast])
            if l == 0:
                tst = B.dma("sp", stx[q % 2], x1s[q * 128:(q + 1) * 128, :], xt, deps=[xlast])
            elif q == 0:
                tst = None
            elif q < NQ - 1:
                tst = B.dma("sp", stx[q % 2], yp_o[(q - 1) * 128:q * 128, :], xt, deps=[xlast])
            else:
                tst = B.dma("sp", stx[q % 2], ys_o, xt, deps=[xlast])
            rd = [tst] if tst is not None else [xlast]
            if l == 0:
                t4 = p0_front(xt, xn2[q % 2], junk, q, [xlast], xn2_free[q % 2])
                rd.append(t4)
                if pend_back is not None:
                    pq_, pt4 = pend_back
                    pel, evs = p0_back(xn2[pq_ % 2], pq_, 1, pt4)
                    xn2_free[pq_ % 2] = [pel]
                    hT_toks[pq_] = evs
                pend_back = (q, t4)
            xt_free[q % 2] = rd
        if l == 0 and pend_back is not None:
            pq_, pt4 = pend_back
            pel, evs = p0_back(xn2[pq_ % 2], pq_, 1, pt4)
            hT_toks[pq_] = evs
        for i in range(8):
            release_slab(si + i, f2_last)
        si += 8
        stx_final = [stx[0].v, stx[1].v]
        B.barrier(extra=[(stx[0], stx[0].v), (stx[1], stx[1].v)], pe=False)

    for sem in (stx[0], stx[1], st_p2[0], st_p2[1], st_s2[0], st_s2[1], st_v):
        B.wait("sp", (sem, sem.v))
    return nc


def _slab(w, c0, K):
    return np.ascontiguousarray(w[:, c0:c0 + 128].reshape(K, 128, 128).transpose(1, 0, 2)).reshape(128, K * 128)


def _pack_weights(w_in, w_br_a, w_br_b, w_br_c, w_out, pool_w):
    out = np.empty((L * NSLAB_L, 128, 2048), np.float32)
    i = 0
    for l in range(L):
        wi = w_in[l]
        for c in range(8):
            out[i] = _slab(wi, c * 128, 16); i += 1
            out[i] = _slab(wi, 1024 + c * 128, 16); i += 1
        for c in range(8):
            out[i] = _slab(wi, 2048 + c * 128, 16); i += 1
        for c in range(8):
            out[i] = _slab(wi, 4096 + c * 128, 16); i += 1
        for g in range(8):
            out[i] = _slab(wi, 5120 + g * 128, 16); i += 1
            out[i] = _slab(wi, 3072 + g * 128, 16); i += 1
        for c in CORDER:
            out[i] = _slab(wi, 6144 + c * 128, 16); i += 1
        out[i] = np.ascontiguousarray(pool_w[l].reshape(4, 2, 128, 256).transpose(2, 0, 1, 3)).reshape(128, 2048); i += 1
        for c in range(8):
            out[i] = _slab(wi, 7168 + c * 128, 16); i += 1
        for d in range(16):
            out[i] = _slab(wi, 8192 + d * 128, 16); i += 1
            out[i, :, 0:1024] = _slab(w_br_a[l], d * 128, 8)
            out[i, :, 1024:2048] = _slab(w_br_b[l], d * 128, 8)
            i += 1
            out[i] = _slab(wi, 8192 + 2048 + d * 128, 16); i += 1
            out[i] = _slab(wi, 8192 + 4096 + d * 128, 16); i += 1
            if d % 2 == 0:
                out[i, :, 0:1024] = _slab(w_br_c[l], d * 128, 8)
                out[i, :, 1024:2048] = _slab(w_br_c[l], (d + 1) * 128, 8)
                i += 1
        wo = w_out[l]
        forder = [(q, dg) for q in range(2) for dg in range(4)] + [(q, dg) for dg in range(4) for q in (2, 3)]
        for (q, dg) in forder:
            blk = wo[dg * 512:(dg + 1) * 512, q * 512:(q + 1) * 512].reshape(4, 128, 512).transpose(1, 0, 2)
            out[i] = np.ascontiguousarray(blk).reshape(128, 2048); i += 1
    assert i == L * NSLAB_L
    return out


_PROG = {}


def kernel(x_prompt, x_sample, state_conv, state_pool, g_pre, w_in, conv_w, conv_b, conv_ln_g, conv_ln_b,
           w_br_a, gmlp_ln_g, gmlp_ln_b, gmlp_ws, gmlp_bs, w_br_b, pool_w, pool_scale, w_br_c, w_out, g_post):
    f = lambda a: np.asarray(a, dtype=np.float32)
    x_prompt, x_sample, state_conv, state_pool = f(x_prompt), f(x_sample), f(state_conv), f(state_pool)
    g_pre, w_in, conv_w, conv_b, conv_ln_g, conv_ln_b = f(g_pre), f(w_in), f(conv_w), f(conv_b), f(conv_ln_g), f(conv_ln_b)
    w_br_a, gmlp_ln_g, gmlp_ln_b, gmlp_ws, gmlp_bs, w_br_b = f(w_br_a), f(gmlp_ln_g), f(gmlp_ln_b), f(gmlp_ws), f(gmlp_bs), f(w_br_b)
    pool_w, pool_scale, w_br_c, w_out, g_post = f(pool_w), f(pool_scale), f(w_br_c), f(w_out), f(g_post)

    wst = _pack_weights(w_in, w_br_a, w_br_b, w_br_c, w_out, pool_w)
    par = np.zeros((L, 128, NPAR), np.float32)
    for l in range(L):
        par[l, :, P_GPRE:P_GPRE + 16] = g_pre[l].reshape(16, 128).T
        par[l, :, P_CW:P_CW + 248] = conv_w[l].reshape(31, 8, 128).transpose(2, 1, 0).reshape(128, 248)
        par[l, :, P_CB:P_CB + 8] = conv_b[l].reshape(8, 128).T
        par[l, :, P_LG:P_LG + 8] = conv_ln_g[l].reshape(8, 128).T
        par[l, :, P_LB:P_LB + 8] = conv_ln_b[l].reshape(8, 128).T
        par[l, :, P_PS:P_PS + 8] = pool_scale[l].reshape(8, 128).T
        par[l, :, P_GLG:P_GLG + 8] = gmlp_ln_g[l].reshape(8, 128).T
        par[l, :, P_GLB:P_GLB + 8] = gmlp_ln_b[l].reshape(8, 128).T
    gpost = np.ascontiguousarray(np.broadcast_to(g_post[:, None, :], (L, 128, D)))
    wsT = np.ascontiguousarray(gmlp_ws.transpose(0, 3, 1, 2)).reshape(L, 128, 1024)
    a8 = gmlp_ws[:, :, :8, :8].transpose(0, 3, 1, 2)
    wsS = np.ascontiguousarray(np.broadcast_to(a8[:, None, :, :, None, :], (L, 16, 8, 8, 16, 8))).reshape(L, 128, 1024)
    bsrow = np.zeros((L, 128, 2048), np.float32)
    bsrow[:, :, 0:1024] = gmlp_bs.reshape(L, 1, 1024)
    bsrow[:, :, 1024:2048] = np.tile(gmlp_bs[:, :, :8], (1, 1, 16)).reshape(L, 1, 1024)
    cst = np.zeros((128, 384), np.float32)
    cst[:, 0:128] = np.eye(128, dtype=np.float32)
    s_idx = np.arange(128)
    cst[:, 128:256] = (s_idx[None, :] >= s_idx[:, None]).astype(np.float32)
    cst[:, 256:384] = ((s_idx[:, None] // 8 == s_idx[None, :] // 8) & (s_idx[None, :] % 8 >= s_idx[:, None] % 8)).astype(np.float32)

    in_maps = []
    for c in range(NCORE):
        seq, half = c // 2, c % 2
        xin = np.zeros((NT, D), np.float32)
        if half == 1:
            xin[0:128] = x_prompt[seq, 896:1024]
        xin[128:1152] = x_prompt[seq, half * 1024:(half + 1) * 1024]
        xin[1152:1280] = x_sample[16 * c:16 * c + 16].reshape(128, D)
        icnt = np.zeros((128, 4, 16), np.float32)
        for g in range(4):
            W = 2 ** (g + 1)
            pos = half * 1024 + np.arange(16)
            icnt[:, g, :] = (1.0 / np.minimum(W, pos + 1))[None, :]
        hmask = np.full((128, 1), float(half), np.float32)
        sc = state_conv[:, 16 * c:16 * c + 16]
        sconv = np.ascontiguousarray(sc.reshape(L, 16, 30, 8, 128).transpose(0, 3, 4, 1, 2))
        sp_ = state_pool[:, 16 * c:16 * c + 16]
        spool = np.ascontiguousarray(sp_.reshape(L, 16, 15, 8, 128).transpose(0, 3, 4, 1, 2))
        in_maps.append({"xin": xin, "wst": wst, "par": par, "gpost": gpost, "wsT": wsT, "wsS": wsS, "bsrow": bsrow,
                        "cst": cst, "icnt": icnt.reshape(128, 64), "hmask": hmask, "sconv": sconv, "spool": spool})

    if "nc" not in _PROG:
        _PROG["nc"] = build_program()
    res = run_bass_kernel_spmd(_PROG["nc"], in_maps, core_ids=list(range(NCORE)))
    R = res.results

    y_prompt = np.zeros((4, 2048, D), np.float32)
    y_sample = np.zeros((128, 8, D), np.float32)
    ncp = np.zeros((L, 4, 30, 1024), np.float32)
    npp = np.zeros((L, 4, 15, 1024), np.float32)
    ncs = np.zeros((L, 128, 30, 1024), np.float32)
    nps = np.zeros((L, 128, 15, 1024), np.float32)
    nvs = np.zeros((L, 128, 8, 1024), np.float32)
    for c in range(NCORE):
        seq, half = c // 2, c % 2
        r = R[c]
        y_prompt[seq, half * 1024:(half + 1) * 1024] = r["yp"]
        y_sample[16 * c:16 * c + 16] = r["ys"].reshape(16, 8, D)
        if half == 1:
            ncp[:, seq] = r["convp"].transpose(0, 3, 1, 2).reshape(L, 30, 1024)
            npp[:, seq] = r["poolp"].transpose(0, 3, 1, 2).reshape(L, 15, 1024)
        cs = r["convs"].reshape(L, 8, 128, 16, 38)[..., 8:38]
        ncs[:, 16 * c:16 * c + 16] = cs.transpose(0, 3, 4, 1, 2).reshape(L, 16, 30, 1024)
        ps = r["pools"].reshape(L, 8, 128, 16, 23)[..., 8:23]
        nps[:, 16 * c:16 * c + 16] = ps.transpose(0, 3, 4, 1, 2).reshape(L, 16, 15, 1024)
        nvs[:, 16 * c:16 * c + 16] = r["vs"].reshape(L, 16, 8, 1024)
    return (y_prompt, y_sample, ncp, npp, ncs, nps, nvs)
```

```python
import numpy as np
import concourse.bass as bass
import concourse.mybir as mybir
from concourse.bass_utils import run_bass_kernel_spmd

F32 = mybir.dt.float32
BF16 = mybir.dt.bfloat16
AF = mybir.ActivationFunctionType
ALU = mybir.AluOpType
AX = mybir.AxisListType

NCORE = 8
D = 2048
NT = 1280
NQ = 10
L = 2
EPS = 1e-6
TILES = [(0, 512), (512, 512), (1024, 256)]
TILES2 = [(128, 384), (512, 384), (896, 384)]
NIN = 14336
P_GPRE, P_CW, P_CB, P_LG, P_LB, P_PS, P_GLG, P_GLB, NPAR = 0, 16, 264, 272, 280, 288, 296, 304, 312
NSLAB_L = 153
RING = 8
CORDER = [6, 7, 4, 5, 2, 3, 0, 1]


class Sem:
    def __init__(self, nc, name):
        self.h = nc.alloc_semaphore(name)
        self.name = name
        self.v = 0


class Builder:
    def __init__(self, nc):
        self.nc = nc
        self.E = {"pe": nc.tensor, "act": nc.scalar, "dve": nc.vector, "pool": nc.gpsimd, "sp": nc.sync}
        self.S = {e: Sem(nc, "s_" + e) for e in ("pe", "act", "dve")}
        self.waited = {}
        self.last = {e: None for e in ("pe", "act", "dve")}

    def wait(self, eng, tok):
        if tok is None:
            return
        sem, v = tok
        key = (eng, sem.name)
        if self.waited.get(key, 0) >= v:
            return
        self.E[eng].wait_ge(sem.h, v)
        self.waited[key] = v

    def waits(self, eng, deps):
        for d in deps:
            if isinstance(d, (list, tuple)) and d and isinstance(d[0], (tuple, list)):
                for dd in d:
                    self.wait(eng, dd)
            elif isinstance(d, list):
                for dd in d:
                    self.wait(eng, dd)
            else:
                self.wait(eng, d)

    def op(self, eng, fn, deps=()):
        self.waits(eng, deps)
        ins = fn(self.E[eng])
        sem = self.S[eng]
        sem.v += 1
        ins.then_inc(sem.h, 1)
        tok = (sem, sem.v)
        self.last[eng] = tok
        return tok

    def dma(self, eng, sem, out, in_, deps=()):
        self.waits(eng, deps)
        self.E[eng].dma_start(out=out, in_=in_).then_inc(sem.h, 16)
        sem.v += 16
        return (sem, sem.v)

    def mm(self, out, pairs, deps=()):
        self.waits("pe", deps)
        n = len(pairs)
        ins = None
        for i, (l, r) in enumerate(pairs):
            ins = self.nc.tensor.matmul(out, lhsT=l, rhs=r, start=(i == 0), stop=(i == n - 1))
        sem = self.S["pe"]
        sem.v += 1
        ins.then_inc(sem.h, 1)
        tok = (sem, sem.v)
        self.last["pe"] = tok
        return tok

    def pe_mark(self, ins):
        sem = self.S["pe"]
        sem.v += 1
        ins.then_inc(sem.h, 1)
        tok = (sem, sem.v)
        self.last["pe"] = tok
        return tok

    def barrier(self, extra=(), pe=True):
        toks = [self.last[e] for e in ("pe", "act", "dve")] + list(extra)
        for e in (("pe",) if pe else ()) + ("act", "dve", "sp"):
            self.waits(e, toks)


def build_program():
    nc = bass.Bass("TRN2", target_bir_lowering=False)
    B = Builder(nc)
    dt_in = lambda name, shape: nc.dram_tensor(name, shape, F32, kind="ExternalInput").ap()
    dt_out = lambda name, shape: nc.dram_tensor(name, shape, F32, kind="ExternalOutput").ap()
    xin = dt_in("xin", [NT, D])
    wst = dt_in("wst", [L * NSLAB_L, 128, 2048])
    par_d = dt_in("par", [L, 128, NPAR])
    gpost_d = dt_in("gpost", [L, 128, D])
    wsT_d = dt_in("wsT", [L, 128, 1024])
    wsS_d = dt_in("wsS", [L, 128, 1024])
    bsrow_d = dt_in("bsrow", [L, 128, 2048])
    cst_d = dt_in("cst", [128, 384])
    icnt_d = dt_in("icnt", [128, 64])
    hmask_d = dt_in("hmask", [128, 1])
    sconv_d = dt_in("sconv", [L, 8, 128, 16, 30])
    spool_d = dt_in("spool", [L, 8, 128, 16, 15])
    x1s = nc.dram_tensor("x1s", [NT, D], F32, kind="Internal").ap()
    yp_o = dt_out("yp", [1024, D])
    ys_o = dt_out("ys", [128, D])
    convp_o = dt_out("convp", [L, 8, 128, 30])
    poolp_o = dt_out("poolp", [L, 8, 128, 15])
    convs_o = dt_out("convs", [L, 8, 128, 16 * 38])
    pools_o = dt_out("pools", [L, 8, 128, 16 * 23])
    vs_o = dt_out("vs", [L, 128, 1024])

    RH = nc.alloc_sbuf_tensor("RH", [128, 16, NT], BF16)
    RA = nc.alloc_sbuf_tensor("RA", [128, 24 * NT], BF16)
    RS = nc.alloc_sbuf_tensor("RS", [128, 8 * NT], F32)
    RW = nc.alloc_sbuf_tensor("RW", [128, 7168], F32)
    ring = [nc.alloc_sbuf_tensor(f"ring{i}", [128, 2048], BF16) for i in range(RING)]
    par = nc.alloc_sbuf_tensor("par_sb", [128, L, NPAR], F32)
    cst = nc.alloc_sbuf_tensor("cst_sb", [128, 384], F32)
    onesf = nc.alloc_sbuf_tensor("onesf", [128, 128], F32)
    icnt = nc.alloc_sbuf_tensor("icnt_sb", [128, 4, 16], F32)
    hmask = nc.alloc_sbuf_tensor("hmask_sb", [128, 1], F32)
    sm = nc.alloc_sbuf_tensor("sm", [128, 96], F32)
    ident = cst[:, 0:128]
    maskP = cst[:, 128:256]
    maskS = cst[:, 256:384]
    banks = [nc.alloc_psum_tensor(f"bank{i}", [128, 512], F32) for i in range(8)]

    RA_f = RA.bitcast(F32)
    RS_b = RS.bitcast(BF16)
    RW_b = RW.bitcast(BF16)
    actA = lambda c, a, b: RA[:, c * NT + a: c * NT + b]
    actB = lambda c, a, b: RA[:, (8 + c) * NT + a: (8 + c) * NT + b]
    actC = lambda c, a, b: RA[:, (16 + c) * NT + a: (16 + c) * NT + b]
    acts = [actA, actB, actC]

    ld0 = Sem(nc, "ld0")
    slot_sem = [Sem(nc, f"slot{i}") for i in range(RING)]
    ldx = [Sem(nc, "ldx0"), Sem(nc, "ldx1")]
    stx = [Sem(nc, "stx0"), Sem(nc, "stx1")]
    ld_s2 = [Sem(nc, "ld_s0"), Sem(nc, "ld_s1")]
    st_p2 = [Sem(nc, "st_p0"), Sem(nc, "st_p1")]
    st_s2 = [Sem(nc, "st_s0"), Sem(nc, "st_s1")]
    ld_s, st_p, st_s = ld_s2[0], st_p2[0], st_s2[0]
    ld_m = Sem(nc, "ld_m")
    st_v = Sem(nc, "st_v")

    st = {"bank_i": 0, "slab_issued": 0}
    bank_free = [[] for _ in range(8)]

    bank_held = [False] * 8

    def acquire():
        i = st["bank_i"] % 8
        st["bank_i"] += 1
        assert not bank_held[i], f"PSUM bank {i} re-acquired before its release was recorded"
        bank_held[i] = True
        return i, banks[i], list(bank_free[i])

    def release_bank(i, toks):
        bank_free[i] = list(toks)
        bank_held[i] = False

    slot_free = [[] for _ in range(RING)]
    NSLAB = L * NSLAB_L
    slab_tok = {}

    released = set()

    def issue_slab(i):
        s = i % RING
        assert i < RING or (i - RING) in released, f"slab {i}: slot still held by {i - RING}"
        B.waits("pool", slot_free[s])
        nc.gpsimd.dma_start(out=ring[s][:], in_=wst[i], max_dma_last_dim=2048).then_inc(slot_sem[s].h, 16)
        slot_sem[s].v += 16
        tok = (slot_sem[s], slot_sem[s].v)
        slab_tok[i] = tok

    def get_slab(i):
        while st["slab_issued"] <= i:
            issue_slab(st["slab_issued"])
            st["slab_issued"] += 1
        return ring[i % RING], slab_tok[i]

    def release_slab(i, tok):
        slot_free[i % RING] = [tok]
        released.add(i)
        while st["slab_issued"] < NSLAB and (st["slab_issued"] - RING) in released:
            issue_slab(st["slab_issued"])
            st["slab_issued"] += 1

    B.dma("sp", ld0, par[:], par_d.rearrange("l p n -> p l n"))
    B.dma("sp", ld0, cst[:], cst_d)
    B.dma("sp", ld0, icnt[:], icnt_d.rearrange("p (g j) -> p g j", g=4))
    t_ld0 = B.dma("sp", ld0, hmask[:], hmask_d)
    for i in range(RING):
        get_slab(i)
    for e in ("pe", "act", "dve"):
        B.wait(e, t_ld0)
    t_ones = B.op("dve", lambda e: e.memset(onesf[:], 1.0))
    B.wait("pe", t_ones)

    def p0_tile(xt, xn, junk, q, ln, x_deps, xn_free):
        t4 = p0_front(xt, xn, junk, q, x_deps, xn_free)
        pe_last, evs = p0_back(xn, q, ln, t4)
        return t4, pe_last, evs

    def p0_front(xt, xn, junk, q, x_deps, xn_free):
        col = q
        t1 = B.op("act", lambda e: e.activation(out=junk, in_=xt, func=AF.Square, accum_out=sm[:, col:col + 1]),
                  deps=list(x_deps) + [junk_tok[0]])
        junk_tok[0] = t1
        t2 = B.op("act", lambda e: e.activation(out=sm[:, 10 + col:11 + col], in_=sm[:, col:col + 1], func=AF.Sqrt,
                                                scale=1.0 / D, bias=EPS), deps=[t1])
        t3 = B.op("dve", lambda e: e.reciprocal(out=sm[:, 20 + col:21 + col], in_=sm[:, 10 + col:11 + col]), deps=[t2])
        t4 = B.op("act", lambda e: e.activation(out=xn, in_=xt, func=AF.Copy, scale=sm[:, 20 + col:21 + col]),
                  deps=[t3] + list(xn_free))
        return t4

    def p0_back(xn, q, ln, t4):
        evs = []
        pe_last = None
        for j in range(4):
            bi, bk, bdeps = acquire()
            B.waits("pe", [t4] + bdeps)
            ins = None
            for kk in range(4):
                k = 4 * j + kk
                ins = nc.tensor.transpose(bk[:, kk * 128:(kk + 1) * 128], xn[:, k * 128:(k + 1) * 128], ident)
            tp = B.pe_mark(ins)
            pe_last = tp
            gb = par[:, ln, P_GPRE + 4 * j:P_GPRE + 4 * j + 4].unsqueeze(2).broadcast_to([128, 4, 128])
            te = B.op("dve", lambda e, bk=bk, j=j, gb=gb: e.tensor_tensor(
                out=RH[:, 4 * j:4 * j + 4, q * 128:(q + 1) * 128],
                in0=bk[:].rearrange("p (a b) -> p a b", a=4), in1=gb, op=ALU.mult), deps=[tp])
            release_bank(bi, [te])
            evs.append(te)
        return pe_last, evs

    def proj(slab, slab_t, K, koff, src, consumer, tiles=TILES):
        last = None
        for ti, (c0, w) in enumerate(tiles):
            bi, bk, bdeps = acquire()
            tok = B.mm(bk[:, 0:w], [(slab[:, (koff + k) * 128:(koff + k + 1) * 128], src(k, c0, c0 + w)) for k in range(K)],
                       deps=[slab_t] + bdeps)
            last = tok
            consumer(bi, bk, ti, c0, w, tok)
        return last

    hsrc = lambda k, a, b: RH[:, k, a:b]

    ln_tv = [None]

    def ln_finish(S1, S2, tmp, deps, tiles=TILES):
        return [ln_tile(S1, S2, tmp, deps, c0, w) for (c0, w) in tiles]

    def ln_tile(S1, S2, tmp, deps, c0, w):
        toks = []
        tv = ln_tv[0]
        if True:
            b1i, b1, d1 = acquire()
            tA = B.mm(b1[:, 0:w], [(onesf[:], S1[:, c0:c0 + w])], deps=list(deps) + d1)
            b2i, b2, d2 = acquire()
            tB = B.mm(b2[:, 0:w], [(onesf[:], S2[:, c0:c0 + w])], deps=list(deps) + d2)
            tm = B.op("dve", lambda e: e.tensor_scalar(out=S1[:, c0:c0 + w], in0=b1[:, 0:w], scalar1=1.0 / 1024,
                                                       scalar2=None, op0=ALU.mult), deps=[tA, tB])
            release_bank(b1i, [tm])
            tq = B.op("dve", lambda e: e.tensor_tensor(out=tmp[:, 0:w], in0=S1[:, c0:c0 + w], in1=S1[:, c0:c0 + w],
                                                       op=ALU.mult), deps=[tm, tv])
            tv = B.op("dve", lambda e: e.scalar_tensor_tensor(out=S2[:, c0:c0 + w], in0=b2[:, 0:w], scalar=1.0 / 1024,
                                                              in1=tmp[:, 0:w], op0=ALU.mult, op1=ALU.subtract),
                      deps=[tq])
            release_bank(b2i, [tv])
            ts = B.op("act", lambda e: e.activation(out=S2[:, c0:c0 + w], in_=S2[:, c0:c0 + w], func=AF.Sqrt,
                                                    scale=1.0, bias=EPS), deps=[tv])
            tr = B.op("dve", lambda e: e.reciprocal(out=S2[:, c0:c0 + w], in_=S2[:, c0:c0 + w]), deps=[ts])
            tn = B.op("dve", lambda e: e.scalar_tensor_tensor(out=S1[:, c0:c0 + w], in0=S1[:, c0:c0 + w], scalar=-1.0,
                                                              in1=S2[:, c0:c0 + w], op0=ALU.mult, op1=ALU.mult),
                      deps=[tr])
            ln_tv[0] = tv
        return tn

    out_stores = []
    junk_tok = [None]
    hT_toks = [[] for _ in range(NQ)]

    for l in range(L):
        sb = l * NSLAB_L
        TL = [(96, 416), (512, 512), (1024, 256)] if l == 0 else TILES2
        hs = 96 if l == 0 else 128
        TB = TILES if l == 0 else TILES2
        hsB = 0 if l == 0 else 128
        q0 = 0 if l == 0 else 1
        TG = [(64, 448), (512, 512), (1024, 256)] if l == 0 else [(96, 416), (512, 512), (1024, 256)]
        if l == 0:
            xb = [RS[:, 0:2048], RS[:, 2048:4096]]
            xnb = [RS[:, 4096:6144], RS[:, 6144:8192]]
            junk = RS_b[:, 16384:18432]
            xb_free = [[], []]
            xn_free = [[], []]
            for q in range(NQ):
                tl = B.dma("sp", ldx[q % 2], xb[q % 2], xin[q * 128:(q + 1) * 128, :], deps=xb_free[q % 2])
                t4, pel, evs = p0_tile(xb[q % 2], xnb[q % 2], junk, q, 0, [tl], xn_free[q % 2])
                xb_free[q % 2] = [t4]
                xn_free[q % 2] = [pel]
                hT_toks[q] = evs

        yconv = lambda c, a, b: RS[:, c * NT + a: c * NT + b]
        S1 = RA_f[:, 10240:11520]
        S2 = RA_f[:, 11520:12800]
        sq = RA_f[:, 12800:14080]
        dgb = [RA[:, 10240 + i * 3968:10240 + (i + 1) * 3968].rearrange("p (k m) -> p k m", m=128) for i in range(2)]
        zbP = [RA[:, 18176:19360], RA[:, 28160:29344]]
        zbSf = [RA[:, 19360:19968], RA[:, 29344:29952]]
        zbS = [z.rearrange("p (b j) -> p b j", j=38) for z in zbSf]
        zSf = [RW[:, 0:608], RW[:, 608:1216]]
        zS = [z.rearrange("p (b j) -> p b j", j=38) for z in zSf]
        zP128 = [RW[:, 1216:1344], RW[:, 1344:1472]]
        sg = [RW[:, 1536:2048], RW[:, 2048:2560]]
        tmpA = RW[:, 2560:3072]
        pcv = [RW[:, 3072:4352], RW[:, 4352:5632]]
        NDTC = [8] * 7 + [2]
        pcv_rd = [None, None]
        t_z0 = B.op("dve", lambda e: e.memset(zbP[0][:, 0:30], 0.0))
        t_z1 = B.op("dve", lambda e: e.memset(zbP[1][:, 0:30], 0.0))
        sg_free = [[], []]
        sgi = 0
        si = sb
        conv_pe = [None] * 8
        zst = [None] * 8
        st_s_hist = [None] * 8
        st_p_hist = [None] * 8
        CG = [(96, 352), (448, 352), (800, 352)] if l == 0 else [(128, 512), (640, 512)]
        NCG = len(CG)

        def build_diag(c):
            dg = dgb[c % 2]
            deps = [conv_pe[c - 2]] if c >= 2 else []
            t = None
            for k in range(NDTC[c], 31):
                t = B.op("act", lambda e, k=k: e.activation(out=dg[:, k, :], in_=ident, func=AF.Copy,
                                                            scale=par[:, l, P_CW + c * 31 + k:P_CW + c * 31 + k + 1]),
                         deps=deps)
            return t

        def proj_glu(c):
            nonlocal sgi, si
            p = c % 2
            av, av_t = get_slab(si)
            ag, ag_t = get_slab(si + 1)
            old = [conv_pe[c - 2], st_s_hist[c - 2], st_p_hist[c - 2]] if c >= 2 else [t_z0, t_z1]
            tpre = B.dma("sp", ld_s2[p], zS[p][:, :, 0:30], sconv_d[l, c], deps=old)
            tcast = B.op("act", lambda e: e.activation(out=zbS[p][:, :, 0:30], in_=zS[p][:, :, 0:30], func=AF.Copy),
                         deps=[tpre] + old)
            z_toks = [tcast]
            f32_toks = []
            last_pe = None
            for ti, (c0, w) in enumerate(TG):
                bai, ba, da = acquire()
                hdeps = [t for qq in range(c0 // 128, (c0 + w) // 128) for t in hT_toks[qq]] if c == 0 else []
                ta = B.mm(ba[:, 0:w], [(av[:, k * 128:(k + 1) * 128], RH[:, k, c0:c0 + w]) for k in range(16)],
                          deps=[av_t] + da + hdeps)
                bgi, bg, dg_ = acquire()
                tg = B.mm(bg[:, 0:w], [(ag[:, k * 128:(k + 1) * 128], RH[:, k, c0:c0 + w]) for k in range(16)],
                          deps=[ag_t] + dg_)
                last_pe = tg
                sgb = sg[sgi % 2]
                tsg = B.op("act", lambda e: e.activation(out=sgb[:, 0:w], in_=bg[:, 0:w], func=AF.Sigmoid),
                           deps=[tg] + sg_free[sgi % 2])
                release_bank(bgi, [tsg])
                if ti < 2:
                    tz = B.op("dve", lambda e: e.tensor_tensor(out=zbP[p][:, 30 + c0:30 + c0 + w], in0=ba[:, 0:w],
                                                               in1=sgb[:, 0:w], op=ALU.mult), deps=[ta, tsg] + old)
                    z_toks.append(tz)
                else:
                    tz1 = B.op("dve", lambda e: e.tensor_tensor(out=zbP[p][:, 30 + 1024:30 + 1152], in0=ba[:, 0:128],
                                                                in1=sgb[:, 0:128], op=ALU.mult), deps=[ta, tsg] + old)
                    tz2 = B.op("dve", lambda e: e.tensor_tensor(out=zP128[p], in0=ba[:, 0:128], in1=sgb[:, 0:128],
                                                                op=ALU.mult), deps=[ta, tsg] + old)
                    tz3 = B.op("dve", lambda e: e.tensor_tensor(
                        out=zbS[p][:, :, 30:38], in0=ba[:, 128:256].rearrange("p (b j) -> p b j", j=8),
                        in1=sgb[:, 128:256].rearrange("p (b j) -> p b j", j=8), op=ALU.mult), deps=[ta, tsg] + old)
                    tz = B.op("dve", lambda e: e.tensor_tensor(
                        out=zS[p][:, :, 30:38], in0=ba[:, 128:256].rearrange("p (b j) -> p b j", j=8),
                        in1=sgb[:, 128:256].rearrange("p (b j) -> p b j", j=8), op=ALU.mult), deps=[ta, tsg] + old)
                    z_toks += [tz1, tz3]
                    f32_toks = [tz2, tz]
                release_bank(bai, [tz])
                sg_free[sgi % 2] = [tz]
                sgi += 1
            release_slab(si, last_pe)
            release_slab(si + 1, last_pe)
            si += 2
            st_p_hist[c] = B.dma("sp", st_p2[p], convp_o[l, c], zP128[p][:, 98:128], deps=f32_toks)
            st_s_hist[c] = B.dma("sp", st_s2[p], convs_o[l, c], zSf[p], deps=f32_toks + [tpre])
            out_stores.extend([st_p_hist[c], st_s_hist[c]])
            zst[c] = z_toks

        def conv(c, t_dg, s_toks):
            p = c % 2
            dg = dgb[p]
            NDT = NDTC[c]
            pc = pcv[p]
            cwk = lambda k: par[:, l, P_CW + c * 31 + k:P_CW + c * 31 + k + 1]
            cbk = par[:, l, P_CB + c:P_CB + c + 1]
            pcP = pc[:, hs:1152]
            pcS = pc[:, 1152:1280].rearrange("p (b j) -> p b j", j=8)
            tP = B.op("dve", lambda e: e.tensor_scalar(out=pcP, in0=zbP[p][:, hs:1152], scalar1=cwk(0), scalar2=cbk,
                                                       op0=ALU.mult, op1=ALU.add), deps=zst[c] + [pcv_rd[p]])
            tS = B.op("dve", lambda e: e.tensor_scalar(out=pcS, in0=zbS[p][:, :, 0:8], scalar1=cwk(0), scalar2=cbk,
                                                       op0=ALU.mult, op1=ALU.add), deps=zst[c] + [pcv_rd[p]])
            for k in range(1, NDT):
                tP = B.op("dve", lambda e: e.scalar_tensor_tensor(out=pcP, in0=zbP[p][:, hs + k:1152 + k], scalar=cwk(k),
                                                                  in1=pcP, op0=ALU.mult, op1=ALU.add), deps=[tP])
                tS = B.op("dve", lambda e: e.scalar_tensor_tensor(out=pcS, in0=zbS[p][:, :, k:k + 8], scalar=cwk(k),
                                                                  in1=pcS, op0=ALU.mult, op1=ALU.add), deps=[tS])
            bks = [acquire() for _ in range(NCG + 1)]
            B.waits("pe", zst[c] + [t_dg] + [d for (_, _, dd) in bks for d in dd])
            ins = None
            for k in range(NDT, 31):
                for gi_, (o, n) in enumerate(CG):
                    ins = nc.tensor.matmul(bks[gi_][1][:, 0:n], lhsT=dg[:, k, :], rhs=zbP[p][:, o + k:o + k + n],
                                           start=(k == NDT), stop=(k == 30))
                ins = nc.tensor.matmul(bks[NCG][1][:, 0:128].rearrange("p (b j) -> p b j", j=8), lhsT=dg[:, k, :],
                                       rhs=zbS[p][:, :, k:k + 8], start=(k == NDT), stop=(k == 30))
            tp = B.pe_mark(ins)
            conv_pe[c] = tp
            cb = par[:, l, P_CB + c:P_CB + c + 1]
            ev = []
            for gi_, (o, n) in enumerate(CG):
                te = B.op("dve", lambda e: e.tensor_tensor(out=yconv(c, o, o + n), in0=bks[gi_][1][:, 0:n],
                                                           in1=pc[:, o:o + n], op=ALU.add), deps=[tp, tP])
                release_bank(bks[gi_][0], [te])
                ev.append(te)
            te = B.op("dve", lambda e: e.tensor_tensor(out=yconv(c, 1152, 1280), in0=bks[NCG][1][:, 0:128],
                                                       in1=pc[:, 1152:1280], op=ALU.add), deps=[tp, tS])
            release_bank(bks[NCG][0], [te])
            ev.append(te)
            pcv_rd[p] = te
            yc = yconv(c, hs, NT)
            s1v, s2v, sqv = S1[:, hs:NT], S2[:, hs:NT], sq[:, hs:NT]
            tsq = B.op("act", lambda e: e.activation(out=sqv, in_=yc, func=AF.Square), deps=ev + s_toks)
            if c == 0:
                ts1 = B.op("dve", lambda e: e.tensor_copy(out=s1v, in_=yc), deps=ev)
                ts2 = B.op("dve", lambda e: e.tensor_copy(out=s2v, in_=sqv), deps=[tsq])
            else:
                ts1 = B.op("dve", lambda e: e.tensor_tensor(out=s1v, in0=s1v, in1=yc, op=ALU.add), deps=ev + s_toks)
                ts2 = B.op("dve", lambda e: e.tensor_tensor(out=s2v, in0=s2v, in1=sqv, op=ALU.add), deps=[tsq, ts1])
            return [ts1, ts2]

        t_dgs = [None] * 8
        t_dgs[0] = build_diag(0)
        proj_glu(0)
        s_toks = []
        for c in range(8):
            if c + 1 < 8:
                t_dgs[c + 1] = build_diag(c + 1)
                proj_glu(c + 1)
            s_toks = conv(c, t_dgs[c], s_toks)
        st_p_tok = [st_p_hist[6], st_p_hist[7]]
        st_s_tok = [st_s_hist[6], st_s_hist[7]]
        ln_toks = ln_finish(S1, S2, tmpA, s_toks, TL)
        for c in range(8):
            yc = yconv(c, hs, NT)
            t1 = B.op("dve", lambda e: e.tensor_tensor(out=yc, in0=yc, in1=S2[:, hs:NT], op=ALU.mult), deps=ln_toks)
            t2 = B.op("dve", lambda e: e.tensor_tensor(out=yc, in0=yc, in1=S1[:, hs:NT], op=ALU.add), deps=[t1])
            t3n = B.op("act", lambda e: e.activation(out=yc, in_=yc, func=AF.Silu,
                                                     bias=par[:, l, P_LB + c:P_LB + c + 1],
                                                     scale=par[:, l, P_LG + c:P_LG + c + 1]), deps=[t2])
            sl, sl_t = get_slab(si)

            def consA(bi, bk, ti, c0, w, tok, c=c, t3n=t3n):
                nonlocal sgi
                sgb = sg[sgi % 2]
                t1 = B.op("act", lambda e: e.activation(out=sgb[:, 0:w], in_=bk[:, 0:w], func=AF.Silu),
                          deps=[tok] + sg_free[sgi % 2])
                release_bank(bi, [t1])
                t2 = B.op("dve", lambda e: e.tensor_tensor(out=actA(c, c0, c0 + w), in0=sgb[:, 0:w],
                                                           in1=yconv(c, c0, c0 + w), op=ALU.mult),
                          deps=[t1, t3n])
                sg_free[sgi % 2] = [t2]
                sgi += 1
            lp = proj(sl, sl_t, 16, 0, hsrc, consA, TL)
            release_slab(si, lp)
            si += 1
        B.barrier(extra=[st_p_tok, st_s_tok], pe=False)

        vT = lambda c, a, b: RS[:, c * NT + a: c * NT + b]
        vbf = RA[:, 16 * NT:24 * NT].rearrange("p (q f) -> p q f", f=1024)
        S1 = RA_f[:, 5120:6400]
        S2 = RA_f[:, 6400:7680]
        sq = RA_f[:, 7680:8960]
        bsb = RW[:, 0:2048]
        tmpB = RW[:, 2048:2560]
        wsTm = RW_b[:, 5120:6144].rearrange("p (g t) -> p g t", g=8)
        wsSm = RW_b[:, 6144:7168].rearrange("p (g t) -> p g t", g=8)
        stg = RW[:, 3584:4608]
        vn32 = RW[:, 3584:4608]
        t_bs = B.dma("sp", ld_m, bsb, bsrow_d[l])
        t_w1 = B.dma("sp", ld_m, stg, wsT_d[l])
        t_m1 = B.op("dve", lambda e: e.tensor_tensor(out=wsTm, in0=stg.rearrange("p (g t) -> p g t", g=8),
                                                     in1=maskP.unsqueeze(1).broadcast_to([128, 8, 128]), op=ALU.mult),
                    deps=[t_w1])
        t_w2 = B.dma("sp", ld_m, stg, wsS_d[l], deps=[t_m1])
        t_m2 = B.op("dve", lambda e: e.tensor_tensor(out=wsSm, in0=stg.rearrange("p (g t) -> p g t", g=8),
                                                     in1=maskS.unsqueeze(1).broadcast_to([128, 8, 128]), op=ALU.mult),
                    deps=[t_w2])
        s_toks = []
        for c in range(8):
            sl, sl_t = get_slab(si)
            ev = []

            def consV(bi, bk, ti, c0, w, tok, c=c, ev=ev):
                t1 = B.op("act", lambda e: e.activation(out=vT(c, c0, c0 + w), in_=bk[:, 0:w], func=AF.Copy), deps=[tok])
                release_bank(bi, [t1])
                ev.append(t1)
            lp = proj(sl, sl_t, 16, 0, hsrc, consV, TB)
            release_slab(si, lp)
            si += 1
            vc = vT(c, hsB, NT)
            s1v, s2v, sqv = S1[:, hsB:NT], S2[:, hsB:NT], sq[:, hsB:NT]
            tsq = B.op("act", lambda e: e.activation(out=sqv, in_=vc, func=AF.Square), deps=ev + s_toks)
            if c == 0:
                ts1 = B.op("dve", lambda e: e.tensor_copy(out=s1v, in_=vc), deps=ev)
                ts2 = B.op("dve", lambda e: e.tensor_copy(out=s2v, in_=sqv), deps=[tsq])
            else:
                ts1 = B.op("dve", lambda e: e.tensor_tensor(out=s1v, in0=s1v, in1=vc, op=ALU.add), deps=ev + s_toks)
                ts2 = B.op("dve", lambda e: e.tensor_tensor(out=s2v, in0=s2v, in1=sqv, op=ALU.add), deps=[tsq, ts1])
            s_toks = [ts1, ts2]
        ln_toks = [None, None, None]
        vb_toks = []
        tvs_box = [None]
        t3f = [RW[:, 4608:5888], RW[:, 5888:7168]]
        t3_rd = [[], []]
        proj_state = {}
        si_b = si

        def b_norm(ti):
            c0, w = TB[ti]
            n_toks = []
            for c in range(8):
                vc = vT(c, c0, c0 + w)
                t1 = B.op("dve", lambda e: e.tensor_tensor(out=vc, in0=vc, in1=S2[:, c0:c0 + w], op=ALU.mult),
                          deps=[ln_toks[ti]])
                t2 = B.op("dve", lambda e: e.tensor_tensor(out=vc, in0=vc, in1=S1[:, c0:c0 + w], op=ALU.add), deps=[t1])
                t3 = B.op("act", lambda e: e.activation(out=vc, in_=vc, func=AF.Identity,
                                                        bias=par[:, l, P_GLB + c:P_GLB + c + 1],
                                                        scale=par[:, l, P_GLG + c:P_GLG + c + 1]), deps=[t2])
                n_toks.append(t3)
            return n_toks

        def b_transposes(ti, n_toks):
            c0, w = TB[ti]
            for q in range(c0 // 128, (c0 + w) // 128):
                evq = []
                for hh in range(2):
                    bi, bk, bd = acquire()
                    B.waits("pe", n_toks + bd)
                    ins = None
                    for cc in range(4):
                        c = 4 * hh + cc
                        ins = nc.tensor.transpose(bk[:, cc * 128:(cc + 1) * 128], vT(c, q * 128, (q + 1) * 128), ident)
                    tp = B.pe_mark(ins)
                    te = B.op("act", lambda e: e.activation(out=vbf[:, q, hh * 512:(hh + 1) * 512], in_=bk[:],
                                                            func=AF.Copy), deps=[tp])
                    rel = [te]
                    if q == NQ - 1:
                        t32 = B.op("act", lambda e: e.activation(out=vn32[:, hh * 512:(hh + 1) * 512], in_=bk[:],
                                                                 func=AF.Copy), deps=[tp, te])
                        rel.append(t32)
                        evq.append(t32)
                    release_bank(bi, rel)
                    vb_toks.append(te)
                if q == NQ - 1:
                    tvs_box[0] = B.dma("sp", st_v, vs_o[l], vn32, deps=evq)
                    out_stores.append(tvs_box[0])

        def b_proj_pe(g):
            bs_i, bu_i = si_b + 2 * g, si_b + 2 * g + 1
            buf = t3f[g % 2]
            bs_, bs_t = get_slab(bs_i)
            sil = []
            lp = None
            for ti, (c0, w) in enumerate(TL):
                psi, ps, dps = acquire()
                tsl = B.mm(ps[:, 0:w], [(bs_[:, k * 128:(k + 1) * 128], RH[:, k, c0:c0 + w]) for k in range(16)],
                           deps=[bs_t] + dps)
                lp = tsl
                t1 = B.op("act", lambda e: e.activation(out=buf[:, c0:c0 + w], in_=ps[:, 0:w], func=AF.Silu),
                          deps=[tsl] + t3_rd[g % 2])
                release_bank(psi, [t1])
                sil.append(t1)
            release_slab(bs_i, lp)
            bu, bu_t = get_slab(bu_i)
            pus = []
            for ti, (c0, w) in enumerate(TL):
                pui, pu, dpu = acquire()
                tu = B.mm(pu[:, 0:w], [(bu[:, k * 128:(k + 1) * 128], RH[:, k, c0:c0 + w]) for k in range(16)],
                          deps=[bu_t] + dpu)
                lp = tu
                pus.append((pui, pu, tu))
            release_slab(bu_i, lp)
            proj_state[g] = (sil, pus)

        def b_proj_dve(g):
            sil, pus = proj_state[g]
            buf = t3f[g % 2]
            toks = []
            for ti, (c0, w) in enumerate(TL):
                pui, pu, tu = pus[ti]
                t2 = B.op("dve", lambda e: e.tensor_tensor(out=buf[:, c0:c0 + w], in0=pu[:, 0:w], in1=buf[:, c0:c0 + w],
                                                           op=ALU.mult), deps=[tu, sil[ti]])
                release_bank(pui, [t2])
                toks.append(t2)
            proj_state[g] = toks

        tmpm_rd = [None]

        def b_mix(g):
            toks = proj_state[g]
            buf = t3f[g % 2]
            rd = []
            for ti, (c0, w) in enumerate(TL):
                pmi, pm, dpm = acquire()
                B.waits("pe", vb_toks + dpm + [t_m1, t_m2, t_w2])
                ins = None
                segs = []
                pos = c0
                while pos < c0 + w:
                    q = pos // 128
                    end = min((q + 1) * 128, c0 + w)
                    segs.append((q, pos - c0, end - pos, pos - q * 128))
                    pos = end
                for (q, off, n, toff) in segs:
                    wm = wsSm if q == NQ - 1 else wsTm
                    ins = nc.tensor.matmul(pm[:, off:off + n], lhsT=vbf[:, q, g * 128:(g + 1) * 128],
                                           rhs=wm[:, g, toff:toff + n], start=True, stop=True)
                tm = B.pe_mark(ins)
                tmpm = tmpB
                ta_ = None
                for (q, off, n, toff) in segs:
                    so = 1 if q == NQ - 1 else 0
                    ta_ = B.op("dve", lambda e: e.tensor_tensor(
                        out=tmpm[:, off:off + n], in0=pm[:, off:off + n],
                        in1=bsb[:, so * 1024 + g * 128 + toff:so * 1024 + g * 128 + toff + n], op=ALU.add),
                        deps=[tm, t_w2, tmpm_rd[0]])
                release_bank(pmi, [ta_])
                t3 = B.op("dve", lambda e: e.tensor_tensor(out=actB(g, c0, c0 + w), in0=tmpm[:, 0:w], in1=buf[:, c0:c0 + w],
                                                           op=ALU.mult), deps=[ta_, toks[ti]])
                tmpm_rd[0] = t3
                rd.append(t3)
            t3_rd[g % 2] = rd

        ln_toks[0] = ln_tile(S1, S2, tmpB, s_toks, *TB[0])
        nt0 = b_norm(0)
        b_proj_pe(0)
        b_proj_dve(0)
        b_transposes(0, nt0)
        ln_toks[1] = ln_tile(S1, S2, tmpB, s_toks, *TB[1])
        ln_toks[2] = ln_tile(S1, S2, tmpB, s_toks, *TB[2])
        nt1 = b_norm(1)
        b_proj_pe(1)
        b_proj_dve(1)
        b_transposes(1, nt1)
        nt2 = b_norm(2)
        b_transposes(2, nt2)
        for g in range(8):
            b_mix(g)
            if g + 2 < 8:
                b_proj_pe(g + 2)
                b_proj_dve(g + 2)
        si = si_b + 16
        tvs = tvs_box[0]
        tokB_end = B.last["dve"]
        B.barrier(extra=[tvs], pe=False)

        dT = lambda c, a, b: RS_b[:, c * NT + a: c * NT + b]
        cP = RW[:, 0:1168]
        w1 = RW[:, 1168:2336]
        w2 = RW[:, 2336:3504]
        cSf = RW[:, 3504:3872]
        cS = cSf.rearrange("p (b j) -> p b j", j=23)
        u1 = RW[:, 3872:4240].rearrange("p (b j) -> p b j", j=23)
        u2 = RW[:, 4240:4608].rearrange("p (b j) -> p b j", j=23)
        t16 = RW[:, 4608:4624]
        slb = [RW[:, 4624:5136], RW[:, 5136:5648]]
        t_c0 = B.op("dve", lambda e: e.memset(cP[:, 0:16], 0.0))
        t_c1 = B.op("act", lambda e: e.memset(w1[:, 0:16], 0.0)) if False else None
        st_p_tok = None
        st_s_tok = None
        c_last = None
        for c in CORDER:
            sl, sl_t = get_slab(si)
            tpre = B.dma("sp", ld_s, cS[:, :, 0:15], spool_d[l, c], deps=[c_last, st_s_tok])
            wdeps = [c_last, st_p_tok, st_s_tok, t_c0]
            ev = []

            def consC(bi, bk, ti, c0, w, tok, ev=ev, wdeps=wdeps):
                if ti < 2:
                    t1 = B.op("act", lambda e: e.activation(out=cP[:, 16 + c0:16 + c0 + w], in_=bk[:, 0:w], func=AF.Copy),
                              deps=[tok] + wdeps)
                else:
                    t0 = B.op("act", lambda e: e.activation(out=cP[:, 16 + 1024:16 + 1152], in_=bk[:, 0:128], func=AF.Copy),
                              deps=[tok] + wdeps)
                    ev.append(t0)
                    t1 = B.op("act", lambda e: e.activation(out=cS[:, :, 15:23],
                                                            in_=bk[:, 128:256].rearrange("p (b j) -> p b j", j=8),
                                                            func=AF.Copy), deps=[tok] + wdeps)
                release_bank(bi, [t1])
                ev.append(t1)
            lp = proj(sl, sl_t, 16, 0, hsrc, consC, TG)
            release_slab(si, lp)
            si += 1
            st_p_tok = B.dma("sp", st_p, poolp_o[l, c], cP[:, 16 + 128 + 1009:16 + 128 + 1024], deps=ev)
            st_s_tok = B.dma("sp", st_s, pools_o[l, c], cSf, deps=ev + [tpre])
            out_stores += [st_p_tok, st_s_tok]
            g = c // 2
            W = 2 ** (g + 1)
            srcP, srcS = cP, cS
            bufsP, bufsS = [w1, w2], [u1, u2]
            tP = None
            tS = None
            for lev in range(1, g + 2):
                sh = 2 ** (lev - 1)
                v0 = 2 ** lev - 1
                dP = bufsP[(lev - 1) % 2]
                dS = bufsS[(lev - 1) % 2]
                tP = B.op("dve", lambda e, dP=dP, srcP=srcP, v0=v0, sh=sh: e.tensor_tensor(
                    out=dP[:, v0:1168], in0=srcP[:, v0:1168], in1=srcP[:, v0 - sh:1168 - sh], op=ALU.add),
                    deps=ev + [tP, c_last])
                tS = B.op("dve", lambda e, dS=dS, srcS=srcS, v0=v0, sh=sh: e.tensor_tensor(
                    out=dS[:, :, v0:23], in0=srcS[:, :, v0:23], in1=srcS[:, :, v0 - sh:23 - sh], op=ALU.add),
                    deps=ev + [tS, tpre, c_last])
                srcP, srcS = dP, dS
            td1 = B.op("dve", lambda e: e.scalar_tensor_tensor(out=dT(c, 0, 1152), in0=srcP[:, 16:1168], scalar=1.0 / W,
                                                               in1=cP[:, 16:1168], op0=ALU.mult, op1=ALU.subtract),
                       deps=[tP])
            td2 = B.op("dve", lambda e: e.tensor_tensor(out=t16, in0=srcP[:, 144:160], in1=icnt[:, g, :], op=ALU.mult),
                       deps=[tP, c_last])
            td3 = B.op("dve", lambda e: e.tensor_tensor(out=dT(c, 128, 144), in0=t16, in1=cP[:, 144:160],
                                                        op=ALU.subtract), deps=[td2, td1])
            td4 = B.op("dve", lambda e: e.scalar_tensor_tensor(
                out=dT(c, 1152, 1280).rearrange("p (b j) -> p b j", j=8), in0=srcS[:, :, 15:23], scalar=1.0 / W,
                in1=cS[:, :, 15:23], op0=ALU.mult, op1=ALU.subtract), deps=[tS])
            c_last = td4
            d_last = [td3, td4]
        pw, pw_t = get_slab(si)
        pw_i = si
        si += 1
        pwbuf = RW_b[:, 11296:13344]
        pw_t = B.op("act", lambda e: e.activation(out=pwbuf, in_=pw[:], func=AF.Copy), deps=[pw_t])
        release_slab(pw_i, pw_t)
        pwv = pwbuf.rearrange("p (g k n) -> p g k n", g=4, k=2)
        slf = [RS[:, 5120:6400], RS[:, 6400:7680]]
        slf_rd = [[], []]
        c_sil = {}
        si_c = si

        def c_proj(dc):
            buf = slf[dc % 2]
            sl, sl_t = get_slab(si_c + dc)
            toks = []

            def cons(bi, bk, ti, c0, w, tok):
                t1 = B.op("act", lambda e: e.activation(out=buf[:, c0:c0 + w], in_=bk[:, 0:w], func=AF.Silu),
                          deps=[tok] + slf_rd[dc % 2])
                release_bank(bi, [t1])
                toks.append(t1)
            lp = proj(sl, sl_t, 16, 0, hsrc, cons, TL)
            release_slab(si_c + dc, lp)
            c_sil[dc] = toks

        def c_pool(dc):
            g = dc // 2
            buf = slf[dc % 2]
            rd = []
            for ti, (c0, w) in enumerate(TL):
                ppi, pp, dpp = acquire()
                tpp = B.mm(pp[:, 0:w], [(pwv[:, g, kc, (dc % 2) * 128:(dc % 2 + 1) * 128], dT(2 * g + kc, c0, c0 + w))
                                        for kc in range(2)], deps=[pw_t] + dpp + d_last)
                t2 = B.op("dve", lambda e: e.scalar_tensor_tensor(out=actC(dc, c0, c0 + w), in0=pp[:, 0:w],
                                                                  scalar=par[:, l, P_PS + dc:P_PS + dc + 1],
                                                                  in1=buf[:, c0:c0 + w], op0=ALU.mult, op1=ALU.mult),
                          deps=[tpp, c_sil[dc][ti]])
                release_bank(ppi, [t2])
                rd.append(t2)
            slf_rd[dc % 2] = rd

        c_proj(0)
        c_proj(1)
        for dc in range(8):
            c_pool(dc)
            if dc + 2 < 8:
                c_proj(dc + 2)
        si = si_c + 8
        tokC_end = B.last["dve"]
        B.barrier(extra=[st_p_tok, st_s_tok], pe=False)

        mT = lambda d, a, b: RS_b[:, d * NT + a: d * NT + b]
        gbuf = [RW[:, 0:1280], RW[:, 1280:2560]]
        tacc = [RW[:, 2560:3840], RW[:, 3840:5120]]
        tmpb = [RW[:, 5120:5632], RW[:, 5632:6144]]
        gbuf_free = [[], []]
        gstep = 0
        tmi = 0
        tmpb_rd = [None, None]
        tacc_rd = [[None] * 3, [None] * 3]
        scc = None
        for d in range(16):
            i_ga, i_ab, i_gb, i_gc = si, si + 1, si + 2, si + 3
            nsl = 4
            if d % 2 == 0:
                scc_i = si + 4
                nsl = 5
            gate_idx = [i_ga, i_gb, i_gc]
            ta_ = tacc[d % 2]
            ta_tok = [None, None, None]
            last_pe = None
            for i in range(3):
                gsl, gsl_t = get_slab(gate_idx[i])
                gb_ = gbuf[gstep % 2]
                sig = []
                lp = None
                for ti, (c0, w) in enumerate(TL):
                    bgi, bg, dbg = acquire()
                    tg = B.mm(bg[:, 0:w], [(gsl[:, k * 128:(k + 1) * 128], RH[:, k, c0:c0 + w]) for k in range(16)],
                              deps=[gsl_t] + dbg)
                    lp = tg
                    t1 = B.op("act", lambda e: e.activation(out=gb_[:, c0:c0 + w], in_=bg[:, 0:w], func=AF.Sigmoid),
                              deps=[tg] + gbuf_free[gstep % 2])
                    release_bank(bgi, [t1])
                    sig.append(t1)
                release_slab(gate_idx[i], lp)
                if i < 2:
                    wsl, wsl_t = get_slab(i_ab)
                    ko = 8 * i
                else:
                    wsl, wsl_t = get_slab(scc_i)
                    ko = 8 * (d % 2)
                rd = []
                for ti, (c0, w) in enumerate(TL):
                    byi, by, dby = acquire()
                    ty = B.mm(by[:, 0:w], [(wsl[:, (ko + k) * 128:(ko + k + 1) * 128], acts[i](k, c0, c0 + w))
                                           for k in range(8)],
                              deps=[wsl_t] + dby + ([tokC_end if i == 2 else tokB_end] if d == 0 else []))
                    last_pe = ty
                    if i == 0:
                        t2 = B.op("dve", lambda e: e.tensor_tensor(out=ta_[:, c0:c0 + w], in0=by[:, 0:w],
                                                                   in1=gb_[:, c0:c0 + w], op=ALU.mult),
                                  deps=[ty, sig[ti], tacc_rd[d % 2][ti]])
                        release_bank(byi, [t2])
                        rd.append(t2)
                        ta_tok[ti] = t2
                    else:
                        tm_ = tmpb[tmi % 2]
                        tmk = tmi % 2
                        tmi += 1
                        t2 = B.op("dve", lambda e: e.tensor_tensor(out=tm_[:, 0:w], in0=by[:, 0:w],
                                                                   in1=gb_[:, c0:c0 + w], op=ALU.mult),
                                  deps=[ty, sig[ti], tmpb_rd[tmk]])
                        release_bank(byi, [t2])
                        rd.append(t2)
                        if i == 1:
                            ta_tok[ti] = B.op("dve", lambda e: e.tensor_tensor(out=ta_[:, c0:c0 + w], in0=ta_[:, c0:c0 + w],
                                                                               in1=tm_[:, 0:w], op=ALU.add),
                                              deps=[t2, ta_tok[ti]])
                            tmpb_rd[tmk] = ta_tok[ti]
                        else:
                            tfin = B.op("dve", lambda e: e.tensor_tensor(out=mT(d, c0, c0 + w), in0=ta_[:, c0:c0 + w],
                                                                         in1=tm_[:, 0:w], op=ALU.add),
                                        deps=[t2, ta_tok[ti]])
                            tmpb_rd[tmk] = tfin
                            tacc_rd[d % 2][ti] = tfin
                gbuf_free[gstep % 2] = rd
                gstep += 1
                if i == 1:
                    release_slab(i_ab, last_pe)
                if i == 2 and d % 2 == 1:
                    release_slab(scc_i, last_pe)
            si += nsl
        tokG_end = B.last["dve"]
        B.barrier(pe=False)

        y0 = lambda q, a, b: RA_f[:, q * 1024 + a: q * 1024 + b]
        gpost = RA_f[:, 10240:12288]
        xn = RA_f[:, 12288:14336]
        junk = RA[:, 28672:30720]
        junk5 = RA[:, 28672:29184]
        xt_ = [RW[:, 0:2048], RW[:, 2048:4096]]
        t1_ = [RW[:, 4096:4608], RW[:, 4608:5120]]
        xsrc = xin if l == 0 else x1s
        t_gp = B.dma("sp", ld_m, gpost, gpost_d[l])
        for pq in range(2):
            fsl = [get_slab(si + i) for i in range(4)]
            tp = None
            for q in range(q0, NQ):
                bi, bk, dd = acquire()
                B.waits("pe", dd + [tokG_end])
                ins = None
                for d in range(16):
                    if d % 4 == 0:
                        B.wait("pe", fsl[d // 4][1])
                    ins = nc.tensor.matmul(bk[:], lhsT=mT(d, q * 128, (q + 1) * 128),
                                           rhs=fsl[d // 4][0][:, (d % 4) * 512:(d % 4 + 1) * 512],
                                           start=(d == 0), stop=(d == 15))
                tp = B.pe_mark(ins)
                ta = B.op("act", lambda e: e.activation(out=y0(q, pq * 512, (pq + 1) * 512), in_=bk[:], func=AF.Copy),
                          deps=[tp])
                tb = B.op("act", lambda e: e.activation(out=junk5, in_=bk[:], func=AF.Square,
                                                        accum_out=sm[:, 30 + q * 4 + pq:31 + q * 4 + pq]),
                          deps=[ta, junk_tok[0]])
                junk_tok[0] = tb
                release_bank(bi, [tb])
            for i in range(4):
                release_slab(si + i, tp)
            si += 4
        fs = [get_slab(si + i) for i in range(8)]
        xt_free = [[], []]
        t1_free = [[], []]
        xn_free = []
        t1i = 0
        def f2_mm(q):
            b2i, b2, d2 = acquire()
            b3i, b3, d3 = acquire()
            B.waits("pe", d2 + d3)
            ins = None
            for d in range(16):
                for (bk, qq) in ((b2, 0), (b3, 1)):
                    slab, slab_t = fs[(d // 4) * 2 + qq]
                    if d % 4 == 0:
                        B.wait("pe", slab_t)
                    ins = nc.tensor.matmul(bk[:], lhsT=mT(d, q * 128, (q + 1) * 128),
                                           rhs=slab[:, (d % 4) * 512:(d % 4 + 1) * 512], start=(d == 0), stop=(d == 15))
            return b2i, b2, b3i, b3, B.pe_mark(ins)

        pend = f2_mm(q0)
        xn2 = [xn, RW[:, 5120:7168]]
        xn2_free = [[], []]
        pend_back = None
        for q in range(q0, NQ):
            xt = xt_[q % 2]
            xdeps = list(xt_free[q % 2])
            if l == 1:
                xdeps += [(stx[0], stx_final[0]), (stx[1], stx_final[1])]
            tl = B.dma("sp", ldx[q % 2], xt, xsrc[q * 128:(q + 1) * 128, :], deps=xdeps)
            b2i, b2, b3i, b3, tp = pend
            f2_last = tp
            if q + 1 < NQ:
                pend = f2_mm(q + 1)
                f2_last = pend[4]
            tsq = None
            for (bk, jj) in ((b2, 2), (b3, 3)):
                tsq = B.op("act", lambda e, bk=bk, jj=jj: e.activation(out=junk5, in_=bk[:], func=AF.Square,
                                                                       accum_out=sm[:, 30 + q * 4 + jj:31 + q * 4 + jj]),
                           deps=[tp, tsq, junk_tok[0]])
                junk_tok[0] = tsq
            tr1 = B.op("dve", lambda e: e.reduce_sum(out=sm[:, 70 + q:71 + q], in_=sm[:, 30 + q * 4:34 + q * 4], axis=AX.X),
                       deps=[tsq])
            tr2 = B.op("act", lambda e: e.activation(out=sm[:, 80 + q:81 + q], in_=sm[:, 70 + q:71 + q], func=AF.Sqrt,
                                                     scale=1.0 / D, bias=EPS), deps=[tr1])
            tr3 = B.op("dve", lambda e: e.reciprocal(out=sm[:, 80 + q:81 + q], in_=sm[:, 80 + q:81 + q]), deps=[tr2])
            rstd = sm[:, 80 + q:81 + q]
            xlast = None
            for jj in range(4):
                src = y0(q, jj * 512, (jj + 1) * 512) if jj < 2 else (b2 if jj == 2 else b3)[:]
                tb1 = t1_[t1i % 2]
                ta = B.op("dve", lambda e, src=src, tb1=tb1, jj=jj: e.scalar_tensor_tensor(
                    out=tb1, in0=src, scalar=rstd, in1=gpost[:, jj * 512:(jj + 1) * 512], op0=ALU.mult, op1=ALU.mult),
                    deps=[tr3, t_gp] + t1_free[t1i % 2])
                if jj == 2:
                    release_bank(b2i, [ta])
                if jj == 3:
                    release_bank(b3i, [ta])
                tb = B.op("dve", lambda e, tb1=tb1, jj=jj, xt=xt: e.tensor_tensor(
                    out=xt[:, jj * 512:(jj + 1) * 512], in0=xt[:, jj * 512:(jj + 1) * 512], in1=tb1, op=ALU.add),
                    deps=[ta, tl])
                t1_free[t1i % 2] = [tb]
                t1i += 1
                xlast = tb
            if q == 0:
                xlast = B.op("dve", lambda e, xt=xt: e.tensor_scalar(out=xt, in0=xt, scalar1=hmask[:, 0:1], scalar2=None,
                                                                     op0=ALU.mult), deps=[xlast])
                xlast = B.op("dve", lambda e, xt=xt: e.memset(xt[0:96, :], 0.0), deps=[xlast])
            if l == 0:
                tst = B.dma("sp", stx[q % 2], x1s[q * 128:(q + 1) * 128, :], xt, deps=[xlast])
            elif q == 0:
                tst = None
            elif q < NQ - 1:
                tst = B.dma("sp", stx[q % 2], yp_o[(q - 1) * 128:q * 128, :], xt, deps=[xlast])
            else:
                tst = B.dma("sp", stx[q % 2], ys_o, xt, deps=[xlast])
            rd = [tst] if tst is not None else [xlast]
            if l == 0:
                t4 = p0_front(xt, xn2[q % 2], junk, q, [xlast], xn2_free[q % 2])
                rd.append(t4)
                if pend_back is not None:
                    pq_, pt4 = pend_back
                    pel, evs = p0_back(xn2[pq_ % 2], pq_, 1, pt4)
                    xn2_free[pq_ % 2] = [pel]
                    hT_toks[pq_] = evs
                pend_back = (q, t4)
            xt_free[q % 2] = rd
        if l == 0 and pend_back is not None:
            pq_, pt4 = pend_back
            pel, evs = p0_back(xn2[pq_ % 2], pq_, 1, pt4)
            hT_toks[pq_] = evs
        for i in range(8):
            release_slab(si + i, f2_last)
        si += 8
        stx_final = [stx[0].v, stx[1].v]
        B.barrier(extra=[(stx[0], stx[0].v), (stx[1], stx[1].v)], pe=False)

    for sem in (stx[0], stx[1], st_p2[0], st_p2[1], st_s2[0], st_s2[1], st_v):
        B.wait("sp", (sem, sem.v))
    return nc


def _slab(w, c0, K):
    return np.ascontiguousarray(w[:, c0:c0 + 128].reshape(K, 128, 128).transpose(1, 0, 2)).reshape(128, K * 128)


def _pack_weights(w_in, w_br_a, w_br_b, w_br_c, w_out, pool_w):
    out = np.empty((L * NSLAB_L, 128, 2048), np.float32)
    i = 0
    for l in range(L):
        wi = w_in[l]
        for c in range(8):
            out[i] = _slab(wi, c * 128, 16); i += 1
            out[i] = _slab(wi, 1024 + c * 128, 16); i += 1
        for c in range(8):
            out[i] = _slab(wi, 2048 + c * 128, 16); i += 1
        for c in range(8):
            out[i] = _slab(wi, 4096 + c * 128, 16); i += 1
        for g in range(8):
            out[i] = _slab(wi, 5120 + g * 128, 16); i += 1
            out[i] = _slab(wi, 3072 + g * 128, 16); i += 1
        for c in CORDER:
            out[i] = _slab(wi, 6144 + c * 128, 16); i += 1
        out[i] = np.ascontiguousarray(pool_w[l].reshape(4, 2, 128, 256).transpose(2, 0, 1, 3)).reshape(128, 2048); i += 1
        for c in range(8):
            out[i] = _slab(wi, 7168 + c * 128, 16); i += 1
        for d in range(16):
            out[i] = _slab(wi, 8192 + d * 128, 16); i += 1
            out[i, :, 0:1024] = _slab(w_br_a[l], d * 128, 8)
            out[i, :, 1024:2048] = _slab(w_br_b[l], d * 128, 8)
            i += 1
            out[i] = _slab(wi, 8192 + 2048 + d * 128, 16); i += 1
            out[i] = _slab(wi, 8192 + 4096 + d * 128, 16); i += 1
            if d % 2 == 0:
                out[i, :, 0:1024] = _slab(w_br_c[l], d * 128, 8)
                out[i, :, 1024:2048] = _slab(w_br_c[l], (d + 1) * 128, 8)
                i += 1
        wo = w_out[l]
        forder = [(q, dg) for q in range(2) for dg in range(4)] + [(q, dg) for dg in range(4) for q in (2, 3)]
        for (q, dg) in forder:
            blk = wo[dg * 512:(dg + 1) * 512, q * 512:(q + 1) * 512].reshape(4, 128, 512).transpose(1, 0, 2)
            out[i] = np.ascontiguousarray(blk).reshape(128, 2048); i += 1
    assert i == L * NSLAB_L
    return out


_PROG = {}


def kernel(x_prompt, x_sample, state_conv, state_pool, g_pre, w_in, conv_w, conv_b, conv_ln_g, conv_ln_b,
           w_br_a, gmlp_ln_g, gmlp_ln_b, gmlp_ws, gmlp_bs, w_br_b, pool_w, pool_scale, w_br_c, w_out, g_post):
    f = lambda a: np.asarray(a, dtype=np.float32)
    x_prompt, x_sample, state_conv, state_pool = f(x_prompt), f(x_sample), f(state_conv), f(state_pool)
    g_pre, w_in, conv_w, conv_b, conv_ln_g, conv_ln_b = f(g_pre), f(w_in), f(conv_w), f(conv_b), f(conv_ln_g), f(conv_ln_b)
    w_br_a, gmlp_ln_g, gmlp_ln_b, gmlp_ws, gmlp_bs, w_br_b = f(w_br_a), f(gmlp_ln_g), f(gmlp_ln_b), f(gmlp_ws), f(gmlp_bs), f(w_br_b)
    pool_w, pool_scale, w_br_c, w_out, g_post = f(pool_w), f(pool_scale), f(w_br_c), f(w_out), f(g_post)

    wst = _pack_weights(w_in, w_br_a, w_br_b, w_br_c, w_out, pool_w)
    par = np.zeros((L, 128, NPAR), np.float32)
    for l in range(L):
        par[l, :, P_GPRE:P_GPRE + 16] = g_pre[l].reshape(16, 128).T
        par[l, :, P_CW:P_CW + 248] = conv_w[l].reshape(31, 8, 128).transpose(2, 1, 0).reshape(128, 248)
        par[l, :, P_CB:P_CB + 8] = conv_b[l].reshape(8, 128).T
        par[l, :, P_LG:P_LG + 8] = conv_ln_g[l].reshape(8, 128).T
        par[l, :, P_LB:P_LB + 8] = conv_ln_b[l].reshape(8, 128).T
        par[l, :, P_PS:P_PS + 8] = pool_scale[l].reshape(8, 128).T
        par[l, :, P_GLG:P_GLG + 8] = gmlp_ln_g[l].reshape(8, 128).T
        par[l, :, P_GLB:P_GLB + 8] = gmlp_ln_b[l].reshape(8, 128).T
    gpost = np.ascontiguousarray(np.broadcast_to(g_post[:, None, :], (L, 128, D)))
    wsT = np.ascontiguousarray(gmlp_ws.transpose(0, 3, 1, 2)).reshape(L, 128, 1024)
    a8 = gmlp_ws[:, :, :8, :8].transpose(0, 3, 1, 2)
    wsS = np.ascontiguousarray(np.broadcast_to(a8[:, None, :, :, None, :], (L, 16, 8, 8, 16, 8))).reshape(L, 128, 1024)
    bsrow = np.zeros((L, 128, 2048), np.float32)
    bsrow[:, :, 0:1024] = gmlp_bs.reshape(L, 1, 1024)
    bsrow[:, :, 1024:2048] = np.tile(gmlp_bs[:, :, :8], (1, 1, 16)).reshape(L, 1, 1024)
    cst = np.zeros((128, 384), np.float32)
    cst[:, 0:128] = np.eye(128, dtype=np.float32)
    s_idx = np.arange(128)
    cst[:, 128:256] = (s_idx[None, :] >= s_idx[:, None]).astype(np.float32)
    cst[:, 256:384] = ((s_idx[:, None] // 8 == s_idx[None, :] // 8) & (s_idx[None, :] % 8 >= s_idx[:, None] % 8)).astype(np.float32)

    in_maps = []
    for c in range(NCORE):
        seq, half = c // 2, c % 2
        xin = np.zeros((NT, D), np.float32)
        if half == 1:
            xin[0:128] = x_prompt[seq, 896:1024]
        xin[128:1152] = x_prompt[seq, half * 1024:(half + 1) * 1024]
        xin[1152:1280] = x_sample[16 * c:16 * c + 16].reshape(128, D)
        icnt = np.zeros((128, 4, 16), np.float32)
        for g in range(4):
            W = 2 ** (g + 1)
            pos = half * 1024 + np.arange(16)
            icnt[:, g, :] = (1.0 / np.minimum(W, pos + 1))[None, :]
        hmask = np.full((128, 1), float(half), np.float32)
        sc = state_conv[:, 16 * c:16 * c + 16]
        sconv = np.ascontiguousarray(sc.reshape(L, 16, 30, 8, 128).transpose(0, 3, 4, 1, 2))
        sp_ = state_pool[:, 16 * c:16 * c + 16]
        spool = np.ascontiguousarray(sp_.reshape(L, 16, 15, 8, 128).transpose(0, 3, 4, 1, 2))
        in_maps.append({"xin": xin, "wst": wst, "par": par, "gpost": gpost, "wsT": wsT, "wsS": wsS, "bsrow": bsrow,
                        "cst": cst, "icnt": icnt.reshape(128, 64), "hmask": hmask, "sconv": sconv, "spool": spool})

    if "nc" not in _PROG:
        _PROG["nc"] = build_program()
    res = run_bass_kernel_spmd(_PROG["nc"], in_maps, core_ids=list(range(NCORE)))
    R = res.results

    y_prompt = np.zeros((4, 2048, D), np.float32)
    y_sample = np.zeros((128, 8, D), np.float32)
    ncp = np.zeros((L, 4, 30, 1024), np.float32)
    npp = np.zeros((L, 4, 15, 1024), np.float32)
    ncs = np.zeros((L, 128, 30, 1024), np.float32)
    nps = np.zeros((L, 128, 15, 1024), np.float32)
    nvs = np.zeros((L, 128, 8, 1024), np.float32)
    for c in range(NCORE):
        seq, half = c // 2, c % 2
        r = R[c]
        y_prompt[seq, half * 1024:(half + 1) * 1024] = r["yp"]
        y_sample[16 * c:16 * c + 16] = r["ys"].reshape(16, 8, D)
        if half == 1:
            ncp[:, seq] = r["convp"].transpose(0, 3, 1, 2).reshape(L, 30, 1024)
            npp[:, seq] = r["poolp"].transpose(0, 3, 1, 2).reshape(L, 15, 1024)
        cs = r["convs"].reshape(L, 8, 128, 16, 38)[..., 8:38]
        ncs[:, 16 * c:16 * c + 16] = cs.transpose(0, 3, 4, 1, 2).reshape(L, 16, 30, 1024)
        ps = r["pools"].reshape(L, 8, 128, 16, 23)[..., 8:23]
        nps[:, 16 * c:16 * c + 16] = ps.transpose(0, 3, 4, 1, 2).reshape(L, 16, 15, 1024)
        nvs[:, 16 * c:16 * c + 16] = r["vs"].reshape(L, 16, 8, 1024)
    return (y_prompt, y_sample, ncp, npp, ncs, nps, nvs)
```
